# Optimizing a Trainium2 kernel written in Bass

```python
import jax, jax.numpy as jnp
from jax import lax
import numpy as np

D_MODEL = 1024
BATCH = 4
SEQ = 8192
DEPTH = 4

N_MIXERS = 2
EPS = 1e-6

SSM_EXPAND = 2
SSM_D_INNER = SSM_EXPAND * D_MODEL
SSM_HEAD_DIM = 64
SSM_HEADS = SSM_D_INNER // SSM_HEAD_DIM
SSM_GROUPS = 8
SSM_HEADS_PER_GROUP = SSM_HEADS // SSM_GROUPS
SSM_STATE = 128
SSM_CONV = 4
SSM_CHUNK = 128
SSM_BC_DIM = SSM_GROUPS * SSM_STATE
SSM_CONV_DIM = SSM_D_INNER + 2 * SSM_BC_DIM
SSM_IN_DIM = SSM_D_INNER + SSM_CONV_DIM + SSM_HEADS

ATT_HEAD_DIM = 64
ATT_Q_HEADS = D_MODEL // ATT_HEAD_DIM
ATT_KV_HEADS = 4
ATT_GQA = ATT_Q_HEADS // ATT_KV_HEADS
ATT_WIDTH = ATT_Q_HEADS * ATT_HEAD_DIM
ATT_KV_WIDTH = ATT_KV_HEADS * ATT_HEAD_DIM
ATT_IN_DIM = 2 * ATT_WIDTH + 2 * ATT_KV_WIDTH
WINDOW = 128
ATT_BLOCK = 128
ROPE_THETA = 500000.0
ROPE_DIM = ATT_HEAD_DIM // 4

N_SSM_LAYERS = (DEPTH + 1) // 2
N_ATT_LAYERS = DEPTH // 2

kernel_name = "hybrid_ssd_swa_sink_trunk"


def rmsnorm(x, w):
    xf = x.astype(jnp.float32)
    y = xf * lax.rsqrt(jnp.mean(xf * xf, axis=-1, keepdims=True) + EPS)
    return (y * w.astype(jnp.float32)).astype(x.dtype)


def causal_depthwise_conv(u, w, b):
    c = u.shape[-1]
    out = lax.conv_general_dilated(
        u, w[:, None, :].astype(u.dtype), window_strides=(1,),
        padding=((SSM_CONV - 1, 0),), dimension_numbers=('NWC', 'WIO', 'NWC'),
        feature_group_count=c)
    return out + b.astype(u.dtype)


def ssd_chunked(xs, dt, A, Bm, Cm):
    b, L = xs.shape[:2]
    c, l = L // SSM_CHUNK, SSM_CHUNK
    G, R, P, N = SSM_GROUPS, SSM_HEADS_PER_GROUP, SSM_HEAD_DIM, SSM_STATE
    x = (xs * dt[..., None]).reshape(b, c, l, G, R, P)
    a = jnp.moveaxis((dt * A).reshape(b, c, l, G, R), 2, -1)
    a_cs = jnp.cumsum(a, axis=-1)
    Bc = Bm.reshape(b, c, l, G, N)
    Cc = Cm.reshape(b, c, l, G, N)
    causal = jnp.tril(jnp.ones((l, l), dtype=bool))
    seg = a_cs[..., :, None] - a_cs[..., None, :]
    decay = jnp.exp(jnp.where(causal, seg, -jnp.inf))
    cb = jnp.einsum('bclgn,bcsgn->bcgls', Cc, Bc)
    y_diag = jnp.einsum('bcgrls,bcsgrp->bclgrp', cb[:, :, :, None] * decay, x)
    decay_to_end = jnp.exp(a_cs[..., -1:] - a_cs)
    states = jnp.einsum('bclgn,bcgrl,bclgrp->bcgrpn', Bc, decay_to_end, x)
    chunk_decay = jnp.exp(a_cs[..., -1])

    def step(h, inp):
        s, d = inp
        return h * d[..., None, None] + s, h

    h0 = jnp.zeros_like(states[:, 0])
    _, h_in = lax.scan(step, h0, (jnp.moveaxis(states, 1, 0), jnp.moveaxis(chunk_decay, 1, 0)))
    h_in = jnp.moveaxis(h_in, 0, 1)
    y_off = jnp.einsum('bclgn,bcgrpn,bcgrl->bclgrp', Cc, h_in, jnp.exp(a_cs))
    return (y_diag + y_off).reshape(b, L, SSM_HEADS, P)


def mamba2_mixer(h, w_in, conv_w, conv_b, dt_bias, a_log, d_skip, gate_norm, w_out):
    b, L, _ = h.shape
    f32 = jnp.float32
    z, xbc, dt = jnp.split(h @ w_in, [SSM_D_INNER, SSM_D_INNER + SSM_CONV_DIM], axis=-1)
    xbc = jax.nn.silu(causal_depthwise_conv(xbc, conv_w, conv_b))
    xs, Bm, Cm = jnp.split(xbc, [SSM_D_INNER, SSM_D_INNER + SSM_BC_DIM], axis=-1)
    xs = xs.reshape(b, L, SSM_HEADS, SSM_HEAD_DIM).astype(f32)
    Bm = Bm.reshape(b, L, SSM_GROUPS, SSM_STATE).astype(f32)
    Cm = Cm.reshape(b, L, SSM_GROUPS, SSM_STATE).astype(f32)
    dt = jax.nn.softplus(dt.astype(f32) + dt_bias.astype(f32))
    A = -jnp.exp(a_log.astype(f32))
    y = ssd_chunked(xs, dt, A, Bm, Cm) + d_skip.astype(f32)[:, None] * xs
    y = y.reshape(b, L, SSM_D_INNER) * jax.nn.silu(z.astype(f32))
    y = rmsnorm(y, gate_norm)
    return y.astype(h.dtype) @ w_out


def rope_tables(positions):
    inv = ROPE_THETA ** (-jnp.arange(0, ROPE_DIM, 2, dtype=jnp.float32) / ROPE_DIM)
    ang = positions.astype(jnp.float32)[..., None] * inv
    return jnp.cos(ang)[:, :, None, :], jnp.sin(ang)[:, :, None, :]


def apply_partial_rope(t, cos, sin):
    half = ROPE_DIM // 2
    t1, t2, rest = t[..., :half], t[..., half:ROPE_DIM], t[..., ROPE_DIM:]
    return jnp.concatenate([t1 * cos - t2 * sin, t2 * cos + t1 * sin, rest], axis=-1)


def swa_sink_mixer(h, cos, sin, w_in, sinks, w_out):
    b, L, _ = h.shape
    f32 = jnp.float32
    Hk, D, BLK = ATT_KV_HEADS, ATT_HEAD_DIM, ATT_BLOCK
    q, k, v, gate = jnp.split(
        h @ w_in, [ATT_WIDTH, ATT_WIDTH + ATT_KV_WIDTH, ATT_WIDTH + 2 * ATT_KV_WIDTH], axis=-1)
    q = apply_partial_rope(q.reshape(b, L, ATT_Q_HEADS, D).astype(f32), cos, sin)
    k = apply_partial_rope(k.reshape(b, L, Hk, D).astype(f32), cos, sin)
    v = v.reshape(b, L, Hk, D).astype(f32)
    nb = L // BLK
    qb = q.reshape(b, nb, BLK, Hk, ATT_GQA, D)

    def band(t):
        tp = jnp.pad(t, ((0, 0), (BLK, 0), (0, 0), (0, 0))).reshape(b, nb + 1, BLK, Hk, D)
        return jnp.concatenate([tp[:, :-1], tp[:, 1:]], axis=2)

    kb, vb = band(k), band(v)
    s = jnp.einsum('bnikgd,bnjkd->bnkgij', qb, kb) * (D ** -0.5)
    qi = jnp.arange(BLK)[:, None]
    kj = jnp.arange(2 * BLK)[None, :]
    dist = qi + BLK - kj
    blk = jnp.arange(nb)[:, None, None]
    valid = (dist >= 0) & (dist < WINDOW) & ((blk - 1) * BLK + kj >= 0)
    s = jnp.where(valid[None, :, None, None], s, -jnp.inf)
    sink = sinks.astype(f32).reshape(Hk, ATT_GQA)[None, None, :, :, None, None]
    m = jnp.maximum(jnp.max(s, axis=-1, keepdims=True), sink)
    p = jnp.exp(s - m)
    denom = jnp.sum(p, axis=-1, keepdims=True) + jnp.exp(sink - m)
    o = jnp.einsum('bnkgij,bnjkd->bnikgd', p / denom, vb)
    o = o.reshape(b, L, ATT_WIDTH) * jax.nn.silu(gate.astype(f32))
    return o.astype(h.dtype) @ w_out


def setup_inputs(seed: int = 0) -> dict:
    key = jax.random.key(seed)
    ks = jax.random.split(key, 16)
    f32 = jnp.float32
    nS, nA = N_SSM_LAYERS, N_ATT_LAYERS
    x = jax.random.normal(ks[0], (BATCH, SEQ, D_MODEL), f32)
    positions = jnp.broadcast_to(jnp.arange(SEQ, dtype=jnp.int32), (BATCH, SEQ))
    pre_norm = 1.0 + 0.1 * jax.random.normal(ks[1], (DEPTH, D_MODEL), f32)
    post_norm = 1.0 + 0.1 * jax.random.normal(ks[2], (DEPTH, D_MODEL), f32)
    ssm_w_in = jax.random.normal(ks[3], (nS, D_MODEL, SSM_IN_DIM), f32) * D_MODEL ** -0.5
    ssm_conv_w = jax.random.normal(ks[4], (nS, SSM_CONV, SSM_CONV_DIM), f32) * SSM_CONV ** -0.5
    ssm_conv_b = 0.02 * jax.random.normal(ks[5], (nS, SSM_CONV_DIM), f32)
    dt0 = jnp.exp(jax.random.uniform(ks[6], (nS, SSM_HEADS), f32)
                  * (np.log(0.1) - np.log(0.001)) + np.log(0.001)).astype(f32)
    ssm_dt_bias = dt0 + jnp.log(-jnp.expm1(-dt0))
    ssm_a_log = jnp.log(jax.random.uniform(ks[7], (nS, SSM_HEADS), f32, 1.0, 16.0))
    ssm_d = 1.0 + 0.1 * jax.random.normal(ks[8], (nS, SSM_HEADS), f32)
    ssm_gate_norm = 1.0 + 0.1 * jax.random.normal(ks[9], (nS, SSM_D_INNER), f32)
    ssm_w_out = jax.random.normal(ks[10], (nS, SSM_D_INNER, D_MODEL), f32) * SSM_D_INNER ** -0.5
    att_w_in = jax.random.normal(ks[11], (nA, D_MODEL, ATT_IN_DIM), f32) * D_MODEL ** -0.5
    att_sinks = 0.5 * jax.random.normal(ks[12], (nA, ATT_Q_HEADS), f32)
    att_w_out = jax.random.normal(ks[13], (nA, ATT_WIDTH, D_MODEL), f32) * ATT_WIDTH ** -0.5
    return {"x": x, "positions": positions, "pre_norm": pre_norm, "post_norm": post_norm,
            "ssm_w_in": ssm_w_in, "ssm_conv_w": ssm_conv_w, "ssm_conv_b": ssm_conv_b,
            "ssm_dt_bias": ssm_dt_bias, "ssm_a_log": ssm_a_log, "ssm_d": ssm_d,
            "ssm_gate_norm": ssm_gate_norm, "ssm_w_out": ssm_w_out,
            "att_w_in": att_w_in, "att_sinks": att_sinks, "att_w_out": att_w_out}


def reference(x, positions, pre_norm, post_norm, ssm_w_in, ssm_conv_w, ssm_conv_b,
              ssm_dt_bias, ssm_a_log, ssm_d, ssm_gate_norm, ssm_w_out,
              att_w_in, att_sinks, att_w_out):
    cos, sin = rope_tables(positions)
    for i in range(DEPTH):
        h = rmsnorm(x, pre_norm[i])
        j = i // N_MIXERS
        if i % N_MIXERS == 0:
            y = mamba2_mixer(h, ssm_w_in[j], ssm_conv_w[j], ssm_conv_b[j], ssm_dt_bias[j],
                             ssm_a_log[j], ssm_d[j], ssm_gate_norm[j], ssm_w_out[j])
        else:
            y = swa_sink_mixer(h, cos, sin, att_w_in[j], att_sinks[j], att_w_out[j])
        x = x + rmsnorm(y, post_norm[i])
    return x
```

```python
import numpy as np
import concourse.bass as bass
import concourse.mybir as mybir
from concourse.bass_utils import run_bass_kernel_spmd
from contextlib import ExitStack

F32 = mybir.dt.float32
BF16 = mybir.dt.bfloat16
I32 = mybir.dt.int32
AF = mybir.ActivationFunctionType
ALU = mybir.AluOpType

D = 1024
SEQ = 8192
BATCH = 4
DEPTH = 4
CH = 128
NCHUNK = SEQ // CH
EPS = 1e-6
DI = 2048
NH = 32
HP = 64
NG = 8
NST = 128
XBC = 4096
SSM_IN = 6176
AQ = 16
AKV = 4
AD = 64
ATT_IN = 2560
TWO_PI_HI = 6.28125
TWO_PI_LO = 2.0 * np.pi - 6.28125
NEG = -30000.0


class Res:
    __slots__ = ("name", "w", "r", "track", "excl")

    def __init__(self, name, track=True, excl=False):
        self.name = name
        self.w = None
        self.r = {}
        self.track = track
        self.excl = excl


class Sched:
    EPOCH = 30000
    LIMIT = None
    SAME_ENGINE_SYNC = ("act", "dve", "pool")

    def __init__(self, nc, st):
        self.nops = 0
        self.nc = nc
        self.st = st
        self.engs = ["pe", "act", "dve", "pool", "sp"]
        self.prog = {k: [] for k in self.engs}
        self.semobj = {}
        self.owner = {}
        self.cur = {}
        self.cnt = {}
        self.waited = {k: {} for k in self.engs}
        self.nsem = 0
        for k in ["pe", "act", "dve", "pool"]:
            self._new_epoch(k)
        self.dma_cnt = {}

    def _mksem(self, key, owner):
        s = self.st.enter_context(self.nc.semaphore("s_%s" % key))
        self.semobj[key] = s
        self.owner[key] = owner
        self.nsem += 1
        return s

    def _new_epoch(self, eng):
        key = "%s%d" % (eng, self.nsem)
        self._mksem(key, eng)
        self.cur[eng] = key
        self.cnt[eng] = 0

    def _deps(self, reads, writes):
        deps = {}

        def add(tok):
            if tok is None:
                return
            k, v = tok
            if deps.get(k, 0) < v:
                deps[k] = v
        for r in reads:
            add(r.w)
        for w in writes:
            add(w.w)
            for k, v in w.r.items():
                add((k, v))
        return deps

    def _waits(self, eng, deps):
        prog = self.prog[eng]
        for k, v in deps.items():
            if self.owner[k] == eng and (eng == "pe" or eng not in Sched.SAME_ENGINE_SYNC):
                continue
            if self.waited[eng].get(k, 0) >= v:
                continue
            self.waited[eng][k] = v
            sem = self.semobj[k]
            prog.append(lambda e, sem=sem, v=v: e.wait_ge(sem, v))

    def _commit(self, tok, reads, writes):
        k, v = tok
        for r in reads:
            if r.track:
                if r.r.get(k, 0) < v:
                    r.r[k] = v
        for w in writes:
            w.w = tok
            w.r = {}

    def op(self, eng, fn, reads=(), writes=()):
        self.nops += 1
        if Sched.LIMIT is not None and self.nops > Sched.LIMIT:
            return None
        ex = [r for r in reads if r.excl]
        if ex:
            writes = list(writes) + [r for r in ex if r not in writes]
            reads = [r for r in reads if not r.excl]
        self._waits(eng, self._deps(reads, writes))
        if self.cnt[eng] >= self.EPOCH:
            self._new_epoch(eng)
        key = self.cur[eng]
        self.cnt[eng] += 1
        val = self.cnt[eng]
        sem = self.semobj[key]
        self.prog[eng].append(lambda e, fn=fn, sem=sem: fn(e).then_inc(sem, 1))
        tok = (key, val)
        self._commit(tok, reads, writes)
        return tok

    def dma(self, semname, out, in_, reads=(), writes=(), q="sp"):
        self.nops += 1
        if Sched.LIMIT is not None and self.nops > Sched.LIMIT:
            return None
        if semname not in self.semobj:
            self._mksem(semname, "dma")
            self.dma_cnt[semname] = 0
        deps = self._deps(reads, writes)
        if self.dma_cnt[semname] > 0:
            if deps.get(semname, 0) < self.dma_cnt[semname]:
                deps[semname] = self.dma_cnt[semname]
        self._waits(q, deps)
        self.dma_cnt[semname] += 16
        val = self.dma_cnt[semname]
        sem = self.semobj[semname]
        self.prog[q].append(lambda e, out=out, in_=in_, sem=sem: e.dma_start(out=out, in_=in_).then_inc(sem, 16))
        tok = (semname, val)
        self._commit(tok, reads, writes)
        return tok

    def dma_group(self, semname, items, reads=(), writes=(), q="sp"):
        if semname not in self.semobj:
            self._mksem(semname, "dma")
            self.dma_cnt[semname] = 0
        deps = self._deps(reads, writes)
        if self.dma_cnt[semname] > 0 and deps.get(semname, 0) < self.dma_cnt[semname]:
            deps[semname] = self.dma_cnt[semname]
        self._waits(q, deps)
        sem = self.semobj[semname]
        for (out, in_) in items:
            self.dma_cnt[semname] += 16
            self.prog[q].append(lambda e, out=out, in_=in_, sem=sem: e.dma_start(out=out, in_=in_).then_inc(sem, 16))
        tok = (semname, self.dma_cnt[semname])
        self._commit(tok, reads, writes)
        return tok

    def barrier(self):
        toks = {}
        for eng in ["pe", "act", "dve", "pool"]:
            if self.cnt[eng] > 0:
                toks[self.cur[eng]] = self.cnt[eng]
        for k, v in self.dma_cnt.items():
            if v > 0:
                toks[k] = v
        for eng in self.engs:
            for k, v in toks.items():
                if self.waited[eng].get(k, 0) >= v:
                    continue
                self.waited[eng][k] = v
                sem = self.semobj[k]
                self.prog[eng].append(lambda e, sem=sem, v=v: e.wait_ge(sem, v))

    def finish(self, block):
        for k, v in self.dma_cnt.items():
            if self.waited["sp"].get(k, 0) < v:
                sem = self.semobj[k]
                self.prog["sp"].append(lambda e, sem=sem, v=v: e.wait_ge(sem, v))
        progs = self.prog

        @block.sync
        def _(e):
            for f in progs["sp"]:
                f(e)

        @block.tensor
        def _(e):
            for f in progs["pe"]:
                f(e)

        @block.scalar
        def _(e):
            for f in progs["act"]:
                f(e)

        @block.vector
        def _(e):
            for f in progs["dve"]:
                f(e)

        @block.gpsimd
        def _(e):
            for f in progs["pool"]:
                f(e)


class Ctx:
    def __init__(self, nc, st):
        self.nc = nc
        self.st = st
        self.s = Sched(nc, st)
        self.pst = None

    def sb(self, name, shape, dt, track=True):
        return self.sb_in(self.st, name, shape, dt, track)

    def sb_in(self, stack, name, shape, dt, track=True):
        self.uid = getattr(self, "uid", 0) + 1
        t = stack.enter_context(self.nc.sbuf_tensor("t%d_%s" % (self.uid, name), list(shape), dt))
        return t, Res(name, track)

    def mm(self, out, lhsT, rhs, start, stop, reads, writes):
        return self.s.op("pe", lambda e: e.matmul(out, lhsT=lhsT, rhs=rhs, start=start, stop=stop), reads, writes)

    def tr(self, out, in_, ident, reads, writes):
        return self.s.op("pe", lambda e: e.transpose(out=out, in_=in_, identity=ident), reads, writes)

    def act(self, out, in_, func, reads, writes, bias=None, scale=None, accum_out=None):
        kw = {}
        if bias is not None:
            kw["bias"] = bias
        if scale is not None:
            kw["scale"] = scale
        if accum_out is not None:
            kw["accum_out"] = accum_out
        return self.s.op("act", lambda e: e.activation(out=out, in_=in_, func=func, **kw), reads, writes)

    def tt(self, eng, out, in0, in1, op, reads, writes):
        return self.s.op(eng, lambda e: e.tensor_tensor(out=out, in0=in0, in1=in1, op=op), reads, writes)

    def ts(self, eng, out, in0, s1, op0, reads, writes, s2=None, op1=None):
        if op1 is None:
            return self.s.op(eng, lambda e: e.tensor_scalar(out=out, in0=in0, scalar1=s1, scalar2=None, op0=op0), reads, writes)
        return self.s.op(eng, lambda e: e.tensor_scalar(out=out, in0=in0, scalar1=s1, scalar2=s2, op0=op0, op1=op1), reads, writes)

    def stt(self, out, in0, scalar, in1, op0, op1, reads, writes):
        return self.s.op("dve", lambda e: e.scalar_tensor_tensor(out=out, in0=in0, scalar=scalar, in1=in1, op0=op0, op1=op1), reads, writes)

    def cp(self, eng, out, in_, reads, writes):
        if eng == "act":
            return self.s.op("act", lambda e: e.activation(out=out, in_=in_, func=AF.Copy), reads, writes)
        return self.s.op(eng, lambda e: e.tensor_copy(out=out, in_=in_), reads, writes)

    def memset(self, eng, ap, val, writes):
        return self.s.op(eng, lambda e: e.memset(ap, val), (), writes)


def bcast_mid(ap2d, n):
    p, f = ap2d.shape
    return ap2d.unsqueeze(1).broadcast_to([p, n, f])


def bcast_last(ap2d, n):
    p, f = ap2d.shape
    return ap2d.unsqueeze(2).broadcast_to([p, f, n])


def build_program(layers, nch=NCHUNK, debug=False):
    nc = bass.Bass("TRN2", target_bir_lowering=False)
    T = nch * CH
    dr = {}

    def din(name, shape, dt):
        dr[name] = nc.dram_tensor(name, list(shape), dt, kind="ExternalInput").ap()
        return dr[name]

    def dscr(name, shape, dt):
        kind = "ExternalOutput" if debug else "Internal"
        dr[name] = nc.dram_tensor(name, list(shape), dt, kind=kind).ap()
        return dr[name]

    x_in = din("x", [T, D], F32)
    pos_in = din("pos", [128, nch], I32)
    invf_in = din("invf", [128, 8], F32)
    ident_in = din("ident", [128, 128], F32)
    tri_in = din("tri", [128, 128], F32)
    strict_in = din("strict", [128, 128], F32)
    mask_in = din("amask", [128, 2, 256], F32)
    sel4_in = din("sel4", [4, 512], F32)
    prew_in = din("prew", [DEPTH, 128, D], F32)
    postw_in = din("postw", [DEPTH, 128, D], F32)
    nS = sum(1 for l in layers if l % 2 == 0)
    nA = sum(1 for l in layers if l % 2 == 1)
    ssm = {}
    att = {}
    for l in layers:
        j = l // 2
        if l % 2 == 0:
            ssm[j] = dict(
                w_in=din("s%d_w_in" % j, [D, SSM_IN], F32),
                conv_w=din("s%d_conv_w" % j, [128, 4, 32], F32),
                conv_b=din("s%d_conv_b" % j, [4, 8 * 128], F32),
                dtb=din("s%d_dtb" % j, [128, NH], F32),
                alog=din("s%d_alog" % j, [128, NH], F32),
                dsk=din("s%d_d" % j, [128, NH], F32),
                gwk=din("s%d_gwk" % j, [128, 16], F32),
                w_out=din("s%d_w_out" % j, [DI, D], F32),
            )
        else:
            att[j] = dict(
                w_in=din("a%d_w_in" % j, [D, ATT_IN], F32),
                sinks=din("a%d_sinks" % j, [128, AQ], F32),
                w_out=din("a%d_w_out" % j, [D, D], F32),
            )
    out_dram = nc.dram_tensor("out", [T, D], F32, kind="ExternalOutput").ap()
    xs_scr = [dscr("xscr%d" % i, [T, D], F32) for i in range(2)] if len(layers) > 1 else []
    if nS:
        szd = dscr("szd", [nch, 128, DI], BF16)
        dtd = dscr("dtd", [nch, 128, NH], F32)
        xbd = dscr("xbd", [nch, 128, 32 * 128], BF16)
        ynd = dscr("ynd", [nch, 128, DI], BF16)

    with ExitStack() as st:
        cx = Ctx(nc, st)
        s = cx.s
        ident_f, r_identf = cx.sb("ident_f", [128, 128], F32, track=False)
        ident_b, r_identb = cx.sb("ident_b", [128, 128], BF16, track=False)
        tri_f, r_trif = cx.sb("tri_f", [128, 128], F32, track=False)
        tri_b, r_trib = cx.sb("tri_b", [128, 128], BF16, track=False)
        strict_b, r_strict = cx.sb("strict_f", [128, 128], F32, track=False)
        ones_f, r_onesf = cx.sb("ones_f", [128, 128], F32, track=False)
        ones_b, r_onesb = cx.sb("ones_b", [128, 128], BF16, track=False)
        mhalf, r_mhalf = cx.sb("mhalf", [128, 1], F32, track=False)
        s.dma("c_ld0", ident_f[:], ident_in[:, :], (), [r_identf])
        s.dma("c_ld1", tri_f[:], tri_in[:, :], (), [r_trif])
        s.dma("c_ld7", strict_b[:], strict_in[:, :], (), [r_strict])
        cx.cp("dve", ident_b[:], ident_f[:], [r_identf], [r_identb])
        cx.cp("dve", tri_b[:], tri_f[:], [r_trif], [r_trib])
        cx.memset("dve", ones_f[:], 1.0, [r_onesf])
        cx.memset("dve", ones_b[:], 1.0, [r_onesb])
        cx.memset("dve", mhalf[:], -0.5, [r_mhalf])
        PS = []
        for i in range(8):
            t = st.enter_context(nc.psum_tensor("ps%d" % i, [128, 512], F32))
            PS.append((t, Res("ps%d" % i, excl=True)))

        def psb(i):
            return PS[i][0][:].bitcast(BF16)

        x_res_chunks = {}

        def dres(name):
            if name not in x_res_chunks:
                x_res_chunks[name] = [Res("%s_%d" % (name, c)) for c in range(nch)]
            return x_res_chunks[name]

        def rstd_from_ss(ss_ap, n, rstd_ap, tmp_ap, reads, r_tmp, r_rstd):
            cx.ts("dve", tmp_ap, ss_ap, 1.0 / n, ALU.mult, reads, [r_tmp], s2=EPS, op1=ALU.add)
            cx.tt("pool", rstd_ap, tmp_ap, mhalf[:, 0:1], ALU.pow, [r_tmp, r_mhalf], [r_rstd])

        def load_weights_bf16(stack, name, w_dram, ncols, c0=0):
            K = w_dram.shape[0]
            kc = K // 128
            wt, r_w = cx.sb_in(stack, name, [128, kc, ncols], BF16, track=False)
            src = w_dram.rearrange("(k p) n -> p k n", p=128)
            items = []
            step = 512
            for a in range(0, ncols, step):
                b = min(ncols, a + step)
                items.append((wt[:, :, a:b], src[:, :, c0 + a:c0 + b]))
            s.dma_group("wld_" + name, items, (), [r_w], q="pool")
            return wt, r_w

        def interleave(*gens):
            gens = [g for g in gens if g is not None]
            while gens:
                alive = []
                for g in gens:
                    try:
                        next(g)
                        alive.append(g)
                    except StopIteration:
                        pass
                gens = alive

        def interleave_pattern(g1, g2, pattern):
            gens = {"1": g1, "2": g2}
            for ch in pattern:
                g = gens[ch]
                if g is None:
                    continue
                try:
                    next(g)
                except StopIteration:
                    gens[ch] = None
            interleave(gens["1"], gens["2"])

        def norm_chain(tiles, xt, r_x, prew, r_prew):
            (junk, r_junk, ss, r_ss, tmp1, r_tmp1, rstd, r_rstd, hb, r_hb) = tiles
            cx.act(junk[:], xt, AF.Square, [r_x], [r_junk, r_ss], accum_out=ss[:, 0:1])
            rstd_from_ss(ss[:, 0:1], D, rstd[:, 0:1], tmp1[:, 0:1], [r_ss], r_tmp1, r_rstd)
            cx.stt(hb[:], xt, rstd[:, 0:1], prew, ALU.mult, ALU.mult, [r_x, r_rstd, r_prew], [r_hb])

        def transpose_h(hb, r_hb, hT, r_hT, bank):
            pb = psb(bank)
            for k in range(8):
                cx.tr(pb[:, k * 128:(k + 1) * 128], hb[:, k * 128:(k + 1) * 128], ident_b[:], [r_hb, r_identb], [PS[bank][1]])
            cx.cp("act", hT[:].rearrange("p a b -> p (a b)"), pb[:, 0:1024], [PS[bank][1]], [r_hT])

        def post_norm_residual(c, o_banks, xt, r_x, postw, r_postw, tiles, out_ap, r_out, semname):
            post_norm_a(o_banks, tiles)
            post_norm_b(o_banks, xt, r_x, postw, r_postw, tiles, out_ap, r_out, semname)

        def post_norm_a(o_banks, tiles):
            (junk, r_junk, ss2, r_ss2, ss, r_ss, tmp1, r_tmp1, rstd, r_rstd, ot, r_ot) = tiles
            for i, bi in enumerate(o_banks):
                cx.act(junk[:, 0:512], PS[bi][0][:, :], AF.Square, [PS[bi][1]], [r_junk, r_ss2], accum_out=ss2[:, i:i + 1])
            cx.tt("dve", ss[:, 0:1], ss2[:, 0:1], ss2[:, 1:2], ALU.add, [r_ss2], [r_ss])
            rstd_from_ss(ss[:, 0:1], D, rstd[:, 0:1], tmp1[:, 0:1], [r_ss], r_tmp1, r_rstd)

        def post_norm_b(o_banks, xt, r_x, postw, r_postw, tiles, out_ap, r_out, semname):
            (junk, r_junk, ss2, r_ss2, ss, r_ss, tmp1, r_tmp1, rstd, r_rstd, ot, r_ot) = tiles
            for i, bi in enumerate(o_banks):
                cx.stt(ot[:, i * 512:(i + 1) * 512], PS[bi][0][:, :], rstd[:, 0:1], postw[:, i * 512:(i + 1) * 512],
                       ALU.mult, ALU.mult, [PS[bi][1], r_rstd, r_postw], [r_ot])
            cx.tt("pool", ot[:], ot[:], xt, ALU.add, [r_ot, r_x], [r_ot])
            s.dma(semname, out_ap, ot[:], [r_ot], [r_out])

        def ssd_layer(l, x_src, x_src_res, x_dst, x_dst_res):
            j = l // 2
            P = ssm[j]
            r_sz = dres("szd")
            r_dt = dres("dtd")
            r_xb = dres("xbd")
            r_yn = dres("ynd")
            psc = ExitStack()
            diag, r_diag = cx.sb_in(psc, "diag", [128, 4, 32, 128], BF16, track=False)
            cw, r_cw = cx.sb_in(psc, "cw", [128, 4, 32], F32, track=False)
            cb_row, r_cbrow = cx.sb_in(psc, "cb_row", [4, 8 * 128], BF16, track=False)
            sel4, r_sel4 = cx.sb_in(psc, "sel4", [4, 512], BF16, track=False)
            dtb, r_dtb = cx.sb_in(psc, "dtb", [128, NH], F32, track=False)
            A_b, r_Ab = cx.sb_in(psc, "A_b", [128, NH], F32, track=False)
            D_b, r_Db = cx.sb_in(psc, "D_b", [128, NH], F32, track=False)
            s.dma("p_ld0", sel4[:], sel4_in[:, :], (), [r_sel4], q="pool")
            s.dma("c_ld0", cw[:], P["conv_w"][:, :, :], (), [r_cw])
            s.dma("p_ld1", cb_row[:], P["conv_b"][:, :], (), [r_cbrow], q="pool")
            s.dma("c_ld1", dtb[:], P["dtb"][:, :], (), [r_dtb])
            s.dma("c_ld3", A_b[:], P["alog"][:, :], (), [r_Ab])
            s.dma("c_ld4", D_b[:], P["dsk"][:, :], (), [r_Db])
            cx.act(A_b[:], A_b[:], AF.Exp, [r_Ab], [r_Ab])
            cx.ts("dve", A_b[:], A_b[:], -1.0, ALU.mult, [r_Ab], [r_Ab])
            for k in range(4):
                for blk in range(32):
                    cx.ts("dve", diag[:, k, blk, :], ident_f[:], cw[:, k, blk:blk + 1], ALU.mult, [r_identf, r_cw], [r_diag])
            with ExitStack() as ps1:
                w_in, r_win = load_weights_bf16(ps1, "w_in", P["w_in"], SSM_IN)
                prew, r_prew = cx.sb_in(ps1, "prew", [128, D], F32, track=False)
                s.dma("c_ld0", prew[:], prew_in[l], (), [r_prew])
                xt2 = [cx.sb_in(ps1, "s1_x%d" % i, [128, D], F32) for i in range(2)]
                junk, r_junk = cx.sb_in(ps1, "s1_junk", [128, D], BF16)
                ss, r_ss = cx.sb_in(ps1, "s1_ss", [128, 1], F32)
                tmp1, r_tmp1 = cx.sb_in(ps1, "s1_tmp1", [128, 1], F32)
                rstd, r_rstd = cx.sb_in(ps1, "s1_rstd", [128, 1], F32)
                hb, r_hb = cx.sb_in(ps1, "s1_hb", [128, D], BF16)
                hT2 = [cx.sb_in(ps1, "s1_hT%d" % i, [128, 8, 128], BF16) for i in range(2)]
                sz2 = [cx.sb_in(ps1, "s1_sz%d" % i, [128, DI], BF16) for i in range(2)]
                dt2 = [cx.sb_in(ps1, "s1_dt%d" % i, [128, NH], F32) for i in range(2)]
                xb2 = [cx.sb_in(ps1, "s1_xb%d" % i, [128, 32, 128], BF16) for i in range(2)]
                xbq = [[Res("s1_xbq%d_%d" % (i, q)) for q in range(8)] for i in range(2)]
                szq = [[Res("s1_szq%d_%d" % (i, q)) for q in range(4)] for i in range(2)]
                ntiles = (junk, r_junk, ss, r_ss, tmp1, r_tmp1, rstd, r_rstd, hb, r_hb)

                def s1_load(c):
                    s.dma("s1_ldx%d" % (c % 2), xt2[c % 2][0][:], x_src[c * CH:(c + 1) * CH, :], [x_src_res[c]], [xt2[c % 2][1]])
                s1_load(0)
                if nch > 1:
                    s1_load(1)
                norm_chain(ntiles, xt2[0][0][:], xt2[0][1], prew[:], r_prew)
                transpose_h(hb, r_hb, hT2[0][0], hT2[0][1], 0)
                for c in range(nch):
                    sl = c % 2
                    hT, r_hT = hT2[sl]
                    if c + 2 < nch:
                        s1_load(c + 2)
                    if c + 1 < nch:
                        norm_chain(ntiles, xt2[1 - sl][0][:], xt2[1 - sl][1], prew[:], r_prew)
                    szt, r_szt = sz2[sl]
                    for nb in range(4):
                        bi = 1 + (nb % 2)
                        for k in range(8):
                            cx.mm(PS[bi][0][:, :], hT[:, k, :], w_in[:, k, nb * 512:(nb + 1) * 512], k == 0, k == 7,
                                  [r_hT, r_win], [PS[bi][1]])
                        cx.act(szt[:, nb * 512:(nb + 1) * 512], PS[bi][0][:, :], AF.Silu, [PS[bi][1]], [szq[sl][nb]])
                    s.dma("s1_stz%d" % sl, szd[c], szt[:], szq[sl], [r_sz[c]])
                    dtt, r_dtt = dt2[sl]
                    for k in range(8):
                        cx.mm(PS[3][0][:, 0:NH], hT[:, k, :], w_in[:, k, 6144:6176], k == 0, k == 7, [r_hT, r_win], [PS[3][1]])
                    cx.cp("dve", dtt[:], PS[3][0][:, 0:NH], [PS[3][1]], [r_dtt])
                    s.dma("s1_stdt%d" % sl, dtd[c], dtt[:], [r_dtt], [r_dt[c]])
                    xbt, r_xbt = xb2[sl]
                    for q in range(8):
                        bi = 4 + (q % 4)
                        for b4 in range(4):
                            blk = q * 4 + b4
                            for k in range(8):
                                cx.mm(PS[bi][0][:, b4 * 128:(b4 + 1) * 128], w_in[:, k, 2048 + blk * 128:2048 + (blk + 1) * 128],
                                      hT[:, k, :], k == 0, k == 7, [r_hT, r_win], [PS[bi][1]])
                        eng = "dve" if q % 2 == 0 else "act"
                        cx.cp(eng, xbt[:, q * 4:(q + 1) * 4, :].rearrange("p a b -> p (a b)"), PS[bi][0][:, :], [PS[bi][1]], [xbq[sl][q]])
                        if q == 3 and c + 1 < nch:
                            transpose_h(hb, r_hb, hT2[1 - sl][0], hT2[1 - sl][1], 0)
                    s.dma("s1_stx%d" % sl, xbd[c], xbt[:].rearrange("p a b -> p (a b)"), xbq[sl], [r_xb[c]])

            s.barrier()
            with ExitStack() as ps2:
                xb2 = [cx.sb_in(ps2, "s2_xb%d" % i, [128, 32, 131], BF16) for i in range(2)]
                sz3 = [cx.sb_in(ps2, "s2_sz%d" % i, [128, DI], BF16) for i in range(3)]
                GB = 8
                dt3 = [cx.sb_in(ps2, "s2_dt%d" % i, [128, GB, NH], F32) for i in range(2)]
                yn2 = [cx.sb_in(ps2, "s2_yn%d" % i, [128, DI], BF16) for i in range(2)]
                t0, r_t0 = cx.sb_in(ps2, "s2_t0", [128, GB, NH], F32)
                t1, r_t1 = cx.sb_in(ps2, "s2_t1", [128, GB, NH], F32)
                t2, r_t2 = cx.sb_in(ps2, "s2_t2", [128, GB, NH], F32)
                av2 = [cx.sb_in(ps2, "s2_av%d" % i, [128, GB, NH], F32) for i in range(2)]
                ahi, r_ahi = cx.sb_in(ps2, "s2_ahi", [128, NH], BF16)
                ahf, r_ahf = cx.sb_in(ps2, "s2_ahf", [128, NH], F32)
                alo, r_alo = cx.sb_in(ps2, "s2_alo", [128, NH], BF16)
                acs, r_acs = cx.sb_in(ps2, "s2_acs", [128, GB, NH], F32)
                dtv2 = [cx.sb_in(ps2, "s2_dtv%d" % i, [128, GB, NH], F32) for i in range(2)]
                eacs2 = [cx.sb_in(ps2, "s2_eacs%d" % i, [128, GB, NH], F32) for i in range(2)]
                dte2 = [cx.sb_in(ps2, "s2_dte%d" % i, [128, GB, NH], F32) for i in range(2)]
                cdv2 = [cx.sb_in(ps2, "s2_cdv%d" % i, [128, GB, NH], F32) for i in range(2)]
                xc, r_xc = cx.sb_in(ps2, "s2_xc", [128, 24, 128], BF16)
                cT2 = [cx.sb_in(ps2, "s2_cT%d" % i, [128, 8, 128], BF16) for i in range(2)]
                xs_tok, r_xst = cx.sb_in(ps2, "s2_xs_tok", [128, DI], BF16)
                skb2 = [cx.sb_in(ps2, "s2_skb%d" % i, [128, DI], BF16) for i in range(2)]
                Bt2 = [cx.sb_in(ps2, "s2_Bt%d" % i, [128, 1024], BF16) for i in range(2)]
                xdt2 = [cx.sb_in(ps2, "s2_xdt%d" % i, [128, DI], BF16) for i in range(2)]
                xdte2 = [cx.sb_in(ps2, "s2_xdte%d" % i, [128, DI], BF16) for i in range(2)]
                cbm, r_cbm = cx.sb_in(ps2, "s2_cbm", [128, 8, 128], BF16)
                rhi, r_rhi = cx.sb_in(ps2, "s2_rf", [128, NH, 128], F32)
                rfq = [Res("s2_rfq%d" % q) for q in range(4)]
                Mt2 = [cx.sb_in(ps2, "s2_Mt%d" % i, [128, NH, 128], BF16) for i in range(2)]
                Mtq = [[Res("s2_Mtq%d_%d" % (i, q)) for q in range(8)] for i in range(2)]
                xcq = [Res("s2_xcq%d" % q) for q in range(6)]
                cTq = [[Res("s2_cTq%d_%d" % (i, q)) for q in range(2)] for i in range(2)]
                xsth = [Res("s2_xsth%d" % q) for q in range(2)]
                skbh = [[Res("s2_skbh%d_%d" % (i, q)) for q in range(2)] for i in range(2)]
                yvq = [Res("s2_yvq%d" % q) for q in range(4)]
                hTsq = [Res("s2_hTsq%d" % q) for q in range(4)]
                hTs, r_hTs = cx.sb_in(ps2, "s2_hTs", [128, DI], F32)
                hTb, r_hTb = cx.sb_in(ps2, "s2_hTb", [128, DI], BF16)
                htmp, r_htmp = cx.sb_in(ps2, "s2_htmp", [128, DI], F32)
                yv, r_yv = cx.sb_in(ps2, "s2_yv", [128, DI], F32)
                gjunk, r_gjunk = cx.sb_in(ps2, "s2_gjunk", [128, DI], BF16)
                ssg, r_ssg = cx.sb_in(ps2, "s2_ssg", [128, 1], F32)
                tmpg, r_tmpg = cx.sb_in(ps2, "s2_tmpg", [128, 1], F32)
                rstdg, r_rstdg = cx.sb_in(ps2, "s2_rstdg", [128, 1], F32)
                cx.memset("pool", hTs[:], 0.0, hTsq)
                cx.memset("pool", hTb[:], 0.0, [r_hTb])

                def s2_load(c):
                    xbt, r_xbt = xb2[c % 2]
                    s.dma("s2_ldx%d" % (c % 2), xbt[:, :, 3:131], xbd[c].rearrange("p (a b) -> p a b", b=128), [r_xb[c]], [r_xbt])
                    s.dma("s2_ldz%d" % (c % 3), sz3[c % 3][0][:], szd[c], [r_sz[c]], [sz3[c % 3][1]])
                    if c % GB == 0:
                        gn = min(GB, nch - c)
                        gi = (c // GB) % 2
                        s.dma("s2_ldd%d" % gi, dt3[gi][0][:, 0:gn, :], dtd[c:c + gn].rearrange("c p h -> p c h"),
                              [r_dt[cc] for cc in range(c, c + gn)], [dt3[gi][1]])

                def stage1(c):
                    sl = c % 2
                    xbt, r_xbt = xb2[sl]
                    gi = (c // GB) % 2
                    ci = c % GB
                    gn = min(GB, nch - (c - ci))
                    dtr8, r_dtr = dt3[gi]
                    dtv8, r_dtv = dtv2[gi]
                    eacs8, r_eacs = eacs2[gi]
                    dte8, r_dte = dte2[gi]
                    cdv8, r_cdv = cdv2[gi]
                    av8, r_av = av2[gi]
                    dtv = dtv8[:, ci, :]
                    dte = dte8[:, ci, :]
                    av = av8[:, ci, :]
                    cT, r_cT = cT2[sl]
                    skb, r_skb = skb2[sl]
                    B_tok, r_Bt = Bt2[sl]
                    xdt, r_xdt = xdt2[sl]
                    xdte, r_xdte = xdte2[sl]
                    Mt, r_Mt = Mt2[sl]
                    if c == 0:
                        cx.memset("pool", xbt[:, :, 0:3], 0.0, [r_xbt])
                    else:
                        pv, r_pv = xb2[1 - sl]
                        cx.cp("pool", xbt[:, :, 0:3], pv[:, :, 128:131], [r_pv], [r_xbt])
                    if c + 1 < nch:
                        s2_load(c + 1)
                    if ci == 0:
                        W = gn * NH
                        g2 = lambda t: t[:, 0:gn, :]
                        f2 = lambda t: t[:, 0:gn, :].rearrange("p a b -> p (a b)")
                        cx.tt("dve", g2(t0), g2(dtr8), bcast_mid(dtb[:], gn), ALU.add, [r_dtr, r_dtb], [r_t0])
                        cx.stt(f2(t1), f2(t0), -1.0, f2(t0), ALU.mult, ALU.max, [r_t0], [r_t1])
                        cx.act(f2(t1), f2(t1), AF.Exp, [r_t1], [r_t1], scale=-1.0)
                        cx.act(f2(t2), f2(t1), AF.Ln, [r_t1], [r_t2], bias=1.0)
                        cx.stt(f2(dtv8), f2(t0), 0.0, f2(t2), ALU.max, ALU.add, [r_t0, r_t2], [r_dtv])
                        cx.tt("dve", g2(av8), g2(dtv8), bcast_mid(A_b[:], gn), ALU.mult, [r_dtv, r_Ab], [r_av])
                        cx.mm(PS[0][0][:, 0:W], tri_f[:], f2(av8), True, True, [r_trif, r_av], [PS[0][1]])
                        cx.cp("dve", f2(acs), PS[0][0][:, 0:W], [PS[0][1]], [r_acs])
                        cx.mm(PS[0][0][:, 0:W], ones_f[:], f2(av8), True, True, [r_onesf, r_av], [PS[0][1]])
                        cx.tt("dve", f2(t0), PS[0][0][:, 0:W], f2(acs), ALU.subtract, [PS[0][1], r_acs], [r_t0])
                        cx.act(f2(cdv8), PS[0][0][:, 0:W], AF.Exp, [PS[0][1]], [r_cdv])
                        cx.act(f2(eacs8), f2(acs), AF.Exp, [r_acs], [r_eacs])
                        cx.act(f2(dte8), f2(t0), AF.Exp, [r_t0], [r_dte])
                    yield
                    def mask_q(q4):
                        cx.tt("dve", rhi[:, q4 * 8:(q4 + 1) * 8, :], bcast_mid(tri_f[:], 8), bcast_last(av[:, q4 * 8:(q4 + 1) * 8], 128), ALU.mult,
                              [r_trif, r_av], [rfq[q4]])
                    for q in range(8):
                        bi = 1 + (q % 2)
                        cx.mm(PS[bi][0][:, :], cb_row[0:4, q * 128:(q + 1) * 128], sel4[0:4, :], True, False, [r_cbrow, r_sel4], [PS[bi][1]])
                        for b4 in range(4):
                            blk = q * 4 + b4
                            o = PS[bi][0][:, b4 * 128:(b4 + 1) * 128]
                            for k in range(4):
                                cx.mm(o, diag[:, k, blk, :], xbt[:, blk, k:k + 128], False, (k == 3 and b4 == 3), [r_diag, r_xbt], [PS[bi][1]])
                        if q < 6:
                            cx.act(xc[:, q * 4:(q + 1) * 4, :].rearrange("p a b -> p (a b)"), PS[bi][0][:, :], AF.Silu, [PS[bi][1]], [xcq[q]])
                        else:
                            cx.act(cT[:, (q - 6) * 4:(q - 5) * 4, :].rearrange("p a b -> p (a b)"), PS[bi][0][:, :], AF.Silu, [PS[bi][1]], [cTq[sl][q - 6]])
                        yield
                        if q == 1:
                            mask_q(0)
                            yield
                            mask_q(1)
                            yield
                        if q == 3:
                            mask_q(2)
                            yield
                            mask_q(3)
                            yield
                    for half in range(2):
                        bi = 3 if half == 0 else 0
                        pb = psb(bi)
                        for b8 in range(8):
                            blk = half * 8 + b8
                            cx.tr(pb[:, b8 * 128:(b8 + 1) * 128], xc[:, blk, :], ident_b[:], [xcq[blk // 4], r_identb], [PS[bi][1]])
                        cx.cp("act", xs_tok[:, half * 1024:(half + 1) * 1024], pb[:, 0:1024], [PS[bi][1]], [xsth[half]])
                        cx.tt("dve", skb[:, half * 1024:(half + 1) * 1024].rearrange("p (h d) -> p h d", d=HP),
                              pb[:, 0:1024].rearrange("p (h d) -> p h d", d=HP), bcast_last(D_b[:, half * 16:(half + 1) * 16], HP), ALU.mult,
                              [PS[bi][1], r_Db], [skbh[sl][half]])
                        yield
                    pb = psb(3)
                    for b8 in range(8):
                        cx.tr(pb[:, b8 * 128:(b8 + 1) * 128], xc[:, 16 + b8, :], ident_b[:], [xcq[4 + b8 // 4], r_identb], [PS[3][1]])
                    cx.cp("act", B_tok[:], pb[:, 0:1024], [PS[3][1]], [r_Bt])
                    cx.tt("pool", xdt[:].rearrange("p (h d) -> p h d", d=HP), xs_tok[:].rearrange("p (h d) -> p h d", d=HP),
                          bcast_last(dtv, HP), ALU.mult, xsth + [r_dtv], [r_xdt])
                    yield
                    cx.tt("pool", xdte[:].rearrange("p (h d) -> p h d", d=HP), xdt[:].rearrange("p (h d) -> p h d", d=HP),
                          bcast_last(dte, HP), ALU.mult, [r_xdt, r_dte], [r_xdte])
                    for half in range(2):
                        bi = 1 + half
                        for g4 in range(4):
                            g = half * 4 + g4
                            cx.mm(PS[bi][0][:, g4 * 128:(g4 + 1) * 128], xc[:, 16 + g, :], cT[:, g, :], True, True, [xcq[4 + g // 4], cTq[sl][g // 4]], [PS[bi][1]])
                        cx.tt("dve", cbm[:, half * 4:(half + 1) * 4, :], PS[bi][0][:, :].rearrange("p (a b) -> p a b", b=128),
                              bcast_mid(tri_b[:], 4), ALU.mult, [PS[bi][1], r_trib], [r_cbm])
                    yield
                    for q in range(8):
                        bi = q % 4
                        cx.mm(PS[bi][0][:, :], strict_b[:], rhi[:, q * 4:(q + 1) * 4, :].rearrange("p a b -> p (a b)"), True, True,
                              [r_strict, rfq[q // 2]], [PS[bi][1]])
                        cx.act(Mt[:, q * 4:(q + 1) * 4, :].rearrange("p a b -> p (a b)"), PS[bi][0][:, :], AF.Exp, [PS[bi][1]], [Mtq[sl][q]])
                        cx.tt("dve", Mt[:, q * 4:(q + 1) * 4, :], Mt[:, q * 4:(q + 1) * 4, :], bcast_mid(cbm[:, q, :], 4), ALU.mult,
                              [Mtq[sl][q], r_cbm], [Mtq[sl][q]])
                        yield

                def stage2(c):
                    sl = c % 2
                    szt, r_szt = sz3[c % 3]
                    ynt, r_ynt = yn2[sl]
                    gi = (c // GB) % 2
                    eacs8, r_eacs = eacs2[gi]
                    cdv8, r_cdv = cdv2[gi]
                    eacs = eacs8[:, c % GB, :]
                    cdv = cdv8[:, c % GB, :]
                    cT, r_cT = cT2[sl]
                    skb, r_skb = skb2[sl]
                    B_tok, r_Bt = Bt2[sl]
                    xdt, r_xdt = xdt2[sl]
                    xdte, r_xdte = xdte2[sl]
                    Mt, r_Mt = Mt2[sl]
                    ytmp, r_ytmp = htmp, r_htmp
                    for hh in range(2):
                        for b2 in range(2):
                            bi = 4 + b2
                            h0 = hh * 16 + b2 * 8
                            cx.mm(PS[bi][0][:, :], ident_b[:], skb[:, h0 * HP:(h0 + 8) * HP], True, False, [r_identb, skbh[sl][hh]], [PS[bi][1]])
                            for h8 in range(8):
                                h = h0 + h8
                                cx.mm(PS[bi][0][:, h8 * 64:(h8 + 1) * 64], Mt[:, h, :], xdt[:, h * HP:(h + 1) * HP], False, h8 == 7,
                                      [Mtq[sl][h // 4], r_xdt], [PS[bi][1]])
                        for g4 in range(4):
                            g = hh * 4 + g4
                            bi = 6 + (g4 // 2)
                            cx.mm(PS[bi][0][:, (g4 % 2) * 256:(g4 % 2 + 1) * 256], cT[:, g, :], hTb[:, g * 256:(g + 1) * 256], True, True,
                                  [cTq[sl][g // 4], r_hTb], [PS[bi][1]])
                        yield
                        for b2 in range(2):
                            h0 = hh * 16 + b2 * 8
                            cx.tt("dve", ytmp[:, b2 * 512:(b2 + 1) * 512].rearrange("p (h d) -> p h d", d=HP),
                                  PS[6 + b2][0][:, :].rearrange("p (h d) -> p h d", d=HP), bcast_last(eacs[:, h0:h0 + 8], HP), ALU.mult,
                                  [PS[6 + b2][1], r_eacs], [r_ytmp])
                            cx.tt("dve", yv[:, h0 * HP:(h0 + 8) * HP], PS[4 + b2][0][:, :], ytmp[:, b2 * 512:(b2 + 1) * 512], ALU.add,
                                  [PS[4 + b2][1], r_ytmp], [yvq[hh * 2 + b2]])
                            yield
                    cx.tt("pool", yv[:], yv[:], szt[:], ALU.mult, yvq + [r_szt], yvq)
                    yield
                    cx.act(gjunk[:], yv[:], AF.Square, yvq, [r_gjunk, r_ssg], accum_out=ssg[:, 0:1])
                    rstd_from_ss(ssg[:, 0:1], DI, rstdg[:, 0:1], tmpg[:, 0:1], [r_ssg], r_tmpg, r_rstdg)
                    cx.act(ynt[:], yv[:], AF.Copy, yvq + [r_rstdg], [r_ynt], scale=rstdg[:, 0:1])
                    s.dma("s2_sty%d" % sl, ynd[c], ynt[:], [r_ynt], [r_yn[c]])
                    yield
                    cx.tt("pool", htmp[:].rearrange("p (h d) -> p h d", d=HP), hTs[:].rearrange("p (h d) -> p h d", d=HP),
                          bcast_last(cdv, HP), ALU.mult, hTsq + [r_cdv], [r_htmp])
                    yield
                    for q in range(4):
                        bi = 4 + q
                        for g2 in range(2):
                            g = q * 2 + g2
                            cx.mm(PS[bi][0][:, g2 * 256:(g2 + 1) * 256], B_tok[:, g * 128:(g + 1) * 128], xdte[:, g * 256:(g + 1) * 256], True, True,
                                  [r_Bt, r_xdte], [PS[bi][1]])
                        cx.tt("dve", hTs[:, q * 512:(q + 1) * 512], PS[bi][0][:, :], htmp[:, q * 512:(q + 1) * 512], ALU.add,
                              [PS[bi][1], r_htmp], [hTsq[q]])
                        yield
                    cx.cp("act", hTb[:], hTs[:], hTsq, [r_hTb])

                s2_load(0)
                interleave(stage1(0))
                for c in range(nch):
                    interleave_pattern(stage1(c + 1) if c + 1 < nch else None, stage2(c),
                                       "121121212112121121212121212121211111111")

            s.barrier()
            psc.close()
            s.barrier()
            with ExitStack() as ps3:
                w_out, r_wout = load_weights_bf16(ps3, "w_out", P["w_out"], D)
                postw, r_postw = cx.sb_in(ps3, "postw", [128, D], F32, track=False)
                s.dma("c_ld0", postw[:], postw_in[l], (), [r_postw])
                gwk, r_gwk = cx.sb_in(ps3, "gwk", [128, 16], F32, track=False)
                s.dma("c_ld1", gwk[:], P["gwk"][:, :], (), [r_gwk])
                for k in range(16):
                    cx.ts("dve", w_out[:, k, :], w_out[:, k, :], gwk[:, k:k + 1], ALU.mult, [r_wout, r_gwk], [r_wout])
                xt2 = [cx.sb_in(ps3, "s3_x%d" % i, [128, D], F32) for i in range(3)]
                yn2 = [cx.sb_in(ps3, "s3_yn%d" % i, [128, DI], BF16) for i in range(2)]
                ot2 = [cx.sb_in(ps3, "s3_ot%d" % i, [128, D], F32) for i in range(2)]
                ynT2 = [cx.sb_in(ps3, "s3_ynT%d" % i, [128, 16, 128], BF16) for i in range(2)]
                junk, r_junk = cx.sb_in(ps3, "s3_junk", [128, D], BF16)
                ss2, r_ss2 = cx.sb_in(ps3, "s3_ss2", [128, 2], F32)
                ss, r_ss = cx.sb_in(ps3, "s3_ss", [128, 1], F32)
                tmp1, r_tmp1 = cx.sb_in(ps3, "s3_tmp1", [128, 1], F32)
                rstd, r_rstd = cx.sb_in(ps3, "s3_rstd", [128, 1], F32)

                def s3_load(c):
                    sl = c % 2
                    s.dma("s3_ldx%d" % (c % 3), xt2[c % 3][0][:], x_src[c * CH:(c + 1) * CH, :], [x_src_res[c]], [xt2[c % 3][1]])
                    s.dma("s3_ldy%d" % sl, yn2[sl][0][:], ynd[c], [r_yn[c]], [yn2[sl][1]])

                def s3_transposes(c):
                    ynt, r_ynt = yn2[c % 2]
                    ynT, r_ynT = ynT2[c % 2]
                    for half in range(2):
                        bi = half
                        pb = psb(bi)
                        for b8 in range(8):
                            blk = half * 8 + b8
                            cx.tr(pb[:, b8 * 128:(b8 + 1) * 128], ynt[:, blk * 128:(blk + 1) * 128], ident_b[:], [r_ynt, r_identb], [PS[bi][1]])
                        cx.cp("dve" if half == 0 else "act", ynT[:, half * 8:(half + 1) * 8, :].rearrange("p a b -> p (a b)"), pb[:, 0:1024],
                              [PS[bi][1]], [r_ynT])
                s3_load(0)
                if nch > 1:
                    s3_load(1)
                s3_transposes(0)
                for c in range(nch):
                    sl = c % 2
                    xt, r_x = xt2[c % 3]
                    ot, r_ot = ot2[sl]
                    ynT, r_ynT = ynT2[sl]
                    if c + 2 < nch:
                        s3_load(c + 2)
                    banks = [2 + 2 * (c % 2), 3 + 2 * (c % 2)]
                    for nb in range(2):
                        bi = banks[nb]
                        for k in range(16):
                            cx.mm(PS[bi][0][:, :], ynT[:, k, :], w_out[:, k, nb * 512:(nb + 1) * 512], k == 0, k == 15, [r_ynT, r_wout], [PS[bi][1]])
                        if nb == 0 and c + 1 < nch:
                            s3_transposes(c + 1)
                    post_norm_residual(c, banks, xt[:], r_x, postw[:], r_postw,
                                       (junk, r_junk, ss2, r_ss2, ss, r_ss, tmp1, r_tmp1, rstd, r_rstd, ot, r_ot),
                                       x_dst[c * CH:(c + 1) * CH, :], x_dst_res[c], "s3_st%d" % sl)


        def swa_layer(l, x_src, x_src_res, x_dst, x_dst_res):
            j = l // 2
            P = att[j]
            NT = 4 * nch
            with ExitStack() as pa:
                w_in, r_win = load_weights_bf16(pa, "aw_in", P["w_in"], ATT_IN)
                w_out, r_wout = load_weights_bf16(pa, "aw_out", P["w_out"], D)
                prew, r_prew = cx.sb_in(pa, "a_prew", [128, D], F32, track=False)
                postw, r_postw = cx.sb_in(pa, "a_postw", [128, D], F32, track=False)
                sinks, r_sinks = cx.sb_in(pa, "a_sinks", [128, AQ], F32, track=False)
                amask, r_amask = cx.sb_in(pa, "a_mask", [128, 2, 256], F32, track=False)
                invf, r_invf = cx.sb_in(pa, "a_invf", [128, 8], F32, track=False)
                posi, r_posi = cx.sb_in(pa, "a_posi", [128, nch], I32, track=False)
                posf, r_posf = cx.sb_in(pa, "a_posf", [128, nch], F32, track=False)
                cosT, r_cos = cx.sb_in(pa, "a_cos", [128, nch, 8], F32, track=False)
                sinT, r_sin = cx.sb_in(pa, "a_sin", [128, nch, 8], F32, track=False)
                s.dma("c_ld0", prew[:], prew_in[l], (), [r_prew])
                s.dma("c_ld1", postw[:], postw_in[l], (), [r_postw])
                s.dma("c_ld3", sinks[:], P["sinks"][:, :], (), [r_sinks])
                negsinks, r_negsinks = cx.sb_in(pa, "a_negsinks", [128, AQ], F32, track=False)
                cx.ts("dve", negsinks[:], sinks[:], -1.0, ALU.mult, [r_sinks], [r_negsinks])
                s.dma("c_ld4", amask[:], mask_in[:, :, :], (), [r_amask])
                s.dma("c_ld5", invf[:], invf_in[:, :], (), [r_invf])
                s.dma("c_ld6", posi[:], pos_in[:, :], (), [r_posi])
                with ExitStack() as prt:
                    ang, r_ang = cx.sb_in(prt, "a_ang", [128, nch, 8], F32)
                    kf, r_kf = cx.sb_in(prt, "a_kf", [128, nch, 8], F32)
                    ki, r_ki = cx.sb_in(prt, "a_ki", [128, nch, 8], I32)
                    rr, r_rr = cx.sb_in(prt, "a_rr", [128, nch, 8], F32)
                    fx, r_fx = cx.sb_in(prt, "a_fx", [128, nch, 8], F32)
                    cx.cp("dve", posf[:], posi[:], [r_posi], [r_posf])
                    cx.tt("dve", ang[:], bcast_last(posf[:], 8), bcast_mid(invf[:], nch), ALU.mult, [r_posf, r_invf], [r_ang])

                    def reduced_sin(dst, r_dst, shift):
                        cx.ts("dve", kf[:], ang[:], shift, ALU.add, [r_ang], [r_kf], s2=1.0 / (2.0 * np.pi), op1=ALU.mult)
                        cx.cp("dve", ki[:], kf[:], [r_kf], [r_ki])
                        cx.cp("dve", kf[:], ki[:], [r_ki], [r_kf])
                        cx.stt(rr[:], kf[:], -TWO_PI_HI, ang[:], ALU.mult, ALU.add, [r_kf, r_ang], [r_rr])
                        cx.stt(rr[:], kf[:], -TWO_PI_LO, rr[:], ALU.mult, ALU.add, [r_kf, r_rr], [r_rr])
                        cx.ts("dve", rr[:], rr[:], shift, ALU.add, [r_rr], [r_rr])
                        cx.ts("dve", fx[:], rr[:], float(np.pi), ALU.is_gt, [r_rr], [r_fx], s2=-2.0 * np.pi, op1=ALU.mult)
                        cx.tt("dve", rr[:], rr[:], fx[:], ALU.add, [r_rr, r_fx], [r_rr])
                        cx.ts("dve", fx[:], rr[:], -float(np.pi), ALU.is_lt, [r_rr], [r_fx], s2=2.0 * np.pi, op1=ALU.mult)
                        cx.tt("dve", rr[:], rr[:], fx[:], ALU.add, [r_rr, r_fx], [r_rr])
                        cx.ts("dve", rr[:], rr[:], 3.1415925, ALU.min, [r_rr], [r_rr], s2=-3.1415925, op1=ALU.max)
                        cx.act(dst[:], rr[:], AF.Sin, [r_rr], [r_dst])
                    reduced_sin(sinT, r_sin, 0.0)
                    reduced_sin(cosT, r_cos, float(np.pi / 2.0))
                    s.barrier()

                xt2 = [cx.sb_in(pa, "a_x%d" % i, [128, D], F32) for i in range(2)]
                xr2 = [cx.sb_in(pa, "a_xr%d" % i, [128, D], F32) for i in range(2)]
                ot2 = [cx.sb_in(pa, "a_ot%d" % i, [128, D], F32) for i in range(2)]
                junk, r_junk = cx.sb_in(pa, "a_junk", [128, D], BF16)
                ss, r_ss = cx.sb_in(pa, "a_ss", [128, 1], F32)
                tmp1, r_tmp1 = cx.sb_in(pa, "a_tmp1", [128, 1], F32)
                rstd, r_rstd = cx.sb_in(pa, "a_rstd", [128, 1], F32)
                junk_b, r_junk_b = cx.sb_in(pa, "a_junk_b", [128, 512], BF16)
                ss2, r_ss2 = cx.sb_in(pa, "a_ss2", [128, 2], F32)
                ss_b, r_ss_b = cx.sb_in(pa, "a_ss_b", [128, 1], F32)
                tmp1_b, r_tmp1_b = cx.sb_in(pa, "a_tmp1_b", [128, 1], F32)
                rstd_b, r_rstd_b = cx.sb_in(pa, "a_rstd_b", [128, 1], F32)
                hb, r_hb = cx.sb_in(pa, "a_hb", [128, D], BF16)
                hT, r_hT = cx.sb_in(pa, "a_hT", [128, 8, 128], BF16)
                qk, r_qk = cx.sb_in(pa, "a_qk", [128, 1280], F32)
                qkb, r_qkb = cx.sb_in(pa, "a_qkb", [128, 1280], BF16)
                ra, r_ra = cx.sb_in(pa, "a_ra", [128, 20, 8], F32)
                rb, r_rb = cx.sb_in(pa, "a_rb", [128, 20, 8], F32)
                sg3 = [cx.sb_in(pa, "a_sg%d" % i, [128, D], BF16) for i in range(4)]
                qT2 = [cx.sb_in(pa, "a_qT%d" % i, [64, AQ, 128], BF16) for i in range(2)]
                kT4 = [cx.sb_in(pa, "a_kT%d" % i, [64, AKV, 128], BF16) for i in range(4)]
                v4 = [cx.sb_in(pa, "a_v%d" % i, [128, 256], BF16) for i in range(5)]
                sm2 = [cx.sb_in(pa, "a_sm%d" % i, [128, 4, 256], F32) for i in range(3)]
                rmax2 = [cx.sb_in(pa, "a_rmax%d" % i, [128, 4], F32) for i in range(3)]
                negm2 = [cx.sb_in(pa, "a_negm%d" % i, [128, AQ], F32) for i in range(3)]
                negmq = [[Res("a_negmq%d_%d" % (i, k)) for k in range(4)] for i in range(3)]
                rsum2 = [cx.sb_in(pa, "a_rsum%d" % i, [128, AQ], F32) for i in range(3)]
                esk2 = [cx.sb_in(pa, "a_esk%d" % i, [128, AQ], F32) for i in range(3)]
                rden, r_rden = cx.sb_in(pa, "a_rden", [128, AQ], F32)
                pt2 = [cx.sb_in(pa, "a_pt%d" % i, [128, 4, 256], BF16) for i in range(3)]
                ptg = [[Res("a_ptg%d_%d" % (i, g)) for g in range(4)] for i in range(3)]
                rsh = [[Res("a_rsh%d_%d" % (i, h)) for h in range(AQ)] for i in range(3)]
                pT2 = [cx.sb_in(pa, "a_pT%d" % i, [128, 4, 2, 128], BF16) for i in range(3)]
                og, r_og = cx.sb_in(pa, "a_og", [128, D], BF16)
                otmp, r_otmp = cx.sb_in(pa, "a_otmp", [128, D], F32)
                ogT, r_ogT = cx.sb_in(pa, "a_ogT", [128, 8, 128], BF16)
                ntiles = (junk, r_junk, ss, r_ss, tmp1, r_tmp1, rstd, r_rstd, hb, r_hb)
                SB = [2, 3]
                PB = [4, 5]
                OB = [6, 7]

                def a_load(c):
                    s.dma("a_ldx%d" % (c % 2), xt2[c % 2][0][:], x_src[c * CH:(c + 1) * CH, :], [x_src_res[c]], [xt2[c % 2][1]])

                def F1(c):
                    xt, r_x = xt2[c % 2]
                    norm_chain(ntiles, xt[:], r_x, prew[:], r_prew)

                def F2(c):
                    transpose_h(hb, r_hb, hT, r_hT, 0)

                def F3(c):
                    vv, r_v = v4[c % 5]
                    for nb in range(3):
                        bi = nb % 2
                        for k in range(8):
                            cx.mm(PS[bi][0][:, :], hT[:, k, :], w_in[:, k, nb * 512:(nb + 1) * 512], k == 0, k == 7, [r_hT, r_win], [PS[bi][1]])
                        if nb < 2:
                            cx.cp("dve", qk[:, nb * 512:(nb + 1) * 512], PS[bi][0][:, :], [PS[bi][1]], [r_qk])
                        else:
                            cx.cp("dve", qk[:, 1024:1280], PS[bi][0][:, 0:256], [PS[bi][1]], [r_qk])
                            cx.cp("act", vv[:], PS[bi][0][:, 256:512], [PS[bi][1], r_qk], [r_v])

                def F4(c):
                    sg, r_sg = sg3[c % 4]
                    for nb in range(3, 5):
                        bi = nb % 2
                        for k in range(8):
                            cx.mm(PS[bi][0][:, :], hT[:, k, :], w_in[:, k, nb * 512:(nb + 1) * 512], k == 0, k == 7, [r_hT, r_win], [PS[bi][1]])
                        cx.act(sg[:, (nb - 3) * 512:(nb - 2) * 512], PS[bi][0][:, :], AF.Silu, [PS[bi][1]], [r_sg])

                def F45(c):
                    q3 = qk[:].rearrange("p (h d) -> p h d", d=AD)
                    qb3 = qkb[:].rearrange("p (h d) -> p h d", d=AD)
                    cosb = bcast_mid(cosT[:, c, :], 20)
                    sinb = bcast_mid(sinT[:, c, :], 20)
                    cx.cp("pool", qkb[:], qk[:], [r_qk], [r_qkb])
                    cx.tt("dve", ra[:], q3[:, :, 0:8], cosb, ALU.mult, [r_qk, r_cos], [r_ra])
                    cx.tt("dve", rb[:], q3[:, :, 8:16], sinb, ALU.mult, [r_qk, r_sin], [r_rb])
                    cx.tt("dve", qb3[:, :, 0:8], ra[:], rb[:], ALU.subtract, [r_ra, r_rb, r_qkb], [r_qkb])
                    cx.tt("dve", ra[:], q3[:, :, 8:16], cosb, ALU.mult, [r_qk, r_cos, r_qkb], [r_ra])
                    cx.tt("dve", rb[:], q3[:, :, 0:8], sinb, ALU.mult, [r_qk, r_sin, r_qkb], [r_rb])
                    cx.tt("dve", qb3[:, :, 8:16], ra[:], rb[:], ALU.add, [r_ra, r_rb, r_qkb], [r_qkb])

                def F6(c):
                    qT, r_qT = qT2[c % 2]
                    kT, r_kT = kT4[c % 4]
                    for half in range(2):
                        bi = half
                        pb = psb(bi)
                        for h8 in range(8):
                            h = half * 8 + h8
                            cx.tr(pb[0:64, h8 * 128:(h8 + 1) * 128], qkb[:, h * AD:(h + 1) * AD], ident_b[:], [r_qkb, r_identb], [PS[bi][1]])
                        cx.cp("act" if half else "dve", qT[:, half * 8:(half + 1) * 8, :].rearrange("p a b -> p (a b)"), pb[0:64, 0:1024],
                              [PS[bi][1]], [r_qT])
                    pb = psb(0)
                    for h4 in range(4):
                        cx.tr(pb[0:64, h4 * 128:(h4 + 1) * 128], qkb[:, 1024 + h4 * AD:1024 + (h4 + 1) * AD], ident_b[:], [r_qkb, r_identb], [PS[0][1]])
                    cx.cp("dve", kT[:].rearrange("p a b -> p (a b)"), pb[0:64, 0:512], [PS[0][1]], [r_kT])

                def St(t):
                    c, kh = divmod(t, 4)
                    qT, r_qT = qT2[c % 2]
                    kT, r_kT = kT4[c % 4]
                    kTp, r_kTp = kT4[(c - 1) % 4]
                    for g in range(4):
                        h = kh * 4 + g
                        bi = SB[g // 2]
                        o = PS[bi][0][:, (g % 2) * 256:(g % 2 + 1) * 256]
                        if c > 0:
                            cx.mm(o[:, 0:128], qT[:, h, :], kTp[:, kh, :], True, True, [r_qT, r_kTp], [PS[bi][1]])
                        cx.mm(o[:, 128:256], qT[:, h, :], kT[:, kh, :], True, True, [r_qT, r_kT], [PS[bi][1]])

                def Mt_(t):
                    c, kh = divmod(t, 4)
                    sm, r_sm = sm2[t % 3]
                    rmax, r_rmax = rmax2[t % 3]
                    negm, r_negm = negm2[c % 3]
                    mk = amask[:, 0 if c == 0 else 1, :]
                    for i2 in range(2):
                        bi = SB[i2]
                        if c > 0:
                            cx.stt(sm[:, i2 * 2:(i2 + 1) * 2, :], PS[bi][0][:, :].rearrange("p (a b) -> p a b", b=256), 0.125,
                                   bcast_mid(mk, 2), ALU.mult, ALU.add, [PS[bi][1], r_amask], [r_sm])
                        else:
                            cx.memset("dve", sm[:, i2 * 2:(i2 + 1) * 2, 0:128], NEG, [r_sm])
                            cx.stt(sm[:, i2 * 2:(i2 + 1) * 2, 128:256], PS[bi][0][:, :].rearrange("p (a b) -> p a b", b=256)[:, :, 128:256], 0.125,
                                   bcast_mid(mk[:, 128:256], 2), ALU.mult, ALU.add, [PS[bi][1], r_amask], [r_sm])
                    cx.s.op("dve", lambda e, o_=rmax[:, 0:4], i_=sm[:]: e.tensor_reduce(out=o_, in_=i_, axis=mybir.AxisListType.X, op=ALU.max),
                            [r_sm], [r_rmax])
                    cx.stt(negm[:, kh * 4:(kh + 1) * 4], rmax[:], -1.0, negsinks[:, kh * 4:(kh + 1) * 4], ALU.mult, ALU.min,
                           [r_rmax, r_negsinks], [negmq[c % 3][kh]])

                def Et_(t):
                    c, kh = divmod(t, 4)
                    sm, r_sm = sm2[t % 3]
                    negm, r_negm = negm2[c % 3]
                    pt, r_pt = pt2[t % 3]
                    rsum, r_rsum = rsum2[c % 3]
                    for g in range(4):
                        h = kh * 4 + g
                        cx.act(pt[:, g, :], sm[:, g, :], AF.Exp, [r_sm, negmq[c % 3][kh]], [ptg[t % 3][g], rsh[c % 3][h]], bias=negm[:, h:h + 1], accum_out=rsum[:, h:h + 1])

                def Tt(t):
                    pt, r_pt = pt2[t % 3]
                    bi = PB[t % 2]
                    pbk = psb(bi)
                    for g in range(4):
                        for hf in range(2):
                            cx.tr(pbk[:, (g * 2 + hf) * 128:(g * 2 + hf + 1) * 128], pt[:, g, hf * 128:(hf + 1) * 128], ident_b[:],
                                  [ptg[t % 3][g], r_identb], [PS[bi][1]])

                def Ct(t):
                    pT, r_pT = pT2[t % 3]
                    bi = PB[t % 2]
                    cx.cp("dve", pT[:].rearrange("p a b c -> p (a b c)"), psb(bi)[:, 0:1024], [PS[bi][1]], [r_pT])

                def Vt(t):
                    c, kh = divmod(t, 4)
                    pT, r_pT = pT2[t % 3]
                    vv, r_v = v4[c % 5]
                    vp, r_vp = v4[(c - 1) % 5]
                    ob = OB[kh // 2]
                    for g in range(4):
                        h = kh * 4 + g
                        o = PS[ob][0][:, (h % 8) * 64:(h % 8 + 1) * 64]
                        if c > 0:
                            cx.mm(o, pT[:, g, 0, :], vp[:, kh * AD:(kh + 1) * AD], True, False, [r_pT, r_vp], [PS[ob][1]])
                            cx.mm(o, pT[:, g, 1, :], vv[:, kh * AD:(kh + 1) * AD], False, True, [r_pT, r_v], [PS[ob][1]])
                        else:
                            cx.mm(o, pT[:, g, 1, :], vv[:, kh * AD:(kh + 1) * AD], True, True, [r_pT, r_v], [PS[ob][1]])

                def BN(c):
                    sg, r_sg = sg3[c % 4]
                    rsum, r_rsum = rsum2[c % 3]
                    esk, r_esk = esk2[c % 3]
                    negm, r_negm = negm2[c % 3]
                    cx.tt("dve", esk[:], sinks[:], negm[:], ALU.add, [r_sinks] + negmq[c % 3], [r_esk])
                    s.dma("a_ldr%d" % (c % 2), xr2[c % 2][0][:], x_src[c * CH:(c + 1) * CH, :], [x_src_res[c]], [xr2[c % 2][1]])
                    cx.act(esk[:], esk[:], AF.Exp, [r_esk], [r_esk])
                    cx.tt("dve", rden[:], rsum[:], esk[:], ALU.add, rsh[c % 3] + [r_esk], [r_rden])
                    cx.s.op("dve", lambda e, o_=rden[:], i_=rden[:]: e.reciprocal(out=o_, in_=i_), [r_rden], [r_rden])
                    for half in range(2):
                        bi = OB[half]
                        cx.tt("dve", otmp[:, half * 512:(half + 1) * 512].rearrange("p (h d) -> p h d", d=AD),
                              PS[bi][0][:, :].rearrange("p (h d) -> p h d", d=AD), bcast_last(rden[:, half * 8:(half + 1) * 8], AD), ALU.mult,
                              [PS[bi][1], r_rden], [r_otmp])
                    cx.tt("pool", og[:], otmp[:], sg[:], ALU.mult, [r_otmp, r_sg], [r_og])

                def BT(c):
                    pb = psb(1)
                    for k in range(8):
                        cx.tr(pb[:, k * 128:(k + 1) * 128], og[:, k * 128:(k + 1) * 128], ident_b[:], [r_og, r_identb], [PS[1][1]])
                    cx.cp("act", ogT[:].rearrange("p a b -> p (a b)"), pb[:, 0:1024], [PS[1][1]], [r_ogT])

                def BO(c):
                    for nb in range(2):
                        bi = nb
                        for k in range(8):
                            cx.mm(PS[bi][0][:, :], ogT[:, k, :], w_out[:, k, nb * 512:(nb + 1) * 512], k == 0, k == 7, [r_ogT, r_wout], [PS[bi][1]])
                    ot, r_ot = ot2[c % 2]
                    post_norm_a([0, 1], (junk_b, r_junk_b, ss2, r_ss2, ss_b, r_ss_b, tmp1_b, r_tmp1_b, rstd_b, r_rstd_b, ot, r_ot))

                def BP(c):
                    xr, r_xr = xr2[c % 2]
                    ot, r_ot = ot2[c % 2]
                    post_norm_b([0, 1], xr[:], r_xr, postw[:], r_postw,
                                (junk_b, r_junk_b, ss2, r_ss2, ss_b, r_ss_b, tmp1_b, r_tmp1_b, rstd_b, r_rstd_b, ot, r_ot),
                                x_dst[c * CH:(c + 1) * CH, :], x_dst_res[c], "a_st%d" % (c % 2))

                def okc(c):
                    return 0 <= c < nch

                def okt(t):
                    return 0 <= t < NT

                a_load(0)
                if nch > 1:
                    a_load(1)
                for u in range(-5, 4 * (nch - 1) + 17):
                    if (u - 15) % 4 == 0 and okc((u - 15) // 4):
                        BP((u - 15) // 4)
                    if (u - 12) % 4 == 0 and okc((u - 12) // 4):
                        BN((u - 12) // 4)
                    if (u + 5) % 4 == 0 and okc((u + 5) // 4):
                        F1((u + 5) // 4)
                    if okt(u - 1):
                        Mt_(u - 1)
                    if okt(u - 3):
                        Et_(u - 3)
                    if okt(u - 6):
                        Ct(u - 6)
                    if okt(u - 5):
                        Tt(u - 5)
                    if okt(u - 8):
                        Vt(u - 8)
                    cf, ph = divmod(u + 4, 4)
                    if okc(cf):
                        if ph == 0:
                            F2(cf)
                        elif ph == 1:
                            F3(cf)
                            F4(cf)
                            if cf + 2 < nch:
                                a_load(cf + 2)
                        elif ph == 2:
                            F45(cf)
                        else:
                            F6(cf)
                    if (u - 13) % 4 == 0 and okc((u - 13) // 4):
                        BT((u - 13) // 4)
                    if (u - 14) % 4 == 0 and okc((u - 14) // 4):
                        BO((u - 14) // 4)
                    if okt(u):
                        St(u)


        src = x_in
        src_res = [Res("xin_%d" % c, track=True) for c in range(nch)]
        for li, l in enumerate(layers):
            last = li == len(layers) - 1
            if last:
                dst = out_dram
                dst_res = [Res("xout_%d" % c) for c in range(nch)]
            else:
                dst = xs_scr[li % 2]
                dst_res = dres("xscr%d_%d" % (li % 2, li))
            if l % 2 == 0:
                ssd_layer(l, src, src_res, dst, dst_res)
            else:
                swa_layer(l, src, src_res, dst, dst_res)
            s.barrier()
            src, src_res = dst, dst_res

        block = st.enter_context(nc.Block())
        s.finish(block)
    return nc


def _rep(v, n=128):
    return np.ascontiguousarray(np.broadcast_to(np.asarray(v, np.float32)[None, :], (n, v.shape[0])))


def host_consts(nch):
    idx = np.arange(128)
    tri = (idx[:, None] <= idx[None, :]).astype(np.float32)
    strict = (idx[:, None] > idx[None, :]).astype(np.float32)
    qi = idx[:, None]
    kj = np.arange(256)[None, :]
    dist = qi + 128 - kj
    valid = (dist >= 0) & (dist < 128)
    m1 = np.where(valid, 0.0, NEG).astype(np.float32)
    m0 = np.where(valid & (kj >= 128), 0.0, NEG).astype(np.float32)
    amask = np.ascontiguousarray(np.stack([m0, m1], axis=1))
    invf = (500000.0 ** (-np.arange(0, 16, 2, dtype=np.float32) / 16.0)).astype(np.float32)
    sel4 = np.zeros((4, 512), np.float32)
    for r in range(4):
        sel4[r, r * 128:(r + 1) * 128] = 1.0
    return dict(ident=np.eye(128, dtype=np.float32), tri=tri, strict=strict, amask=amask, invf=_rep(invf), sel4=sel4)


def make_in_map(b, layers, nch, inputs):
    T = nch * CH
    m = dict(host_consts(nch))
    m["x"] = np.ascontiguousarray(inputs["x"][b, :T])
    pos = np.asarray(inputs["positions"][b, :T]).astype(np.int32)
    m["pos"] = np.ascontiguousarray(pos.reshape(nch, 128).T)
    m["prew"] = np.ascontiguousarray(np.broadcast_to(np.asarray(inputs["pre_norm"], np.float32)[:, None, :], (DEPTH, 128, D)))
    m["postw"] = np.ascontiguousarray(np.broadcast_to(np.asarray(inputs["post_norm"], np.float32)[:, None, :], (DEPTH, 128, D)))
    for l in layers:
        j = l // 2
        if l % 2 == 0:
            m["s%d_w_in" % j] = np.ascontiguousarray(inputs["ssm_w_in"][j])
            cw = np.asarray(inputs["ssm_conv_w"][j], np.float32)
            m["s%d_conv_w" % j] = np.ascontiguousarray(cw.reshape(4, 32, 128).transpose(2, 0, 1))
            cbv = np.asarray(inputs["ssm_conv_b"][j], np.float32)
            m["s%d_conv_b" % j] = np.ascontiguousarray(cbv.reshape(8, 4, 128).transpose(1, 0, 2).reshape(4, 1024))
            m["s%d_dtb" % j] = _rep(inputs["ssm_dt_bias"][j])
            m["s%d_alog" % j] = _rep(inputs["ssm_a_log"][j])
            m["s%d_d" % j] = _rep(inputs["ssm_d"][j])
            m["s%d_gwk" % j] = np.ascontiguousarray(np.asarray(inputs["ssm_gate_norm"][j], np.float32).reshape(16, 128).T)
            m["s%d_w_out" % j] = np.ascontiguousarray(inputs["ssm_w_out"][j])
        else:
            m["a%d_w_in" % j] = np.ascontiguousarray(inputs["att_w_in"][j])
            m["a%d_sinks" % j] = _rep(inputs["att_sinks"][j])
            m["a%d_w_out" % j] = np.ascontiguousarray(inputs["att_w_out"][j])
    return m


_NC_CACHE = {}


def kernel(**inputs):
    inputs = {k: np.asarray(v) for k, v in inputs.items()}
    layers = (0, 1, 2, 3)
    key = (layers, NCHUNK)
    if key not in _NC_CACHE:
        _NC_CACHE[key] = build_program(list(layers), NCHUNK)
    nc = _NC_CACHE[key]
    in_maps = [make_in_map(b, layers, NCHUNK, inputs) for b in range(BATCH)]
    res = run_bass_kernel_spmd(nc, in_maps, core_ids=list(range(BATCH)))
    out = np.stack([np.asarray(r["out"]).reshape(SEQ, D) for r in res.results], axis=0)
    return out.astype(np.float32)
```

```python
import numpy as np
import concourse.bass as bass
import concourse.mybir as mybir
from concourse.bass_utils import run_bass_kernel_spmd
from contextlib import ExitStack

F32 = mybir.dt.float32
BF16 = mybir.dt.bfloat16
I32 = mybir.dt.int32
AF = mybir.ActivationFunctionType
ALU = mybir.AluOpType

D = 1024
SEQ = 8192
BATCH = 4
DEPTH = 4
CH = 128
NCHUNK = SEQ // CH
EPS = 1e-6
DI = 2048
NH = 32
HP = 64
NG = 8
NST = 128
XBC = 4096
SSM_IN = 6176
AQ = 16
AKV = 4
AD = 64
ATT_IN = 2560
TWO_PI_HI = 6.28125
TWO_PI_LO = 2.0 * np.pi - 6.28125
NEG = -30000.0


class Res:
    __slots__ = ("name", "w", "r", "track", "excl")

    def __init__(self, name, track=True, excl=False):
        self.name = name
        self.w = None
        self.r = {}
        self.track = track
        self.excl = excl


class Sched:
    EPOCH = 30000
    LIMIT = None
    SAME_ENGINE_SYNC = ("act", "dve", "pool")

    def __init__(self, nc, st):
        self.nops = 0
        self.nc = nc
        self.st = st
        self.engs = ["pe", "act", "dve", "pool", "sp"]
        self.prog = {k: [] for k in self.engs}
        self.semobj = {}
        self.owner = {}
        self.cur = {}
        self.cnt = {}
        self.waited = {k: {} for k in self.engs}
        self.nsem = 0
        for k in ["pe", "act", "dve", "pool"]:
            self._new_epoch(k)
        self.dma_cnt = {}

    def _mksem(self, key, owner):
        s = self.st.enter_context(self.nc.semaphore("s_%s" % key))
        self.semobj[key] = s
        self.owner[key] = owner
        self.nsem += 1
        return s

    def _new_epoch(self, eng):
        key = "%s%d" % (eng, self.nsem)
        self._mksem(key, eng)
        self.cur[eng] = key
        self.cnt[eng] = 0

    def _deps(self, reads, writes):
        deps = {}

        def add(tok):
            if tok is None:
                return
            k, v = tok
            if deps.get(k, 0) < v:
                deps[k] = v
        for r in reads:
            add(r.w)
        for w in writes:
            add(w.w)
            for k, v in w.r.items():
                add((k, v))
        return deps

    def _waits(self, eng, deps):
        prog = self.prog[eng]
        for k, v in deps.items():
            if self.owner[k] == eng and (eng == "pe" or eng not in Sched.SAME_ENGINE_SYNC):
                continue
            if self.waited[eng].get(k, 0) >= v:
                continue
            self.waited[eng][k] = v
            sem = self.semobj[k]
            prog.append(lambda e, sem=sem, v=v: e.wait_ge(sem, v))

    def _commit(self, tok, reads, writes):
        k, v = tok
        for r in reads:
            if r.track:
                if r.r.get(k, 0) < v:
                    r.r[k] = v
        for w in writes:
            w.w = tok
            w.r = {}

    def op(self, eng, fn, reads=(), writes=()):
        self.nops += 1
        if Sched.LIMIT is not None and self.nops > Sched.LIMIT:
            return None
        ex = [r for r in reads if r.excl]
        if ex:
            writes = list(writes) + [r for r in ex if r not in writes]
            reads = [r for r in reads if not r.excl]
        self._waits(eng, self._deps(reads, writes))
        if self.cnt[eng] >= self.EPOCH:
            self._new_epoch(eng)
        key = self.cur[eng]
        self.cnt[eng] += 1
        val = self.cnt[eng]
        sem = self.semobj[key]
        self.prog[eng].append(lambda e, fn=fn, sem=sem: fn(e).then_inc(sem, 1))
        tok = (key, val)
        self._commit(tok, reads, writes)
        return tok

    def dma(self, semname, out, in_, reads=(), writes=(), q="sp"):
        self.nops += 1
        if Sched.LIMIT is not None and self.nops > Sched.LIMIT:
            return None
        if semname not in self.semobj:
            self._mksem(semname, "dma")
            self.dma_cnt[semname] = 0
        deps = self._deps(reads, writes)
        if self.dma_cnt[semname] > 0:
            if deps.get(semname, 0) < self.dma_cnt[semname]:
                deps[semname] = self.dma_cnt[semname]
        self._waits(q, deps)
        self.dma_cnt[semname] += 16
        val = self.dma_cnt[semname]
        sem = self.semobj[semname]
        self.prog[q].append(lambda e, out=out, in_=in_, sem=sem: e.dma_start(out=out, in_=in_).then_inc(sem, 16))
        tok = (semname, val)
        self._commit(tok, reads, writes)
        return tok

    def dma_group(self, semname, items, reads=(), writes=(), q="sp"):
        if semname not in self.semobj:
            self._mksem(semname, "dma")
            self.dma_cnt[semname] = 0
        deps = self._deps(reads, writes)
        if self.dma_cnt[semname] > 0 and deps.get(semname, 0) < self.dma_cnt[semname]:
            deps[semname] = self.dma_cnt[semname]
        self._waits(q, deps)
        sem = self.semobj[semname]
        for (out, in_) in items:
            self.dma_cnt[semname] += 16
            self.prog[q].append(lambda e, out=out, in_=in_, sem=sem: e.dma_start(out=out, in_=in_).then_inc(sem, 16))
        tok = (semname, self.dma_cnt[semname])
        self._commit(tok, reads, writes)
        return tok

    def barrier(self):
        toks = {}
        for eng in ["pe", "act", "dve", "pool"]:
            if self.cnt[eng] > 0:
                toks[self.cur[eng]] = self.cnt[eng]
        for k, v in self.dma_cnt.items():
            if v > 0:
                toks[k] = v
        for eng in self.engs:
            for k, v in toks.items():
                if self.waited[eng].get(k, 0) >= v:
                    continue
                self.waited[eng][k] = v
                sem = self.semobj[k]
                self.prog[eng].append(lambda e, sem=sem, v=v: e.wait_ge(sem, v))

    def finish(self, block):
        for k, v in self.dma_cnt.items():
            if self.waited["sp"].get(k, 0) < v:
                sem = self.semobj[k]
                self.prog["sp"].append(lambda e, sem=sem, v=v: e.wait_ge(sem, v))
        progs = self.prog

        @block.sync
        def _(e):
            for f in progs["sp"]:
                f(e)

        @block.tensor
        def _(e):
            for f in progs["pe"]:
                f(e)

        @block.scalar
        def _(e):
            for f in progs["act"]:
                f(e)

        @block.vector
        def _(e):
            for f in progs["dve"]:
                f(e)

        @block.gpsimd
        def _(e):
            for f in progs["pool"]:
                f(e)


class Ctx:
    def __init__(self, nc, st):
        self.nc = nc
        self.st = st
        self.s = Sched(nc, st)
        self.pst = None

    def sb(self, name, shape, dt, track=True):
        return self.sb_in(self.st, name, shape, dt, track)

    def sb_in(self, stack, name, shape, dt, track=True):
        self.uid = getattr(self, "uid", 0) + 1
        t = stack.enter_context(self.nc.sbuf_tensor("t%d_%s" % (self.uid, name), list(shape), dt))
        return t, Res(name, track)

    def mm(self, out, lhsT, rhs, start, stop, reads, writes):
        return self.s.op("pe", lambda e: e.matmul(out, lhsT=lhsT, rhs=rhs, start=start, stop=stop), reads, writes)

    def tr(self, out, in_, ident, reads, writes):
        return self.s.op("pe", lambda e: e.transpose(out=out, in_=in_, identity=ident), reads, writes)

    def act(self, out, in_, func, reads, writes, bias=None, scale=None, accum_out=None):
        kw = {}
        if bias is not None:
            kw["bias"] = bias
        if scale is not None:
            kw["scale"] = scale
        if accum_out is not None:
            kw["accum_out"] = accum_out
        return self.s.op("act", lambda e: e.activation(out=out, in_=in_, func=func, **kw), reads, writes)

    def tt(self, eng, out, in0, in1, op, reads, writes):
        return self.s.op(eng, lambda e: e.tensor_tensor(out=out, in0=in0, in1=in1, op=op), reads, writes)

    def ts(self, eng, out, in0, s1, op0, reads, writes, s2=None, op1=None):
        if op1 is None:
            return self.s.op(eng, lambda e: e.tensor_scalar(out=out, in0=in0, scalar1=s1, scalar2=None, op0=op0), reads, writes)
        return self.s.op(eng, lambda e: e.tensor_scalar(out=out, in0=in0, scalar1=s1, scalar2=s2, op0=op0, op1=op1), reads, writes)

    def stt(self, out, in0, scalar, in1, op0, op1, reads, writes):
        return self.s.op("dve", lambda e: e.scalar_tensor_tensor(out=out, in0=in0, scalar=scalar, in1=in1, op0=op0, op1=op1), reads, writes)

    def cp(self, eng, out, in_, reads, writes):
        if eng == "act":
            return self.s.op("act", lambda e: e.activation(out=out, in_=in_, func=AF.Copy), reads, writes)
        return self.s.op(eng, lambda e: e.tensor_copy(out=out, in_=in_), reads, writes)

    def memset(self, eng, ap, val, writes):
        return self.s.op(eng, lambda e: e.memset(ap, val), (), writes)


def bcast_mid(ap2d, n):
    p, f = ap2d.shape
    return ap2d.unsqueeze(1).broadcast_to([p, n, f])


def bcast_last(ap2d, n):
    p, f = ap2d.shape
    return ap2d.unsqueeze(2).broadcast_to([p, f, n])


def build_program(layers, nch=NCHUNK, debug=False):
    nc = bass.Bass("TRN2", target_bir_lowering=False)
    T = nch * CH
    dr = {}

    def din(name, shape, dt):
        dr[name] = nc.dram_tensor(name, list(shape), dt, kind="ExternalInput").ap()
        return dr[name]

    def dscr(name, shape, dt):
        kind = "ExternalOutput" if debug else "Internal"
        dr[name] = nc.dram_tensor(name, list(shape), dt, kind=kind).ap()
        return dr[name]

    x_in = din("x", [T, D], F32)
    pos_in = din("pos", [128, nch], I32)
    invf_in = din("invf", [128, 8], F32)
    ident_in = din("ident", [128, 128], F32)
    tri_in = din("tri", [128, 128], F32)
    strict_in = din("strict", [128, 128], F32)
    mask_in = din("amask", [128, 2, 256], F32)
    sel4_in = din("sel4", [4, 512], F32)
    prew_in = din("prew", [DEPTH, 128, D], F32)
    postw_in = din("postw", [DEPTH, 128, D], F32)
    nS = sum(1 for l in layers if l % 2 == 0)
    nA = sum(1 for l in layers if l % 2 == 1)
    ssm = {}
    att = {}
    for l in layers:
        j = l // 2
        if l % 2 == 0:
            ssm[j] = dict(
                w_in=din("s%d_w_in" % j, [D, SSM_IN], F32),
                conv_w=din("s%d_conv_w" % j, [128, 4, 32], F32),
                conv_b=din("s%d_conv_b" % j, [4, 8 * 128], F32),
                dtb=din("s%d_dtb" % j, [128, NH], F32),
                alog=din("s%d_alog" % j, [128, NH], F32),
                dsk=din("s%d_d" % j, [128, NH], F32),
                gwk=din("s%d_gwk" % j, [128, 16], F32),
                w_out=din("s%d_w_out" % j, [DI, D], F32),
            )
        else:
            att[j] = dict(
                w_in=din("a%d_w_in" % j, [D, ATT_IN], F32),
                sinks=din("a%d_sinks" % j, [128, AQ], F32),
                w_out=din("a%d_w_out" % j, [D, D], F32),
            )
    out_dram = nc.dram_tensor("out", [T, D], F32, kind="ExternalOutput").ap()
    xs_scr = [dscr("xscr%d" % i, [T, D], F32) for i in range(2)] if len(layers) > 1 else []
    if nS:
        szd = dscr("szd", [nch, 128, DI], BF16)
        dtd = dscr("dtd", [nch, 128, NH], F32)
        xbd = dscr("xbd", [nch, 128, 32 * 128], BF16)
        ynd = dscr("ynd", [nch, 128, DI], BF16)

    with ExitStack() as st:
        cx = Ctx(nc, st)
        s = cx.s
        ident_f, r_identf = cx.sb("ident_f", [128, 128], F32, track=False)
        ident_b, r_identb = cx.sb("ident_b", [128, 128], BF16, track=False)
        tri_f, r_trif = cx.sb("tri_f", [128, 128], F32, track=False)
        tri_b, r_trib = cx.sb("tri_b", [128, 128], BF16, track=False)
        strict_b, r_strict = cx.sb("strict_f", [128, 128], F32, track=False)
        ones_f, r_onesf = cx.sb("ones_f", [128, 128], F32, track=False)
        ones_b, r_onesb = cx.sb("ones_b", [128, 128], BF16, track=False)
        mhalf, r_mhalf = cx.sb("mhalf", [128, 1], F32, track=False)
        s.dma("c_ld0", ident_f[:], ident_in[:, :], (), [r_identf])
        s.dma("c_ld1", tri_f[:], tri_in[:, :], (), [r_trif])
        s.dma("c_ld7", strict_b[:], strict_in[:, :], (), [r_strict])
        cx.cp("dve", ident_b[:], ident_f[:], [r_identf], [r_identb])
        cx.cp("dve", tri_b[:], tri_f[:], [r_trif], [r_trib])
        cx.memset("dve", ones_f[:], 1.0, [r_onesf])
        cx.memset("dve", ones_b[:], 1.0, [r_onesb])
        cx.memset("dve", mhalf[:], -0.5, [r_mhalf])
        PS = []
        for i in range(8):
            t = st.enter_context(nc.psum_tensor("ps%d" % i, [128, 512], F32))
            PS.append((t, Res("ps%d" % i, excl=True)))

        def psb(i):
            return PS[i][0][:].bitcast(BF16)

        x_res_chunks = {}

        def dres(name):
            if name not in x_res_chunks:
                x_res_chunks[name] = [Res("%s_%d" % (name, c)) for c in range(nch)]
            return x_res_chunks[name]

        def rstd_from_ss(ss_ap, n, rstd_ap, tmp_ap, reads, r_tmp, r_rstd):
            cx.ts("dve", tmp_ap, ss_ap, 1.0 / n, ALU.mult, reads, [r_tmp], s2=EPS, op1=ALU.add)
            cx.tt("pool", rstd_ap, tmp_ap, mhalf[:, 0:1], ALU.pow, [r_tmp, r_mhalf], [r_rstd])

        def load_weights_bf16(stack, name, w_dram, ncols, c0=0):
            K = w_dram.shape[0]
            kc = K // 128
            wt, r_w = cx.sb_in(stack, name, [128, kc, ncols], BF16, track=False)
            src = w_dram.rearrange("(k p) n -> p k n", p=128)
            items = []
            step = 512
            for a in range(0, ncols, step):
                b = min(ncols, a + step)
                items.append((wt[:, :, a:b], src[:, :, c0 + a:c0 + b]))
            s.dma_group("wld_" + name, items, (), [r_w], q="pool")
            return wt, r_w

        def interleave(*gens):
            gens = [g for g in gens if g is not None]
            while gens:
                alive = []
                for g in gens:
                    try:
                        next(g)
                        alive.append(g)
                    except StopIteration:
                        pass
                gens = alive

        def interleave_pattern(g1, g2, pattern):
            gens = {"1": g1, "2": g2}
            for ch in pattern:
                g = gens[ch]
                if g is None:
                    continue
                try:
                    next(g)
                except StopIteration:
                    gens[ch] = None
            interleave(gens["1"], gens["2"])

        def norm_chain(tiles, xt, r_x, prew, r_prew):
            (junk, r_junk, ss, r_ss, tmp1, r_tmp1, rstd, r_rstd, hb, r_hb) = tiles
            cx.act(junk[:], xt, AF.Square, [r_x], [r_junk, r_ss], accum_out=ss[:, 0:1])
            rstd_from_ss(ss[:, 0:1], D, rstd[:, 0:1], tmp1[:, 0:1], [r_ss], r_tmp1, r_rstd)
            cx.stt(hb[:], xt, rstd[:, 0:1], prew, ALU.mult, ALU.mult, [r_x, r_rstd, r_prew], [r_hb])

        def transpose_h(hb, r_hb, hT, r_hT, bank):
            pb = psb(bank)
            for k in range(8):
                cx.tr(pb[:, k * 128:(k + 1) * 128], hb[:, k * 128:(k + 1) * 128], ident_b[:], [r_hb, r_identb], [PS[bank][1]])
            cx.cp("act", hT[:].rearrange("p a b -> p (a b)"), pb[:, 0:1024], [PS[bank][1]], [r_hT])

        def post_norm_residual(c, o_banks, xt, r_x, postw, r_postw, tiles, out_ap, r_out, semname):
            post_norm_a(o_banks, tiles)
            post_norm_b(o_banks, xt, r_x, postw, r_postw, tiles, out_ap, r_out, semname)

        def post_norm_a(o_banks, tiles):
            (junk, r_junk, ss2, r_ss2, ss, r_ss, tmp1, r_tmp1, rstd, r_rstd, ot, r_ot) = tiles
            for i, bi in enumerate(o_banks):
                cx.act(junk[:, 0:512], PS[bi][0][:, :], AF.Square, [PS[bi][1]], [r_junk, r_ss2[i]], accum_out=ss2[:, i:i + 1])
            cx.tt("dve", ss[:, 0:1], ss2[:, 0:1], ss2[:, 1:2], ALU.add, r_ss2, [r_ss])
            rstd_from_ss(ss[:, 0:1], D, rstd[:, 0:1], tmp1[:, 0:1], [r_ss], r_tmp1, r_rstd)

        def post_norm_b(o_banks, xt, r_x, postw, r_postw, tiles, out_ap, r_out, semname):
            (junk, r_junk, ss2, r_ss2, ss, r_ss, tmp1, r_tmp1, rstd, r_rstd, ot, r_ot) = tiles
            for i, bi in enumerate(o_banks):
                cx.stt(ot[:, i * 512:(i + 1) * 512], PS[bi][0][:, :], rstd[:, 0:1], postw[:, i * 512:(i + 1) * 512],
                       ALU.mult, ALU.mult, [PS[bi][1], r_rstd, r_postw], [r_ot[i]])
            cx.tt("pool", ot[:], ot[:], xt, ALU.add, r_ot + [r_x], r_ot)
            s.dma(semname, out_ap, ot[:], r_ot, [r_out])

        def ssd_layer(l, x_src, x_src_res, x_dst, x_dst_res):
            j = l // 2
            P = ssm[j]
            r_sz = dres("szd")
            r_dt = dres("dtd")
            r_xb = dres("xbd")
            r_yn = dres("ynd")
            psc = ExitStack()
            diag, r_diag = cx.sb_in(psc, "diag", [128, 4, 32, 128], BF16, track=False)
            cw, r_cw = cx.sb_in(psc, "cw", [128, 4, 32], F32, track=False)
            cb_row, r_cbrow = cx.sb_in(psc, "cb_row", [4, 8 * 128], BF16, track=False)
            sel4, r_sel4 = cx.sb_in(psc, "sel4", [4, 512], BF16, track=False)
            dtb, r_dtb = cx.sb_in(psc, "dtb", [128, NH], F32, track=False)
            A_b, r_Ab = cx.sb_in(psc, "A_b", [128, NH], F32, track=False)
            D_b, r_Db = cx.sb_in(psc, "D_b", [128, NH], F32, track=False)
            s.dma("p_ld0", sel4[:], sel4_in[:, :], (), [r_sel4], q="pool")
            s.dma("c_ld0", cw[:], P["conv_w"][:, :, :], (), [r_cw])
            s.dma("p_ld1", cb_row[:], P["conv_b"][:, :], (), [r_cbrow], q="pool")
            s.dma("c_ld1", dtb[:], P["dtb"][:, :], (), [r_dtb])
            s.dma("c_ld3", A_b[:], P["alog"][:, :], (), [r_Ab])
            s.dma("c_ld4", D_b[:], P["dsk"][:, :], (), [r_Db])
            cx.act(A_b[:], A_b[:], AF.Exp, [r_Ab], [r_Ab])
            cx.ts("dve", A_b[:], A_b[:], -1.0, ALU.mult, [r_Ab], [r_Ab])
            for k in range(4):
                for blk in range(32):
                    cx.ts("dve", diag[:, k, blk, :], ident_f[:], cw[:, k, blk:blk + 1], ALU.mult, [r_identf, r_cw], [r_diag])
            with ExitStack() as ps1:
                w_in, r_win = load_weights_bf16(ps1, "w_in", P["w_in"], SSM_IN)
                prew, r_prew = cx.sb_in(ps1, "prew", [128, D], F32, track=False)
                s.dma("c_ld0", prew[:], prew_in[l], (), [r_prew])
                xt2 = [cx.sb_in(ps1, "s1_x%d" % i, [128, D], F32) for i in range(2)]
                junk, r_junk = cx.sb_in(ps1, "s1_junk", [128, D], BF16)
                ss, r_ss = cx.sb_in(ps1, "s1_ss", [128, 1], F32)
                tmp1, r_tmp1 = cx.sb_in(ps1, "s1_tmp1", [128, 1], F32)
                rstd, r_rstd = cx.sb_in(ps1, "s1_rstd", [128, 1], F32)
                hb, r_hb = cx.sb_in(ps1, "s1_hb", [128, D], BF16)
                hT2 = [cx.sb_in(ps1, "s1_hT%d" % i, [128, 8, 128], BF16) for i in range(2)]
                sz2 = [cx.sb_in(ps1, "s1_sz%d" % i, [128, DI], BF16) for i in range(2)]
                dt2 = [cx.sb_in(ps1, "s1_dt%d" % i, [128, NH], F32) for i in range(2)]
                xb2 = [cx.sb_in(ps1, "s1_xb%d" % i, [128, 32, 128], BF16) for i in range(2)]
                xbq = [[Res("s1_xbq%d_%d" % (i, q)) for q in range(8)] for i in range(2)]
                szq = [[Res("s1_szq%d_%d" % (i, q)) for q in range(4)] for i in range(2)]
                ntiles = (junk, r_junk, ss, r_ss, tmp1, r_tmp1, rstd, r_rstd, hb, r_hb)

                def s1_load(c):
                    s.dma("s1_ldx%d" % (c % 2), xt2[c % 2][0][:], x_src[c * CH:(c + 1) * CH, :], [x_src_res[c]], [xt2[c % 2][1]])
                s1_load(0)
                if nch > 1:
                    s1_load(1)
                norm_chain(ntiles, xt2[0][0][:], xt2[0][1], prew[:], r_prew)
                transpose_h(hb, r_hb, hT2[0][0], hT2[0][1], 0)
                for c in range(nch):
                    sl = c % 2
                    hT, r_hT = hT2[sl]
                    if c + 2 < nch:
                        s1_load(c + 2)
                    if c + 1 < nch:
                        norm_chain(ntiles, xt2[1 - sl][0][:], xt2[1 - sl][1], prew[:], r_prew)
                    szt, r_szt = sz2[sl]
                    for nb in range(4):
                        bi = 1 + (nb % 2)
                        for k in range(8):
                            cx.mm(PS[bi][0][:, :], hT[:, k, :], w_in[:, k, nb * 512:(nb + 1) * 512], k == 0, k == 7,
                                  [r_hT, r_win], [PS[bi][1]])
                        cx.act(szt[:, nb * 512:(nb + 1) * 512], PS[bi][0][:, :], AF.Silu, [PS[bi][1]], [szq[sl][nb]])
                    s.dma("s1_stz%d" % sl, szd[c], szt[:], szq[sl], [r_sz[c]])
                    dtt, r_dtt = dt2[sl]
                    for k in range(8):
                        cx.mm(PS[3][0][:, 0:NH], hT[:, k, :], w_in[:, k, 6144:6176], k == 0, k == 7, [r_hT, r_win], [PS[3][1]])
                    cx.cp("dve", dtt[:], PS[3][0][:, 0:NH], [PS[3][1]], [r_dtt])
                    s.dma("s1_stdt%d" % sl, dtd[c], dtt[:], [r_dtt], [r_dt[c]])
                    xbt, r_xbt = xb2[sl]
                    for q in range(8):
                        bi = 4 + (q % 4)
                        for b4 in range(4):
                            blk = q * 4 + b4
                            for k in range(8):
                                cx.mm(PS[bi][0][:, b4 * 128:(b4 + 1) * 128], w_in[:, k, 2048 + blk * 128:2048 + (blk + 1) * 128],
                                      hT[:, k, :], k == 0, k == 7, [r_hT, r_win], [PS[bi][1]])
                        eng = "dve" if q % 2 == 0 else "act"
                        cx.cp(eng, xbt[:, q * 4:(q + 1) * 4, :].rearrange("p a b -> p (a b)"), PS[bi][0][:, :], [PS[bi][1]], [xbq[sl][q]])
                        if q == 3 and c + 1 < nch:
                            transpose_h(hb, r_hb, hT2[1 - sl][0], hT2[1 - sl][1], 0)
                    s.dma("s1_stx%d" % sl, xbd[c], xbt[:].rearrange("p a b -> p (a b)"), xbq[sl], [r_xb[c]])

            s.barrier()
            with ExitStack() as ps2:
                xb2 = [cx.sb_in(ps2, "s2_xb%d" % i, [128, 32, 131], BF16) for i in range(2)]
                sz3 = [cx.sb_in(ps2, "s2_sz%d" % i, [128, DI], BF16) for i in range(3)]
                GB = 8
                dt3 = [cx.sb_in(ps2, "s2_dt%d" % i, [128, GB, NH], F32) for i in range(2)]
                yn2 = [cx.sb_in(ps2, "s2_yn%d" % i, [128, DI], BF16) for i in range(2)]
                t0, r_t0 = cx.sb_in(ps2, "s2_t0", [128, GB, NH], F32)
                t1, r_t1 = cx.sb_in(ps2, "s2_t1", [128, GB, NH], F32)
                t2, r_t2 = cx.sb_in(ps2, "s2_t2", [128, GB, NH], F32)
                av2 = [cx.sb_in(ps2, "s2_av%d" % i, [128, GB, NH], F32) for i in range(2)]
                ahi, r_ahi = cx.sb_in(ps2, "s2_ahi", [128, NH], BF16)
                ahf, r_ahf = cx.sb_in(ps2, "s2_ahf", [128, NH], F32)
                alo, r_alo = cx.sb_in(ps2, "s2_alo", [128, NH], BF16)
                acs, r_acs = cx.sb_in(ps2, "s2_acs", [128, GB, NH], F32)
                dtv2 = [cx.sb_in(ps2, "s2_dtv%d" % i, [128, GB, NH], F32) for i in range(2)]
                eacs2 = [cx.sb_in(ps2, "s2_eacs%d" % i, [128, GB, NH], F32) for i in range(2)]
                dte2 = [cx.sb_in(ps2, "s2_dte%d" % i, [128, GB, NH], F32) for i in range(2)]
                cdv2 = [cx.sb_in(ps2, "s2_cdv%d" % i, [128, GB, NH], F32) for i in range(2)]
                xc, r_xc = cx.sb_in(ps2, "s2_xc", [128, 24, 128], BF16)
                cT2 = [cx.sb_in(ps2, "s2_cT%d" % i, [128, 8, 128], BF16) for i in range(2)]
                xs_tok, r_xst = cx.sb_in(ps2, "s2_xs_tok", [128, DI], BF16)
                skb2 = [cx.sb_in(ps2, "s2_skb%d" % i, [128, DI], BF16) for i in range(2)]
                Bt2 = [cx.sb_in(ps2, "s2_Bt%d" % i, [128, 1024], BF16) for i in range(2)]
                xdt2 = [cx.sb_in(ps2, "s2_xdt%d" % i, [128, DI], BF16) for i in range(2)]
                xdte2 = [cx.sb_in(ps2, "s2_xdte%d" % i, [128, DI], BF16) for i in range(2)]
                cbm = cx.sb_in(ps2, "s2_cbm", [128, 8, 128], BF16)[0]
                r_cbm = [Res("s2_cbm_%d" % k) for k in range(2)]
                rhi, r_rhi = cx.sb_in(ps2, "s2_rf", [128, NH, 128], F32)
                rfq = [Res("s2_rfq%d" % q) for q in range(4)]
                Mt2 = [cx.sb_in(ps2, "s2_Mt%d" % i, [128, NH, 128], BF16) for i in range(2)]
                Mtq = [[Res("s2_Mtq%d_%d" % (i, q)) for q in range(8)] for i in range(2)]
                xcq = [Res("s2_xcq%d" % q) for q in range(6)]
                cTq = [[Res("s2_cTq%d_%d" % (i, q)) for q in range(2)] for i in range(2)]
                xsth = [Res("s2_xsth%d" % q) for q in range(2)]
                skbh = [[Res("s2_skbh%d_%d" % (i, q)) for q in range(2)] for i in range(2)]
                yvq = [Res("s2_yvq%d" % q) for q in range(4)]
                hTsq = [Res("s2_hTsq%d" % q) for q in range(4)]
                hTs, r_hTs = cx.sb_in(ps2, "s2_hTs", [128, DI], F32)
                hTb, r_hTb = cx.sb_in(ps2, "s2_hTb", [128, DI], BF16)
                htmp, r_htmp = cx.sb_in(ps2, "s2_htmp", [128, DI], F32)
                yv, r_yv = cx.sb_in(ps2, "s2_yv", [128, DI], F32)
                gjunk, r_gjunk = cx.sb_in(ps2, "s2_gjunk", [128, DI], BF16)
                ssg, r_ssg = cx.sb_in(ps2, "s2_ssg", [128, 1], F32)
                tmpg, r_tmpg = cx.sb_in(ps2, "s2_tmpg", [128, 1], F32)
                rstdg, r_rstdg = cx.sb_in(ps2, "s2_rstdg", [128, 1], F32)
                cx.memset("pool", hTs[:], 0.0, hTsq)
                cx.memset("pool", hTb[:], 0.0, [r_hTb])

                def s2_load(c):
                    xbt, r_xbt = xb2[c % 2]
                    s.dma("s2_ldx%d" % (c % 2), xbt[:, :, 3:131], xbd[c].rearrange("p (a b) -> p a b", b=128), [r_xb[c]], [r_xbt])
                    s.dma("s2_ldz%d" % (c % 3), sz3[c % 3][0][:], szd[c], [r_sz[c]], [sz3[c % 3][1]])
                    if c % GB == 0:
                        gn = min(GB, nch - c)
                        gi = (c // GB) % 2
                        s.dma("s2_ldd%d" % gi, dt3[gi][0][:, 0:gn, :], dtd[c:c + gn].rearrange("c p h -> p c h"),
                              [r_dt[cc] for cc in range(c, c + gn)], [dt3[gi][1]])

                def stage1(c):
                    sl = c % 2
                    xbt, r_xbt = xb2[sl]
                    gi = (c // GB) % 2
                    ci = c % GB
                    gn = min(GB, nch - (c - ci))
                    dtr8, r_dtr = dt3[gi]
                    dtv8, r_dtv = dtv2[gi]
                    eacs8, r_eacs = eacs2[gi]
                    dte8, r_dte = dte2[gi]
                    cdv8, r_cdv = cdv2[gi]
                    av8, r_av = av2[gi]
                    dtv = dtv8[:, ci, :]
                    dte = dte8[:, ci, :]
                    av = av8[:, ci, :]
                    cT, r_cT = cT2[sl]
                    skb, r_skb = skb2[sl]
                    B_tok, r_Bt = Bt2[sl]
                    xdt, r_xdt = xdt2[sl]
                    xdte, r_xdte = xdte2[sl]
                    Mt, r_Mt = Mt2[sl]
                    if c == 0:
                        cx.memset("pool", xbt[:, :, 0:3], 0.0, [r_xbt])
                    else:
                        pv, r_pv = xb2[1 - sl]
                        cx.cp("pool", xbt[:, :, 0:3], pv[:, :, 128:131], [r_pv], [r_xbt])
                    if c + 1 < nch:
                        s2_load(c + 1)
                    if ci == 0:
                        W = gn * NH
                        g2 = lambda t: t[:, 0:gn, :]
                        f2 = lambda t: t[:, 0:gn, :].rearrange("p a b -> p (a b)")
                        cx.tt("dve", g2(t0), g2(dtr8), bcast_mid(dtb[:], gn), ALU.add, [r_dtr, r_dtb], [r_t0])
                        cx.stt(f2(t1), f2(t0), -1.0, f2(t0), ALU.mult, ALU.max, [r_t0], [r_t1])
                        cx.act(f2(t1), f2(t1), AF.Exp, [r_t1], [r_t1], scale=-1.0)
                        cx.act(f2(t2), f2(t1), AF.Ln, [r_t1], [r_t2], bias=1.0)
                        cx.stt(f2(dtv8), f2(t0), 0.0, f2(t2), ALU.max, ALU.add, [r_t0, r_t2], [r_dtv])
                        cx.tt("dve", g2(av8), g2(dtv8), bcast_mid(A_b[:], gn), ALU.mult, [r_dtv, r_Ab], [r_av])
                        cx.mm(PS[0][0][:, 0:W], tri_f[:], f2(av8), True, True, [r_trif, r_av], [PS[0][1]])
                        cx.cp("dve", f2(acs), PS[0][0][:, 0:W], [PS[0][1]], [r_acs])
                        cx.mm(PS[0][0][:, 0:W], ones_f[:], f2(av8), True, True, [r_onesf, r_av], [PS[0][1]])
                        cx.tt("dve", f2(t0), PS[0][0][:, 0:W], f2(acs), ALU.subtract, [PS[0][1], r_acs], [r_t0])
                        cx.act(f2(cdv8), PS[0][0][:, 0:W], AF.Exp, [PS[0][1]], [r_cdv])
                        cx.act(f2(eacs8), f2(acs), AF.Exp, [r_acs], [r_eacs])
                        cx.act(f2(dte8), f2(t0), AF.Exp, [r_t0], [r_dte])
                    yield
                    def mask_q(q4):
                        cx.tt("dve", rhi[:, q4 * 8:(q4 + 1) * 8, :], bcast_mid(tri_f[:], 8), bcast_last(av[:, q4 * 8:(q4 + 1) * 8], 128), ALU.mult,
                              [r_trif, r_av], [rfq[q4]])
                    for q in range(8):
                        bi = 1 + (q % 2)
                        cx.mm(PS[bi][0][:, :], cb_row[0:4, q * 128:(q + 1) * 128], sel4[0:4, :], True, False, [r_cbrow, r_sel4], [PS[bi][1]])
                        for b4 in range(4):
                            blk = q * 4 + b4
                            o = PS[bi][0][:, b4 * 128:(b4 + 1) * 128]
                            for k in range(4):
                                cx.mm(o, diag[:, k, blk, :], xbt[:, blk, k:k + 128], False, (k == 3 and b4 == 3), [r_diag, r_xbt], [PS[bi][1]])
                        if q < 6:
                            cx.act(xc[:, q * 4:(q + 1) * 4, :].rearrange("p a b -> p (a b)"), PS[bi][0][:, :], AF.Silu, [PS[bi][1]], [xcq[q]])
                        else:
                            cx.act(cT[:, (q - 6) * 4:(q - 5) * 4, :].rearrange("p a b -> p (a b)"), PS[bi][0][:, :], AF.Silu, [PS[bi][1]], [cTq[sl][q - 6]])
                        yield
                        if q == 1:
                            mask_q(0)
                            yield
                            mask_q(1)
                            yield
                        if q == 3:
                            mask_q(2)
                            yield
                            mask_q(3)
                            yield
                    for half in range(2):
                        bi = 3 if half == 0 else 0
                        pb = psb(bi)
                        for b8 in range(8):
                            blk = half * 8 + b8
                            cx.tr(pb[:, b8 * 128:(b8 + 1) * 128], xc[:, blk, :], ident_b[:], [xcq[blk // 4], r_identb], [PS[bi][1]])
                        cx.cp("act", xs_tok[:, half * 1024:(half + 1) * 1024], pb[:, 0:1024], [PS[bi][1]], [xsth[half]])
                        cx.tt("dve", skb[:, half * 1024:(half + 1) * 1024].rearrange("p (h d) -> p h d", d=HP),
                              pb[:, 0:1024].rearrange("p (h d) -> p h d", d=HP), bcast_last(D_b[:, half * 16:(half + 1) * 16], HP), ALU.mult,
                              [PS[bi][1], r_Db], [skbh[sl][half]])
                        yield
                    pb = psb(3)
                    for b8 in range(8):
                        cx.tr(pb[:, b8 * 128:(b8 + 1) * 128], xc[:, 16 + b8, :], ident_b[:], [xcq[4 + b8 // 4], r_identb], [PS[3][1]])
                    cx.cp("act", B_tok[:], pb[:, 0:1024], [PS[3][1]], [r_Bt])
                    cx.tt("pool", xdt[:].rearrange("p (h d) -> p h d", d=HP), xs_tok[:].rearrange("p (h d) -> p h d", d=HP),
                          bcast_last(dtv, HP), ALU.mult, xsth + [r_dtv], [r_xdt])
                    yield
                    cx.tt("pool", xdte[:].rearrange("p (h d) -> p h d", d=HP), xdt[:].rearrange("p (h d) -> p h d", d=HP),
                          bcast_last(dte, HP), ALU.mult, [r_xdt, r_dte], [r_xdte])
                    for half in range(2):
                        bi = 1 + half
                        for g4 in range(4):
                            g = half * 4 + g4
                            cx.mm(PS[bi][0][:, g4 * 128:(g4 + 1) * 128], xc[:, 16 + g, :], cT[:, g, :], True, True, [xcq[4 + g // 4], cTq[sl][g // 4]], [PS[bi][1]])
                        cx.tt("dve", cbm[:, half * 4:(half + 1) * 4, :], PS[bi][0][:, :].rearrange("p (a b) -> p a b", b=128),
                              bcast_mid(tri_b[:], 4), ALU.mult, [PS[bi][1], r_trib], [r_cbm[half]])
                    yield
                    for q in range(8):
                        bi = q % 4
                        cx.mm(PS[bi][0][:, :], strict_b[:], rhi[:, q * 4:(q + 1) * 4, :].rearrange("p a b -> p (a b)"), True, True,
                              [r_strict, rfq[q // 2]], [PS[bi][1]])
                        cx.act(Mt[:, q * 4:(q + 1) * 4, :].rearrange("p a b -> p (a b)"), PS[bi][0][:, :], AF.Exp, [PS[bi][1]], [Mtq[sl][q]])
                        cx.tt("dve", Mt[:, q * 4:(q + 1) * 4, :], Mt[:, q * 4:(q + 1) * 4, :], bcast_mid(cbm[:, q, :], 4), ALU.mult,
                              [Mtq[sl][q], r_cbm[q // 4]], [Mtq[sl][q]])
                        yield

                def stage2(c):
                    sl = c % 2
                    szt, r_szt = sz3[c % 3]
                    ynt, r_ynt = yn2[sl]
                    gi = (c // GB) % 2
                    eacs8, r_eacs = eacs2[gi]
                    cdv8, r_cdv = cdv2[gi]
                    eacs = eacs8[:, c % GB, :]
                    cdv = cdv8[:, c % GB, :]
                    cT, r_cT = cT2[sl]
                    skb, r_skb = skb2[sl]
                    B_tok, r_Bt = Bt2[sl]
                    xdt, r_xdt = xdt2[sl]
                    xdte, r_xdte = xdte2[sl]
                    Mt, r_Mt = Mt2[sl]
                    ytmp, r_ytmp = htmp, r_htmp
                    for hh in range(2):
                        for b2 in range(2):
                            bi = 4 + b2
                            h0 = hh * 16 + b2 * 8
                            cx.mm(PS[bi][0][:, :], ident_b[:], skb[:, h0 * HP:(h0 + 8) * HP], True, False, [r_identb, skbh[sl][hh]], [PS[bi][1]])
                            for h8 in range(8):
                                h = h0 + h8
                                cx.mm(PS[bi][0][:, h8 * 64:(h8 + 1) * 64], Mt[:, h, :], xdt[:, h * HP:(h + 1) * HP], False, h8 == 7,
                                      [Mtq[sl][h // 4], r_xdt], [PS[bi][1]])
                        for g4 in range(4):
                            g = hh * 4 + g4
                            bi = 6 + (g4 // 2)
                            cx.mm(PS[bi][0][:, (g4 % 2) * 256:(g4 % 2 + 1) * 256], cT[:, g, :], hTb[:, g * 256:(g + 1) * 256], True, True,
                                  [cTq[sl][g // 4], r_hTb], [PS[bi][1]])
                        yield
                        for b2 in range(2):
                            h0 = hh * 16 + b2 * 8
                            cx.tt("dve", ytmp[:, b2 * 512:(b2 + 1) * 512].rearrange("p (h d) -> p h d", d=HP),
                                  PS[6 + b2][0][:, :].rearrange("p (h d) -> p h d", d=HP), bcast_last(eacs[:, h0:h0 + 8], HP), ALU.mult,
                                  [PS[6 + b2][1], r_eacs], [r_ytmp])
                            cx.tt("dve", yv[:, h0 * HP:(h0 + 8) * HP], PS[4 + b2][0][:, :], ytmp[:, b2 * 512:(b2 + 1) * 512], ALU.add,
                                  [PS[4 + b2][1], r_ytmp], [yvq[hh * 2 + b2]])
                            yield
                    cx.tt("pool", yv[:], yv[:], szt[:], ALU.mult, yvq + [r_szt], yvq)
                    yield
                    cx.act(gjunk[:], yv[:], AF.Square, yvq, [r_gjunk, r_ssg], accum_out=ssg[:, 0:1])
                    rstd_from_ss(ssg[:, 0:1], DI, rstdg[:, 0:1], tmpg[:, 0:1], [r_ssg], r_tmpg, r_rstdg)
                    cx.act(ynt[:], yv[:], AF.Copy, yvq + [r_rstdg], [r_ynt], scale=rstdg[:, 0:1])
                    s.dma("s2_sty%d" % sl, ynd[c], ynt[:], [r_ynt], [r_yn[c]])
                    yield
                    cx.tt("pool", htmp[:].rearrange("p (h d) -> p h d", d=HP), hTs[:].rearrange("p (h d) -> p h d", d=HP),
                          bcast_last(cdv, HP), ALU.mult, hTsq + [r_cdv], [r_htmp])
                    yield
                    for q in range(4):
                        bi = 4 + q
                        for g2 in range(2):
                            g = q * 2 + g2
                            cx.mm(PS[bi][0][:, g2 * 256:(g2 + 1) * 256], B_tok[:, g * 128:(g + 1) * 128], xdte[:, g * 256:(g + 1) * 256], True, True,
                                  [r_Bt, r_xdte], [PS[bi][1]])
                        cx.tt("dve", hTs[:, q * 512:(q + 1) * 512], PS[bi][0][:, :], htmp[:, q * 512:(q + 1) * 512], ALU.add,
                              [PS[bi][1], r_htmp], [hTsq[q]])
                        yield
                    cx.cp("act", hTb[:], hTs[:], hTsq, [r_hTb])

                s2_load(0)
                interleave(stage1(0))
                for c in range(nch):
                    interleave_pattern(stage1(c + 1) if c + 1 < nch else None, stage2(c),
                                       "121121212112121121212121212121211111111")

            s.barrier()
            psc.close()
            s.barrier()
            with ExitStack() as ps3:
                w_out, r_wout = load_weights_bf16(ps3, "w_out", P["w_out"], D)
                postw, r_postw = cx.sb_in(ps3, "postw", [128, D], F32, track=False)
                s.dma("c_ld0", postw[:], postw_in[l], (), [r_postw])
                gwk, r_gwk = cx.sb_in(ps3, "gwk", [128, 16], F32, track=False)
                s.dma("c_ld1", gwk[:], P["gwk"][:, :], (), [r_gwk])
                for k in range(16):
                    cx.ts("dve", w_out[:, k, :], w_out[:, k, :], gwk[:, k:k + 1], ALU.mult, [r_wout, r_gwk], [r_wout])
                xt2 = [cx.sb_in(ps3, "s3_x%d" % i, [128, D], F32) for i in range(3)]
                yn2 = [cx.sb_in(ps3, "s3_yn%d" % i, [128, DI], BF16) for i in range(2)]
                ot2 = [(cx.sb_in(ps3, "s3_ot%d" % i, [128, D], F32)[0], [Res("s3_ot%d_%d" % (i, k)) for k in range(2)]) for i in range(2)]
                ynT2 = [(cx.sb_in(ps3, "s3_ynT%d" % i, [128, 16, 128], BF16)[0], [Res("s3_ynT%d_%d" % (i, k)) for k in range(2)]) for i in range(2)]
                junk, r_junk = cx.sb_in(ps3, "s3_junk", [128, D], BF16)
                ss2 = cx.sb_in(ps3, "s3_ss2", [128, 2], F32)[0]
                r_ss2 = [Res("s3_ss2_%d" % k) for k in range(2)]
                ss, r_ss = cx.sb_in(ps3, "s3_ss", [128, 1], F32)
                tmp1, r_tmp1 = cx.sb_in(ps3, "s3_tmp1", [128, 1], F32)
                rstd, r_rstd = cx.sb_in(ps3, "s3_rstd", [128, 1], F32)

                def s3_load(c):
                    sl = c % 2
                    s.dma("s3_ldx%d" % (c % 3), xt2[c % 3][0][:], x_src[c * CH:(c + 1) * CH, :], [x_src_res[c]], [xt2[c % 3][1]])
                    s.dma("s3_ldy%d" % sl, yn2[sl][0][:], ynd[c], [r_yn[c]], [yn2[sl][1]])

                def s3_transposes(c):
                    ynt, r_ynt = yn2[c % 2]
                    ynT, r_ynT = ynT2[c % 2]
                    for half in range(2):
                        bi = half
                        pb = psb(bi)
                        for b8 in range(8):
                            blk = half * 8 + b8
                            cx.tr(pb[:, b8 * 128:(b8 + 1) * 128], ynt[:, blk * 128:(blk + 1) * 128], ident_b[:], [r_ynt, r_identb], [PS[bi][1]])
                        cx.cp("dve" if half == 0 else "act", ynT[:, half * 8:(half + 1) * 8, :].rearrange("p a b -> p (a b)"), pb[:, 0:1024],
                              [PS[bi][1]], [r_ynT[half]])
                s3_load(0)
                if nch > 1:
                    s3_load(1)
                s3_transposes(0)
                for c in range(nch):
                    sl = c % 2
                    xt, r_x = xt2[c % 3]
                    ot, r_ot = ot2[sl]
                    ynT, r_ynT = ynT2[sl]
                    if c + 2 < nch:
                        s3_load(c + 2)
                    banks = [2 + 2 * (c % 2), 3 + 2 * (c % 2)]
                    for nb in range(2):
                        bi = banks[nb]
                        for k in range(16):
                            cx.mm(PS[bi][0][:, :], ynT[:, k, :], w_out[:, k, nb * 512:(nb + 1) * 512], k == 0, k == 15, [r_ynT[k // 8], r_wout], [PS[bi][1]])
                        if nb == 0 and c + 1 < nch:
                            s3_transposes(c + 1)
                    post_norm_residual(c, banks, xt[:], r_x, postw[:], r_postw,
                                       (junk, r_junk, ss2, r_ss2, ss, r_ss, tmp1, r_tmp1, rstd, r_rstd, ot, r_ot),
                                       x_dst[c * CH:(c + 1) * CH, :], x_dst_res[c], "s3_st%d" % sl)


        def swa_layer(l, x_src, x_src_res, x_dst, x_dst_res):
            j = l // 2
            P = att[j]
            NT = 4 * nch
            with ExitStack() as pa:
                w_in, r_win = load_weights_bf16(pa, "aw_in", P["w_in"], ATT_IN)
                w_out, r_wout = load_weights_bf16(pa, "aw_out", P["w_out"], D)
                prew, r_prew = cx.sb_in(pa, "a_prew", [128, D], F32, track=False)
                postw, r_postw = cx.sb_in(pa, "a_postw", [128, D], F32, track=False)
                sinks, r_sinks = cx.sb_in(pa, "a_sinks", [128, AQ], F32, track=False)
                amask, r_amask = cx.sb_in(pa, "a_mask", [128, 2, 256], F32, track=False)
                invf, r_invf = cx.sb_in(pa, "a_invf", [128, 8], F32, track=False)
                posi, r_posi = cx.sb_in(pa, "a_posi", [128, nch], I32, track=False)
                posf, r_posf = cx.sb_in(pa, "a_posf", [128, nch], F32, track=False)
                cosT, r_cos = cx.sb_in(pa, "a_cos", [128, nch, 8], F32, track=False)
                sinT, r_sin = cx.sb_in(pa, "a_sin", [128, nch, 8], F32, track=False)
                s.dma("c_ld0", prew[:], prew_in[l], (), [r_prew])
                s.dma("c_ld1", postw[:], postw_in[l], (), [r_postw])
                s.dma("c_ld3", sinks[:], P["sinks"][:, :], (), [r_sinks])
                negsinks, r_negsinks = cx.sb_in(pa, "a_negsinks", [128, AQ], F32, track=False)
                cx.ts("dve", negsinks[:], sinks[:], -1.0, ALU.mult, [r_sinks], [r_negsinks])
                s.dma("c_ld4", amask[:], mask_in[:, :, :], (), [r_amask])
                s.dma("c_ld5", invf[:], invf_in[:, :], (), [r_invf])
                s.dma("c_ld6", posi[:], pos_in[:, :], (), [r_posi])
                with ExitStack() as prt:
                    ang, r_ang = cx.sb_in(prt, "a_ang", [128, nch, 8], F32)
                    kf, r_kf = cx.sb_in(prt, "a_kf", [128, nch, 8], F32)
                    ki, r_ki = cx.sb_in(prt, "a_ki", [128, nch, 8], I32)
                    rr, r_rr = cx.sb_in(prt, "a_rr", [128, nch, 8], F32)
                    fx, r_fx = cx.sb_in(prt, "a_fx", [128, nch, 8], F32)
                    cx.cp("dve", posf[:], posi[:], [r_posi], [r_posf])
                    cx.tt("dve", ang[:], bcast_last(posf[:], 8), bcast_mid(invf[:], nch), ALU.mult, [r_posf, r_invf], [r_ang])

                    def reduced_sin(dst, r_dst, shift):
                        cx.ts("dve", kf[:], ang[:], shift, ALU.add, [r_ang], [r_kf], s2=1.0 / (2.0 * np.pi), op1=ALU.mult)
                        cx.cp("dve", ki[:], kf[:], [r_kf], [r_ki])
                        cx.cp("dve", kf[:], ki[:], [r_ki], [r_kf])
                        cx.stt(rr[:], kf[:], -TWO_PI_HI, ang[:], ALU.mult, ALU.add, [r_kf, r_ang], [r_rr])
                        cx.stt(rr[:], kf[:], -TWO_PI_LO, rr[:], ALU.mult, ALU.add, [r_kf, r_rr], [r_rr])
                        cx.ts("dve", rr[:], rr[:], shift, ALU.add, [r_rr], [r_rr])
                        cx.ts("dve", fx[:], rr[:], float(np.pi), ALU.is_gt, [r_rr], [r_fx], s2=-2.0 * np.pi, op1=ALU.mult)
                        cx.tt("dve", rr[:], rr[:], fx[:], ALU.add, [r_rr, r_fx], [r_rr])
                        cx.ts("dve", fx[:], rr[:], -float(np.pi), ALU.is_lt, [r_rr], [r_fx], s2=2.0 * np.pi, op1=ALU.mult)
                        cx.tt("dve", rr[:], rr[:], fx[:], ALU.add, [r_rr, r_fx], [r_rr])
                        cx.ts("dve", rr[:], rr[:], 3.1415925, ALU.min, [r_rr], [r_rr], s2=-3.1415925, op1=ALU.max)
                        cx.act(dst[:], rr[:], AF.Sin, [r_rr], [r_dst])
                    reduced_sin(sinT, r_sin, 0.0)
                    reduced_sin(cosT, r_cos, float(np.pi / 2.0))
                    s.barrier()

                xt2 = [cx.sb_in(pa, "a_x%d" % i, [128, D], F32) for i in range(2)]
                xr2 = [cx.sb_in(pa, "a_xr%d" % i, [128, D], F32) for i in range(2)]
                ot2 = [(cx.sb_in(pa, "a_ot%d" % i, [128, D], F32)[0], [Res("a_ot%d_%d" % (i, k)) for k in range(2)]) for i in range(2)]
                junk, r_junk = cx.sb_in(pa, "a_junk", [128, D], BF16)
                ss, r_ss = cx.sb_in(pa, "a_ss", [128, 1], F32)
                tmp1, r_tmp1 = cx.sb_in(pa, "a_tmp1", [128, 1], F32)
                rstd, r_rstd = cx.sb_in(pa, "a_rstd", [128, 1], F32)
                junk_b, r_junk_b = cx.sb_in(pa, "a_junk_b", [128, 512], BF16)
                ss2 = cx.sb_in(pa, "a_ss2", [128, 2], F32)[0]
                r_ss2 = [Res("a_ss2_%d" % k) for k in range(2)]
                ss_b, r_ss_b = cx.sb_in(pa, "a_ss_b", [128, 1], F32)
                tmp1_b, r_tmp1_b = cx.sb_in(pa, "a_tmp1_b", [128, 1], F32)
                rstd_b, r_rstd_b = cx.sb_in(pa, "a_rstd_b", [128, 1], F32)
                hb, r_hb = cx.sb_in(pa, "a_hb", [128, D], BF16)
                hT, r_hT = cx.sb_in(pa, "a_hT", [128, 8, 128], BF16)
                qk, r_qk = cx.sb_in(pa, "a_qk", [128, 1280], F32)
                qkb, r_qkb = cx.sb_in(pa, "a_qkb", [128, 1280], BF16)
                ra, r_ra = cx.sb_in(pa, "a_ra", [128, 20, 8], F32)
                rb, r_rb = cx.sb_in(pa, "a_rb", [128, 20, 8], F32)
                sg3 = [cx.sb_in(pa, "a_sg%d" % i, [128, D], BF16) for i in range(4)]
                qT2 = [(cx.sb_in(pa, "a_qT%d" % i, [64, AQ, 128], BF16)[0], [Res("a_qT%d_%d" % (i, k)) for k in range(2)]) for i in range(2)]
                kT4 = [cx.sb_in(pa, "a_kT%d" % i, [64, AKV, 128], BF16) for i in range(4)]
                v4 = [cx.sb_in(pa, "a_v%d" % i, [128, 256], BF16) for i in range(5)]
                sm2 = [cx.sb_in(pa, "a_sm%d" % i, [128, 4, 256], F32) for i in range(3)]
                rmax2 = [cx.sb_in(pa, "a_rmax%d" % i, [128, 4], F32) for i in range(3)]
                negm2 = [cx.sb_in(pa, "a_negm%d" % i, [128, AQ], F32) for i in range(3)]
                negmq = [[Res("a_negmq%d_%d" % (i, k)) for k in range(4)] for i in range(3)]
                rsum2 = [cx.sb_in(pa, "a_rsum%d" % i, [128, AQ], F32) for i in range(3)]
                esk2 = [cx.sb_in(pa, "a_esk%d" % i, [128, AQ], F32) for i in range(3)]
                rden, r_rden = cx.sb_in(pa, "a_rden", [128, AQ], F32)
                pt2 = [cx.sb_in(pa, "a_pt%d" % i, [128, 4, 256], BF16) for i in range(3)]
                ptg = [[Res("a_ptg%d_%d" % (i, g)) for g in range(4)] for i in range(3)]
                rsh = [[Res("a_rsh%d_%d" % (i, h)) for h in range(AQ)] for i in range(3)]
                pT2 = [cx.sb_in(pa, "a_pT%d" % i, [128, 4, 2, 128], BF16) for i in range(3)]
                og, r_og = cx.sb_in(pa, "a_og", [128, D], BF16)
                otmp, r_otmp = cx.sb_in(pa, "a_otmp", [128, D], F32)
                ogT, r_ogT = cx.sb_in(pa, "a_ogT", [128, 8, 128], BF16)
                ntiles = (junk, r_junk, ss, r_ss, tmp1, r_tmp1, rstd, r_rstd, hb, r_hb)
                SB = [2, 3]
                PB = [4, 5]
                OB = [6, 7]

                def a_load(c):
                    s.dma("a_ldx%d" % (c % 2), xt2[c % 2][0][:], x_src[c * CH:(c + 1) * CH, :], [x_src_res[c]], [xt2[c % 2][1]])

                def F1(c):
                    xt, r_x = xt2[c % 2]
                    norm_chain(ntiles, xt[:], r_x, prew[:], r_prew)

                def F2(c):
                    transpose_h(hb, r_hb, hT, r_hT, 0)

                def F3(c):
                    vv, r_v = v4[c % 5]
                    for nb in range(3):
                        bi = nb % 2
                        for k in range(8):
                            cx.mm(PS[bi][0][:, :], hT[:, k, :], w_in[:, k, nb * 512:(nb + 1) * 512], k == 0, k == 7, [r_hT, r_win], [PS[bi][1]])
                        if nb < 2:
                            cx.cp("dve", qk[:, nb * 512:(nb + 1) * 512], PS[bi][0][:, :], [PS[bi][1]], [r_qk])
                        else:
                            cx.cp("dve", qk[:, 1024:1280], PS[bi][0][:, 0:256], [PS[bi][1]], [r_qk])
                            cx.cp("act", vv[:], PS[bi][0][:, 256:512], [PS[bi][1], r_qk], [r_v])

                def F4(c):
                    sg, r_sg = sg3[c % 4]
                    for nb in range(3, 5):
                        bi = nb % 2
                        for k in range(8):
                            cx.mm(PS[bi][0][:, :], hT[:, k, :], w_in[:, k, nb * 512:(nb + 1) * 512], k == 0, k == 7, [r_hT, r_win], [PS[bi][1]])
                        cx.act(sg[:, (nb - 3) * 512:(nb - 2) * 512], PS[bi][0][:, :], AF.Silu, [PS[bi][1]], [r_sg])

                def F45(c):
                    q3 = qk[:].rearrange("p (h d) -> p h d", d=AD)
                    qb3 = qkb[:].rearrange("p (h d) -> p h d", d=AD)
                    cosb = bcast_mid(cosT[:, c, :], 20)
                    sinb = bcast_mid(sinT[:, c, :], 20)
                    cx.cp("pool", qkb[:], qk[:], [r_qk], [r_qkb])
                    cx.tt("dve", ra[:], q3[:, :, 0:8], cosb, ALU.mult, [r_qk, r_cos], [r_ra])
                    cx.tt("dve", rb[:], q3[:, :, 8:16], sinb, ALU.mult, [r_qk, r_sin], [r_rb])
                    cx.tt("dve", qb3[:, :, 0:8], ra[:], rb[:], ALU.subtract, [r_ra, r_rb, r_qkb], [r_qkb])
                    cx.tt("dve", ra[:], q3[:, :, 8:16], cosb, ALU.mult, [r_qk, r_cos, r_qkb], [r_ra])
                    cx.tt("dve", rb[:], q3[:, :, 0:8], sinb, ALU.mult, [r_qk, r_sin, r_qkb], [r_rb])
                    cx.tt("dve", qb3[:, :, 8:16], ra[:], rb[:], ALU.add, [r_ra, r_rb, r_qkb], [r_qkb])

                def F6(c):
                    qT, r_qT = qT2[c % 2]
                    kT, r_kT = kT4[c % 4]
                    for half in range(2):
                        bi = half
                        pb = psb(bi)
                        for h8 in range(8):
                            h = half * 8 + h8
                            cx.tr(pb[0:64, h8 * 128:(h8 + 1) * 128], qkb[:, h * AD:(h + 1) * AD], ident_b[:], [r_qkb, r_identb], [PS[bi][1]])
                        cx.cp("act" if half else "dve", qT[:, half * 8:(half + 1) * 8, :].rearrange("p a b -> p (a b)"), pb[0:64, 0:1024],
                              [PS[bi][1]], [r_qT[half]])
                    pb = psb(0)
                    for h4 in range(4):
                        cx.tr(pb[0:64, h4 * 128:(h4 + 1) * 128], qkb[:, 1024 + h4 * AD:1024 + (h4 + 1) * AD], ident_b[:], [r_qkb, r_identb], [PS[0][1]])
                    cx.cp("dve", kT[:].rearrange("p a b -> p (a b)"), pb[0:64, 0:512], [PS[0][1]], [r_kT])

                def St(t):
                    c, kh = divmod(t, 4)
                    qT, r_qT = qT2[c % 2]
                    kT, r_kT = kT4[c % 4]
                    kTp, r_kTp = kT4[(c - 1) % 4]
                    for g in range(4):
                        h = kh * 4 + g
                        bi = SB[g // 2]
                        o = PS[bi][0][:, (g % 2) * 256:(g % 2 + 1) * 256]
                        if c > 0:
                            cx.mm(o[:, 0:128], qT[:, h, :], kTp[:, kh, :], True, True, [r_qT[h // 8], r_kTp], [PS[bi][1]])
                        cx.mm(o[:, 128:256], qT[:, h, :], kT[:, kh, :], True, True, [r_qT[h // 8], r_kT], [PS[bi][1]])

                def Mt_(t):
                    c, kh = divmod(t, 4)
                    sm, r_sm = sm2[t % 3]
                    rmax, r_rmax = rmax2[t % 3]
                    negm, r_negm = negm2[c % 3]
                    mk = amask[:, 0 if c == 0 else 1, :]
                    for i2 in range(2):
                        bi = SB[i2]
                        if c > 0:
                            cx.stt(sm[:, i2 * 2:(i2 + 1) * 2, :], PS[bi][0][:, :].rearrange("p (a b) -> p a b", b=256), 0.125,
                                   bcast_mid(mk, 2), ALU.mult, ALU.add, [PS[bi][1], r_amask], [r_sm])
                        else:
                            cx.memset("dve", sm[:, i2 * 2:(i2 + 1) * 2, 0:128], NEG, [r_sm])
                            cx.stt(sm[:, i2 * 2:(i2 + 1) * 2, 128:256], PS[bi][0][:, :].rearrange("p (a b) -> p a b", b=256)[:, :, 128:256], 0.125,
                                   bcast_mid(mk[:, 128:256], 2), ALU.mult, ALU.add, [PS[bi][1], r_amask], [r_sm])
                    cx.s.op("dve", lambda e, o_=rmax[:, 0:4], i_=sm[:]: e.tensor_reduce(out=o_, in_=i_, axis=mybir.AxisListType.X, op=ALU.max),
                            [r_sm], [r_rmax])
                    cx.stt(negm[:, kh * 4:(kh + 1) * 4], rmax[:], -1.0, negsinks[:, kh * 4:(kh + 1) * 4], ALU.mult, ALU.min,
                           [r_rmax, r_negsinks], [negmq[c % 3][kh]])

                def Et_(t):
                    c, kh = divmod(t, 4)
                    sm, r_sm = sm2[t % 3]
                    negm, r_negm = negm2[c % 3]
                    pt, r_pt = pt2[t % 3]
                    rsum, r_rsum = rsum2[c % 3]
                    for g in range(4):
                        h = kh * 4 + g
                        cx.act(pt[:, g, :], sm[:, g, :], AF.Exp, [r_sm, negmq[c % 3][kh]], [ptg[t % 3][g], rsh[c % 3][h]], bias=negm[:, h:h + 1], accum_out=rsum[:, h:h + 1])

                def Tt(t):
                    pt, r_pt = pt2[t % 3]
                    bi = PB[t % 2]
                    pbk = psb(bi)
                    for g in range(4):
                        for hf in range(2):
                            cx.tr(pbk[:, (g * 2 + hf) * 128:(g * 2 + hf + 1) * 128], pt[:, g, hf * 128:(hf + 1) * 128], ident_b[:],
                                  [ptg[t % 3][g], r_identb], [PS[bi][1]])

                def Ct(t):
                    pT, r_pT = pT2[t % 3]
                    bi = PB[t % 2]
                    cx.cp("dve", pT[:].rearrange("p a b c -> p (a b c)"), psb(bi)[:, 0:1024], [PS[bi][1]], [r_pT])

                def Vt(t):
                    c, kh = divmod(t, 4)
                    pT, r_pT = pT2[t % 3]
                    vv, r_v = v4[c % 5]
                    vp, r_vp = v4[(c - 1) % 5]
                    ob = OB[kh // 2]
                    for g in range(4):
                        h = kh * 4 + g
                        o = PS[ob][0][:, (h % 8) * 64:(h % 8 + 1) * 64]
                        if c > 0:
                            cx.mm(o, pT[:, g, 0, :], vp[:, kh * AD:(kh + 1) * AD], True, False, [r_pT, r_vp], [PS[ob][1]])
                            cx.mm(o, pT[:, g, 1, :], vv[:, kh * AD:(kh + 1) * AD], False, True, [r_pT, r_v], [PS[ob][1]])
                        else:
                            cx.mm(o, pT[:, g, 1, :], vv[:, kh * AD:(kh + 1) * AD], True, True, [r_pT, r_v], [PS[ob][1]])

                def BN(c):
                    sg, r_sg = sg3[c % 4]
                    rsum, r_rsum = rsum2[c % 3]
                    esk, r_esk = esk2[c % 3]
                    negm, r_negm = negm2[c % 3]
                    cx.tt("dve", esk[:], sinks[:], negm[:], ALU.add, [r_sinks] + negmq[c % 3], [r_esk])
                    s.dma("a_ldr%d" % (c % 2), xr2[c % 2][0][:], x_src[c * CH:(c + 1) * CH, :], [x_src_res[c]], [xr2[c % 2][1]])
                    cx.act(esk[:], esk[:], AF.Exp, [r_esk], [r_esk])
                    cx.tt("dve", rden[:], rsum[:], esk[:], ALU.add, rsh[c % 3] + [r_esk], [r_rden])
                    cx.s.op("dve", lambda e, o_=rden[:], i_=rden[:]: e.reciprocal(out=o_, in_=i_), [r_rden], [r_rden])
                    for half in range(2):
                        bi = OB[half]
                        cx.tt("dve", otmp[:, half * 512:(half + 1) * 512].rearrange("p (h d) -> p h d", d=AD),
                              PS[bi][0][:, :].rearrange("p (h d) -> p h d", d=AD), bcast_last(rden[:, half * 8:(half + 1) * 8], AD), ALU.mult,
                              [PS[bi][1], r_rden], [r_otmp])
                    cx.tt("pool", og[:], otmp[:], sg[:], ALU.mult, [r_otmp, r_sg], [r_og])

                def BT(c):
                    pb = psb(1)
                    for k in range(8):
                        cx.tr(pb[:, k * 128:(k + 1) * 128], og[:, k * 128:(k + 1) * 128], ident_b[:], [r_og, r_identb], [PS[1][1]])
                    cx.cp("act", ogT[:].rearrange("p a b -> p (a b)"), pb[:, 0:1024], [PS[1][1]], [r_ogT])

                def BO(c):
                    for nb in range(2):
                        bi = nb
                        for k in range(8):
                            cx.mm(PS[bi][0][:, :], ogT[:, k, :], w_out[:, k, nb * 512:(nb + 1) * 512], k == 0, k == 7, [r_ogT, r_wout], [PS[bi][1]])
                    ot, r_ot = ot2[c % 2]
                    post_norm_a([0, 1], (junk_b, r_junk_b, ss2, r_ss2, ss_b, r_ss_b, tmp1_b, r_tmp1_b, rstd_b, r_rstd_b, ot, r_ot))

                def BP(c):
                    xr, r_xr = xr2[c % 2]
                    ot, r_ot = ot2[c % 2]
                    post_norm_b([0, 1], xr[:], r_xr, postw[:], r_postw,
                                (junk_b, r_junk_b, ss2, r_ss2, ss_b, r_ss_b, tmp1_b, r_tmp1_b, rstd_b, r_rstd_b, ot, r_ot),
                                x_dst[c * CH:(c + 1) * CH, :], x_dst_res[c], "a_st%d" % (c % 2))

                def okc(c):
                    return 0 <= c < nch

                def okt(t):
                    return 0 <= t < NT

                a_load(0)
                if nch > 1:
                    a_load(1)
                for u in range(-5, 4 * (nch - 1) + 17):
                    if (u - 15) % 4 == 0 and okc((u - 15) // 4):
                        BP((u - 15) // 4)
                    if (u - 12) % 4 == 0 and okc((u - 12) // 4):
                        BN((u - 12) // 4)
                    if (u + 5) % 4 == 0 and okc((u + 5) // 4):
                        F1((u + 5) // 4)
                    if okt(u - 1):
                        Mt_(u - 1)
                    if okt(u - 3):
                        Et_(u - 3)
                    if okt(u - 6):
                        Ct(u - 6)
                    if okt(u - 5):
                        Tt(u - 5)
                    if okt(u - 8):
                        Vt(u - 8)
                    cf, ph = divmod(u + 4, 4)
                    if okc(cf):
                        if ph == 0:
                            F2(cf)
                        elif ph == 1:
                            F3(cf)
                            F4(cf)
                            if cf + 2 < nch:
                                a_load(cf + 2)
                        elif ph == 2:
                            F45(cf)
                        else:
                            F6(cf)
                    if (u - 13) % 4 == 0 and okc((u - 13) // 4):
                        BT((u - 13) // 4)
                    if (u - 14) % 4 == 0 and okc((u - 14) // 4):
                        BO((u - 14) // 4)
                    if okt(u):
                        St(u)


        src = x_in
        src_res = [Res("xin_%d" % c, track=True) for c in range(nch)]
        for li, l in enumerate(layers):
            last = li == len(layers) - 1
            if last:
                dst = out_dram
                dst_res = [Res("xout_%d" % c) for c in range(nch)]
            else:
                dst = xs_scr[li % 2]
                dst_res = dres("xscr%d_%d" % (li % 2, li))
            if l % 2 == 0:
                ssd_layer(l, src, src_res, dst, dst_res)
            else:
                swa_layer(l, src, src_res, dst, dst_res)
            s.barrier()
            src, src_res = dst, dst_res

        block = st.enter_context(nc.Block())
        s.finish(block)
    return nc


def _rep(v, n=128):
    return np.ascontiguousarray(np.broadcast_to(np.asarray(v, np.float32)[None, :], (n, v.shape[0])))


def host_consts(nch):
    idx = np.arange(128)
    tri = (idx[:, None] <= idx[None, :]).astype(np.float32)
    strict = (idx[:, None] > idx[None, :]).astype(np.float32)
    qi = idx[:, None]
    kj = np.arange(256)[None, :]
    dist = qi + 128 - kj
    valid = (dist >= 0) & (dist < 128)
    m1 = np.where(valid, 0.0, NEG).astype(np.float32)
    m0 = np.where(valid & (kj >= 128), 0.0, NEG).astype(np.float32)
    amask = np.ascontiguousarray(np.stack([m0, m1], axis=1))
    invf = (500000.0 ** (-np.arange(0, 16, 2, dtype=np.float32) / 16.0)).astype(np.float32)
    sel4 = np.zeros((4, 512), np.float32)
    for r in range(4):
        sel4[r, r * 128:(r + 1) * 128] = 1.0
    return dict(ident=np.eye(128, dtype=np.float32), tri=tri, strict=strict, amask=amask, invf=_rep(invf), sel4=sel4)


def make_in_map(b, layers, nch, inputs):
    T = nch * CH
    m = dict(host_consts(nch))
    m["x"] = np.ascontiguousarray(inputs["x"][b, :T])
    pos = np.asarray(inputs["positions"][b, :T]).astype(np.int32)
    m["pos"] = np.ascontiguousarray(pos.reshape(nch, 128).T)
    m["prew"] = np.ascontiguousarray(np.broadcast_to(np.asarray(inputs["pre_norm"], np.float32)[:, None, :], (DEPTH, 128, D)))
    m["postw"] = np.ascontiguousarray(np.broadcast_to(np.asarray(inputs["post_norm"], np.float32)[:, None, :], (DEPTH, 128, D)))
    for l in layers:
        j = l // 2
        if l % 2 == 0:
            m["s%d_w_in" % j] = np.ascontiguousarray(inputs["ssm_w_in"][j])
            cw = np.asarray(inputs["ssm_conv_w"][j], np.float32)
            m["s%d_conv_w" % j] = np.ascontiguousarray(cw.reshape(4, 32, 128).transpose(2, 0, 1))
            cbv = np.asarray(inputs["ssm_conv_b"][j], np.float32)
            m["s%d_conv_b" % j] = np.ascontiguousarray(cbv.reshape(8, 4, 128).transpose(1, 0, 2).reshape(4, 1024))
            m["s%d_dtb" % j] = _rep(inputs["ssm_dt_bias"][j])
            m["s%d_alog" % j] = _rep(inputs["ssm_a_log"][j])
            m["s%d_d" % j] = _rep(inputs["ssm_d"][j])
            m["s%d_gwk" % j] = np.ascontiguousarray(np.asarray(inputs["ssm_gate_norm"][j], np.float32).reshape(16, 128).T)
            m["s%d_w_out" % j] = np.ascontiguousarray(inputs["ssm_w_out"][j])
        else:
            m["a%d_w_in" % j] = np.ascontiguousarray(inputs["att_w_in"][j])
            m["a%d_sinks" % j] = _rep(inputs["att_sinks"][j])
            m["a%d_w_out" % j] = np.ascontiguousarray(inputs["att_w_out"][j])
    return m


_NC_CACHE = {}


def kernel(**inputs):
    inputs = {k: np.asarray(v) for k, v in inputs.items()}
    layers = (0, 1, 2, 3)
    key = (layers, NCHUNK)
    if key not in _NC_CACHE:
        _NC_CACHE[key] = build_program(list(layers), NCHUNK)
    nc = _NC_CACHE[key]
    in_maps = [make_in_map(b, layers, NCHUNK, inputs) for b in range(BATCH)]
    res = run_bass_kernel_spmd(nc, in_maps, core_ids=list(range(BATCH)))
    out = np.stack([np.asarray(r["out"]).reshape(SEQ, D) for r in res.results], axis=0)
    return out.astype(np.float32)
```

```python
import numpy as np
import concourse.bass as bass
import concourse.mybir as mybir
from concourse.bass_utils import run_bass_kernel_spmd
from contextlib import ExitStack

F32 = mybir.dt.float32
BF16 = mybir.dt.bfloat16
I32 = mybir.dt.int32
AF = mybir.ActivationFunctionType
ALU = mybir.AluOpType

D = 1024
SEQ = 8192
BATCH = 4
DEPTH = 4
CH = 128
NCHUNK = SEQ // CH
EPS = 1e-6
DI = 2048
NH = 32
HP = 64
NG = 8
NST = 128
XBC = 4096
SSM_IN = 6176
AQ = 16
AKV = 4
AD = 64
ATT_IN = 2560
TWO_PI_HI = 6.28125
TWO_PI_LO = 2.0 * np.pi - 6.28125
NEG = -30000.0


class Res:
    __slots__ = ("name", "w", "r", "track", "excl")

    def __init__(self, name, track=True, excl=False):
        self.name = name
        self.w = None
        self.r = {}
        self.track = track
        self.excl = excl


class Sched:
    EPOCH = 30000
    LIMIT = None
    SAME_ENGINE_SYNC = ("act", "dve", "pool")

    def __init__(self, nc, st):
        self.nops = 0
        self.nc = nc
        self.st = st
        self.engs = ["pe", "act", "dve", "pool", "sp"]
        self.prog = {k: [] for k in self.engs}
        self.semobj = {}
        self.owner = {}
        self.cur = {}
        self.cnt = {}
        self.waited = {k: {} for k in self.engs}
        self.nsem = 0
        for k in ["pe", "act", "dve", "pool"]:
            self._new_epoch(k)
        self.dma_cnt = {}

    def _mksem(self, key, owner):
        s = self.st.enter_context(self.nc.semaphore("s_%s" % key))
        self.semobj[key] = s
        self.owner[key] = owner
        self.nsem += 1
        return s

    def _new_epoch(self, eng):
        key = "%s%d" % (eng, self.nsem)
        self._mksem(key, eng)
        self.cur[eng] = key
        self.cnt[eng] = 0

    def _deps(self, reads, writes):
        deps = {}

        def add(tok):
            if tok is None:
                return
            k, v = tok
            if deps.get(k, 0) < v:
                deps[k] = v
        for r in reads:
            add(r.w)
        for w in writes:
            add(w.w)
            for k, v in w.r.items():
                add((k, v))
        return deps

    def _waits(self, eng, deps):
        prog = self.prog[eng]
        for k, v in deps.items():
            if self.owner[k] == eng and (eng == "pe" or eng not in Sched.SAME_ENGINE_SYNC):
                continue
            if self.waited[eng].get(k, 0) >= v:
                continue
            self.waited[eng][k] = v
            sem = self.semobj[k]
            prog.append(lambda e, sem=sem, v=v: e.wait_ge(sem, v))

    def _commit(self, tok, reads, writes):
        k, v = tok
        for r in reads:
            if r.track:
                if r.r.get(k, 0) < v:
                    r.r[k] = v
        for w in writes:
            w.w = tok
            w.r = {}

    def op(self, eng, fn, reads=(), writes=()):
        self.nops += 1
        if Sched.LIMIT is not None and self.nops > Sched.LIMIT:
            return None
        ex = [r for r in reads if r.excl]
        if ex:
            writes = list(writes) + [r for r in ex if r not in writes]
            reads = [r for r in reads if not r.excl]
        self._waits(eng, self._deps(reads, writes))
        if self.cnt[eng] >= self.EPOCH:
            self._new_epoch(eng)
        key = self.cur[eng]
        self.cnt[eng] += 1
        val = self.cnt[eng]
        sem = self.semobj[key]
        self.prog[eng].append(lambda e, fn=fn, sem=sem: fn(e).then_inc(sem, 1))
        tok = (key, val)
        self._commit(tok, reads, writes)
        return tok

    def dma(self, semname, out, in_, reads=(), writes=(), q="sp"):
        self.nops += 1
        if Sched.LIMIT is not None and self.nops > Sched.LIMIT:
            return None
        if semname not in self.semobj:
            self._mksem(semname, "dma")
            self.dma_cnt[semname] = 0
        deps = self._deps(reads, writes)
        if self.dma_cnt[semname] > 0:
            if deps.get(semname, 0) < self.dma_cnt[semname]:
                deps[semname] = self.dma_cnt[semname]
        self._waits(q, deps)
        self.dma_cnt[semname] += 16
        val = self.dma_cnt[semname]
        sem = self.semobj[semname]
        self.prog[q].append(lambda e, out=out, in_=in_, sem=sem: e.dma_start(out=out, in_=in_).then_inc(sem, 16))
        tok = (semname, val)
        self._commit(tok, reads, writes)
        return tok

    def dma_group(self, semname, items, reads=(), writes=(), q="sp"):
        if semname not in self.semobj:
            self._mksem(semname, "dma")
            self.dma_cnt[semname] = 0
        deps = self._deps(reads, writes)
        if self.dma_cnt[semname] > 0 and deps.get(semname, 0) < self.dma_cnt[semname]:
            deps[semname] = self.dma_cnt[semname]
        self._waits(q, deps)
        sem = self.semobj[semname]
        for (out, in_) in items:
            self.dma_cnt[semname] += 16
            self.prog[q].append(lambda e, out=out, in_=in_, sem=sem: e.dma_start(out=out, in_=in_).then_inc(sem, 16))
        tok = (semname, self.dma_cnt[semname])
        self._commit(tok, reads, writes)
        return tok

    def barrier(self):
        toks = {}
        for eng in ["pe", "act", "dve", "pool"]:
            if self.cnt[eng] > 0:
                toks[self.cur[eng]] = self.cnt[eng]
        for k, v in self.dma_cnt.items():
            if v > 0:
                toks[k] = v
        for eng in self.engs:
            for k, v in toks.items():
                if self.waited[eng].get(k, 0) >= v:
                    continue
                self.waited[eng][k] = v
                sem = self.semobj[k]
                self.prog[eng].append(lambda e, sem=sem, v=v: e.wait_ge(sem, v))

    def finish(self, block):
        for k, v in self.dma_cnt.items():
            if self.waited["sp"].get(k, 0) < v:
                sem = self.semobj[k]
                self.prog["sp"].append(lambda e, sem=sem, v=v: e.wait_ge(sem, v))
        progs = self.prog

        @block.sync
        def _(e):
            for f in progs["sp"]:
                f(e)

        @block.tensor
        def _(e):
            for f in progs["pe"]:
                f(e)

        @block.scalar
        def _(e):
            for f in progs["act"]:
                f(e)

        @block.vector
        def _(e):
            for f in progs["dve"]:
                f(e)

        @block.gpsimd
        def _(e):
            for f in progs["pool"]:
                f(e)


class Ctx:
    def __init__(self, nc, st):
        self.nc = nc
        self.st = st
        self.s = Sched(nc, st)
        self.pst = None

    def sb(self, name, shape, dt, track=True):
        return self.sb_in(self.st, name, shape, dt, track)

    def sb_in(self, stack, name, shape, dt, track=True):
        self.uid = getattr(self, "uid", 0) + 1
        t = stack.enter_context(self.nc.sbuf_tensor("t%d_%s" % (self.uid, name), list(shape), dt))
        return t, Res(name, track)

    def mm(self, out, lhsT, rhs, start, stop, reads, writes):
        return self.s.op("pe", lambda e: e.matmul(out, lhsT=lhsT, rhs=rhs, start=start, stop=stop), reads, writes)

    def tr(self, out, in_, ident, reads, writes):
        return self.s.op("pe", lambda e: e.transpose(out=out, in_=in_, identity=ident), reads, writes)

    def act(self, out, in_, func, reads, writes, bias=None, scale=None, accum_out=None):
        kw = {}
        if bias is not None:
            kw["bias"] = bias
        if scale is not None:
            kw["scale"] = scale
        if accum_out is not None:
            kw["accum_out"] = accum_out
        return self.s.op("act", lambda e: e.activation(out=out, in_=in_, func=func, **kw), reads, writes)

    def tt(self, eng, out, in0, in1, op, reads, writes):
        return self.s.op(eng, lambda e: e.tensor_tensor(out=out, in0=in0, in1=in1, op=op), reads, writes)

    def ts(self, eng, out, in0, s1, op0, reads, writes, s2=None, op1=None):
        if op1 is None:
            return self.s.op(eng, lambda e: e.tensor_scalar(out=out, in0=in0, scalar1=s1, scalar2=None, op0=op0), reads, writes)
        return self.s.op(eng, lambda e: e.tensor_scalar(out=out, in0=in0, scalar1=s1, scalar2=s2, op0=op0, op1=op1), reads, writes)

    def stt(self, out, in0, scalar, in1, op0, op1, reads, writes):
        return self.s.op("dve", lambda e: e.scalar_tensor_tensor(out=out, in0=in0, scalar=scalar, in1=in1, op0=op0, op1=op1), reads, writes)

    def cp(self, eng, out, in_, reads, writes):
        if eng == "act":
            return self.s.op("act", lambda e: e.activation(out=out, in_=in_, func=AF.Copy), reads, writes)
        return self.s.op(eng, lambda e: e.tensor_copy(out=out, in_=in_), reads, writes)

    def memset(self, eng, ap, val, writes):
        return self.s.op(eng, lambda e: e.memset(ap, val), (), writes)


def bcast_mid(ap2d, n):
    p, f = ap2d.shape
    return ap2d.unsqueeze(1).broadcast_to([p, n, f])


def bcast_last(ap2d, n):
    p, f = ap2d.shape
    return ap2d.unsqueeze(2).broadcast_to([p, f, n])


def build_program(layers, nch=NCHUNK, debug=False):
    nc = bass.Bass("TRN2", target_bir_lowering=False)
    T = nch * CH
    dr = {}

    def din(name, shape, dt):
        dr[name] = nc.dram_tensor(name, list(shape), dt, kind="ExternalInput").ap()
        return dr[name]

    def dscr(name, shape, dt):
        kind = "ExternalOutput" if debug else "Internal"
        dr[name] = nc.dram_tensor(name, list(shape), dt, kind=kind).ap()
        return dr[name]

    x_in = din("x", [T, D], F32)
    pos_in = din("pos", [128, nch], I32)
    invf_in = din("invf", [128, 8], F32)
    ident_in = din("ident", [128, 128], F32)
    tri_in = din("tri", [128, 128], F32)
    strict_in = din("strict", [128, 128], F32)
    mask_in = din("amask", [128, 2, 256], F32)
    sel4_in = din("sel4", [4, 512], F32)
    prew_in = din("prew", [DEPTH, 128, D], F32)
    postw_in = din("postw", [DEPTH, 128, D], F32)
    nS = sum(1 for l in layers if l % 2 == 0)
    nA = sum(1 for l in layers if l % 2 == 1)
    ssm = {}
    att = {}
    for l in layers:
        j = l // 2
        if l % 2 == 0:
            ssm[j] = dict(
                w_in=din("s%d_w_in" % j, [D, SSM_IN], F32),
                conv_w=din("s%d_conv_w" % j, [128, 4, 32], F32),
                conv_b=din("s%d_conv_b" % j, [4, 8 * 128], F32),
                dtb=din("s%d_dtb" % j, [128, NH], F32),
                alog=din("s%d_alog" % j, [128, NH], F32),
                dsk=din("s%d_d" % j, [128, NH], F32),
                gwk=din("s%d_gwk" % j, [128, 16], F32),
                w_out=din("s%d_w_out" % j, [DI, D], F32),
            )
        else:
            att[j] = dict(
                w_in=din("a%d_w_in" % j, [D, ATT_IN], F32),
                sinks=din("a%d_sinks" % j, [128, AQ], F32),
                w_out=din("a%d_w_out" % j, [D, D], F32),
            )
    out_dram = nc.dram_tensor("out", [T, D], F32, kind="ExternalOutput").ap()
    xs_scr = [dscr("xscr%d" % i, [T, D], F32) for i in range(2)] if len(layers) > 1 else []
    if nS:
        szd = dscr("szd", [nch, 128, DI], BF16)
        dtd = dscr("dtd", [nch, 128, NH], F32)
        xbd = dscr("xbd", [nch, 128, 32 * 128], BF16)
        ynd = dscr("ynd", [nch, 128, DI], BF16)

    with ExitStack() as st:
        cx = Ctx(nc, st)
        s = cx.s
        ident_f, r_identf = cx.sb("ident_f", [128, 128], F32, track=False)
        ident_b, r_identb = cx.sb("ident_b", [128, 128], BF16, track=False)
        tri_f, r_trif = cx.sb("tri_f", [128, 128], F32, track=False)
        tri_b, r_trib = cx.sb("tri_b", [128, 128], BF16, track=False)
        strict_b, r_strict = cx.sb("strict_f", [128, 128], F32, track=False)
        ones_f, r_onesf = cx.sb("ones_f", [128, 128], F32, track=False)
        ones_b, r_onesb = cx.sb("ones_b", [128, 128], BF16, track=False)
        mhalf, r_mhalf = cx.sb("mhalf", [128, 1], F32, track=False)
        s.dma("c_ld0", ident_f[:], ident_in[:, :], (), [r_identf])
        s.dma("c_ld1", tri_f[:], tri_in[:, :], (), [r_trif])
        s.dma("c_ld7", strict_b[:], strict_in[:, :], (), [r_strict])
        cx.cp("dve", ident_b[:], ident_f[:], [r_identf], [r_identb])
        cx.cp("dve", tri_b[:], tri_f[:], [r_trif], [r_trib])
        cx.memset("dve", ones_f[:], 1.0, [r_onesf])
        cx.memset("dve", ones_b[:], 1.0, [r_onesb])
        cx.memset("dve", mhalf[:], -0.5, [r_mhalf])
        PS = []
        for i in range(8):
            t = st.enter_context(nc.psum_tensor("ps%d" % i, [128, 512], F32))
            PS.append((t, Res("ps%d" % i, excl=True)))

        def psb(i):
            return PS[i][0][:].bitcast(BF16)

        x_res_chunks = {}

        def dres(name):
            if name not in x_res_chunks:
                x_res_chunks[name] = [Res("%s_%d" % (name, c)) for c in range(nch)]
            return x_res_chunks[name]

        def rstd_from_ss(ss_ap, n, rstd_ap, tmp_ap, reads, r_tmp, r_rstd):
            cx.ts("dve", tmp_ap, ss_ap, 1.0 / n, ALU.mult, reads, [r_tmp], s2=EPS, op1=ALU.add)
            cx.tt("pool", rstd_ap, tmp_ap, mhalf[:, 0:1], ALU.pow, [r_tmp, r_mhalf], [r_rstd])

        def load_weights_bf16(stack, name, w_dram, ncols, c0=0):
            K = w_dram.shape[0]
            kc = K // 128
            wt, r_w = cx.sb_in(stack, name, [128, kc, ncols], BF16, track=False)
            src = w_dram.rearrange("(k p) n -> p k n", p=128)
            items = []
            step = 512
            for a in range(0, ncols, step):
                b = min(ncols, a + step)
                items.append((wt[:, :, a:b], src[:, :, c0 + a:c0 + b]))
            s.dma_group("wld_" + name, items, (), [r_w], q="pool")
            return wt, r_w

        def interleave(*gens):
            gens = [g for g in gens if g is not None]
            while gens:
                alive = []
                for g in gens:
                    try:
                        next(g)
                        alive.append(g)
                    except StopIteration:
                        pass
                gens = alive

        def interleave_pattern(g1, g2, pattern):
            gens = {"1": g1, "2": g2}
            for ch in pattern:
                g = gens[ch]
                if g is None:
                    continue
                try:
                    next(g)
                except StopIteration:
                    gens[ch] = None
            interleave(gens["1"], gens["2"])

        def norm_chain(tiles, xt, r_x, prew, r_prew):
            (junk, r_junk, ss, r_ss, tmp1, r_tmp1, rstd, r_rstd, hb, r_hb) = tiles
            cx.act(junk[:], xt, AF.Square, [r_x], [r_junk, r_ss], accum_out=ss[:, 0:1])
            rstd_from_ss(ss[:, 0:1], D, rstd[:, 0:1], tmp1[:, 0:1], [r_ss], r_tmp1, r_rstd)
            cx.stt(hb[:], xt, rstd[:, 0:1], prew, ALU.mult, ALU.mult, [r_x, r_rstd, r_prew], [r_hb])

        def transpose_h(hb, r_hb, hT, r_hT, bank):
            pb = psb(bank)
            for k in range(8):
                cx.tr(pb[:, k * 128:(k + 1) * 128], hb[:, k * 128:(k + 1) * 128], ident_b[:], [r_hb, r_identb], [PS[bank][1]])
            cx.cp("act", hT[:].rearrange("p a b -> p (a b)"), pb[:, 0:1024], [PS[bank][1]], [r_hT])

        def post_norm_residual(c, o_banks, xt, r_x, postw, r_postw, tiles, out_ap, r_out, semname):
            post_norm_a(o_banks, tiles)
            post_norm_b(o_banks, xt, r_x, postw, r_postw, tiles, out_ap, r_out, semname)

        def post_norm_a(o_banks, tiles):
            (junk, r_junk, ss2, r_ss2, ss, r_ss, tmp1, r_tmp1, rstd, r_rstd, ot, r_ot) = tiles
            for i, bi in enumerate(o_banks):
                cx.act(junk[:, 0:512], PS[bi][0][:, :], AF.Square, [PS[bi][1]], [r_junk, r_ss2[i]], accum_out=ss2[:, i:i + 1])
            cx.tt("dve", ss[:, 0:1], ss2[:, 0:1], ss2[:, 1:2], ALU.add, r_ss2, [r_ss])
            rstd_from_ss(ss[:, 0:1], D, rstd[:, 0:1], tmp1[:, 0:1], [r_ss], r_tmp1, r_rstd)

        def post_norm_b(o_banks, xt, r_x, postw, r_postw, tiles, out_ap, r_out, semname):
            (junk, r_junk, ss2, r_ss2, ss, r_ss, tmp1, r_tmp1, rstd, r_rstd, ot, r_ot) = tiles
            for i, bi in enumerate(o_banks):
                cx.stt(ot[:, i * 512:(i + 1) * 512], PS[bi][0][:, :], rstd[:, 0:1], postw[:, i * 512:(i + 1) * 512],
                       ALU.mult, ALU.mult, [PS[bi][1], r_rstd, r_postw], [r_ot[i]])
            cx.tt("pool", ot[:], ot[:], xt, ALU.add, r_ot + [r_x], r_ot)
            s.dma(semname, out_ap, ot[:], r_ot, [r_out])

        prefetched = {}

        def ssd_layer(l, x_src, x_src_res, x_dst, x_dst_res, nxt=None):
            j = l // 2
            P = ssm[j]
            r_sz = dres("szd")
            r_dt = dres("dtd")
            r_xb = dres("xbd")
            r_yn = dres("ynd")
            psc = ExitStack()
            diag, r_diag = cx.sb_in(psc, "diag", [128, 4, 32, 128], BF16, track=False)
            cw, r_cw = cx.sb_in(psc, "cw", [128, 4, 32], F32, track=False)
            cb_row, r_cbrow = cx.sb_in(psc, "cb_row", [4, 8 * 128], BF16, track=False)
            sel4, r_sel4 = cx.sb_in(psc, "sel4", [4, 512], BF16, track=False)
            dtb, r_dtb = cx.sb_in(psc, "dtb", [128, NH], F32, track=False)
            A_b, r_Ab = cx.sb_in(psc, "A_b", [128, NH], F32, track=False)
            D_b, r_Db = cx.sb_in(psc, "D_b", [128, NH], F32, track=False)
            s.dma("p_ld0", sel4[:], sel4_in[:, :], (), [r_sel4], q="pool")
            s.dma("c_ld0", cw[:], P["conv_w"][:, :, :], (), [r_cw])
            s.dma("p_ld1", cb_row[:], P["conv_b"][:, :], (), [r_cbrow], q="pool")
            s.dma("c_ld1", dtb[:], P["dtb"][:, :], (), [r_dtb])
            s.dma("c_ld3", A_b[:], P["alog"][:, :], (), [r_Ab])
            s.dma("c_ld4", D_b[:], P["dsk"][:, :], (), [r_Db])
            cx.act(A_b[:], A_b[:], AF.Exp, [r_Ab], [r_Ab])
            cx.ts("dve", A_b[:], A_b[:], -1.0, ALU.mult, [r_Ab], [r_Ab])
            for k in range(4):
                for blk in range(32):
                    cx.ts("dve", diag[:, k, blk, :], ident_f[:], cw[:, k, blk:blk + 1], ALU.mult, [r_identf, r_cw], [r_diag])
            with ExitStack() as ps1:
                w_in, r_win = load_weights_bf16(ps1, "w_in", P["w_in"], SSM_IN)
                prew, r_prew = cx.sb_in(ps1, "prew", [128, D], F32, track=False)
                s.dma("c_ld0", prew[:], prew_in[l], (), [r_prew])
                xt2 = [cx.sb_in(ps1, "s1_x%d" % i, [128, D], F32) for i in range(2)]
                junk, r_junk = cx.sb_in(ps1, "s1_junk", [128, D], BF16)
                ss, r_ss = cx.sb_in(ps1, "s1_ss", [128, 1], F32)
                tmp1, r_tmp1 = cx.sb_in(ps1, "s1_tmp1", [128, 1], F32)
                rstd, r_rstd = cx.sb_in(ps1, "s1_rstd", [128, 1], F32)
                hb, r_hb = cx.sb_in(ps1, "s1_hb", [128, D], BF16)
                hT2 = [cx.sb_in(ps1, "s1_hT%d" % i, [128, 8, 128], BF16) for i in range(2)]
                sz2 = [cx.sb_in(ps1, "s1_sz%d" % i, [128, DI], BF16) for i in range(2)]
                dt2 = [cx.sb_in(ps1, "s1_dt%d" % i, [128, NH], F32) for i in range(2)]
                xb2 = [cx.sb_in(ps1, "s1_xb%d" % i, [128, 32, 128], BF16) for i in range(2)]
                xbq = [[Res("s1_xbq%d_%d" % (i, q)) for q in range(8)] for i in range(2)]
                szq = [[Res("s1_szq%d_%d" % (i, q)) for q in range(4)] for i in range(2)]
                ntiles = (junk, r_junk, ss, r_ss, tmp1, r_tmp1, rstd, r_rstd, hb, r_hb)

                def s1_load(c):
                    s.dma("s1_ldx%d" % (c % 2), xt2[c % 2][0][:], x_src[c * CH:(c + 1) * CH, :], [x_src_res[c]], [xt2[c % 2][1]])
                s1_load(0)
                if nch > 1:
                    s1_load(1)
                norm_chain(ntiles, xt2[0][0][:], xt2[0][1], prew[:], r_prew)
                transpose_h(hb, r_hb, hT2[0][0], hT2[0][1], 0)
                for c in range(nch):
                    sl = c % 2
                    hT, r_hT = hT2[sl]
                    if c + 2 < nch:
                        s1_load(c + 2)
                    if c + 1 < nch:
                        norm_chain(ntiles, xt2[1 - sl][0][:], xt2[1 - sl][1], prew[:], r_prew)
                    szt, r_szt = sz2[sl]
                    for nb in range(4):
                        bi = 1 + (nb % 2)
                        for k in range(8):
                            cx.mm(PS[bi][0][:, :], hT[:, k, :], w_in[:, k, nb * 512:(nb + 1) * 512], k == 0, k == 7,
                                  [r_hT, r_win], [PS[bi][1]])
                        cx.act(szt[:, nb * 512:(nb + 1) * 512], PS[bi][0][:, :], AF.Silu, [PS[bi][1]], [szq[sl][nb]])
                    s.dma("s1_stz%d" % sl, szd[c], szt[:], szq[sl], [r_sz[c]])
                    dtt, r_dtt = dt2[sl]
                    for k in range(8):
                        cx.mm(PS[3][0][:, 0:NH], hT[:, k, :], w_in[:, k, 6144:6176], k == 0, k == 7, [r_hT, r_win], [PS[3][1]])
                    cx.cp("dve", dtt[:], PS[3][0][:, 0:NH], [PS[3][1]], [r_dtt])
                    s.dma("s1_stdt%d" % sl, dtd[c], dtt[:], [r_dtt], [r_dt[c]])
                    xbt, r_xbt = xb2[sl]
                    for q in range(8):
                        bi = 4 + (q % 4)
                        for b4 in range(4):
                            blk = q * 4 + b4
                            for k in range(8):
                                cx.mm(PS[bi][0][:, b4 * 128:(b4 + 1) * 128], w_in[:, k, 2048 + blk * 128:2048 + (blk + 1) * 128],
                                      hT[:, k, :], k == 0, k == 7, [r_hT, r_win], [PS[bi][1]])
                        eng = "dve" if q % 2 == 0 else "act"
                        cx.cp(eng, xbt[:, q * 4:(q + 1) * 4, :].rearrange("p a b -> p (a b)"), PS[bi][0][:, :], [PS[bi][1]], [xbq[sl][q]])
                        if q == 3 and c + 1 < nch:
                            transpose_h(hb, r_hb, hT2[1 - sl][0], hT2[1 - sl][1], 0)
                    s.dma("s1_stx%d" % sl, xbd[c], xbt[:].rearrange("p a b -> p (a b)"), xbq[sl], [r_xb[c]])

            s.barrier()
            with ExitStack() as ps2:
                xb2 = [cx.sb_in(ps2, "s2_xb%d" % i, [128, 32, 131], BF16) for i in range(2)]
                sz3 = [cx.sb_in(ps2, "s2_sz%d" % i, [128, DI], BF16) for i in range(3)]
                GB = 8
                dt3 = [cx.sb_in(ps2, "s2_dt%d" % i, [128, GB, NH], F32) for i in range(2)]
                yn2 = [cx.sb_in(ps2, "s2_yn%d" % i, [128, DI], BF16) for i in range(2)]
                t0, r_t0 = cx.sb_in(ps2, "s2_t0", [128, GB, NH], F32)
                t1, r_t1 = cx.sb_in(ps2, "s2_t1", [128, GB, NH], F32)
                t2, r_t2 = cx.sb_in(ps2, "s2_t2", [128, GB, NH], F32)
                av2 = [cx.sb_in(ps2, "s2_av%d" % i, [128, GB, NH], F32) for i in range(2)]
                ahi, r_ahi = cx.sb_in(ps2, "s2_ahi", [128, NH], BF16)
                ahf, r_ahf = cx.sb_in(ps2, "s2_ahf", [128, NH], F32)
                alo, r_alo = cx.sb_in(ps2, "s2_alo", [128, NH], BF16)
                acs, r_acs = cx.sb_in(ps2, "s2_acs", [128, GB, NH], F32)
                dtv2 = [cx.sb_in(ps2, "s2_dtv%d" % i, [128, GB, NH], F32) for i in range(2)]
                eacs2 = [cx.sb_in(ps2, "s2_eacs%d" % i, [128, GB, NH], F32) for i in range(2)]
                dte2 = [cx.sb_in(ps2, "s2_dte%d" % i, [128, GB, NH], F32) for i in range(2)]
                cdv2 = [cx.sb_in(ps2, "s2_cdv%d" % i, [128, GB, NH], F32) for i in range(2)]
                xc, r_xc = cx.sb_in(ps2, "s2_xc", [128, 24, 128], BF16)
                cT2 = [cx.sb_in(ps2, "s2_cT%d" % i, [128, 8, 128], BF16) for i in range(2)]
                xs_tok, r_xst = cx.sb_in(ps2, "s2_xs_tok", [128, DI], BF16)
                skb2 = [cx.sb_in(ps2, "s2_skb%d" % i, [128, DI], BF16) for i in range(2)]
                Bt2 = [cx.sb_in(ps2, "s2_Bt%d" % i, [128, 1024], BF16) for i in range(2)]
                xdt2 = [cx.sb_in(ps2, "s2_xdt%d" % i, [128, DI], BF16) for i in range(2)]
                xdte2 = [cx.sb_in(ps2, "s2_xdte%d" % i, [128, DI], BF16) for i in range(2)]
                cbm = cx.sb_in(ps2, "s2_cbm", [128, 8, 128], BF16)[0]
                r_cbm = [Res("s2_cbm_%d" % k) for k in range(2)]
                rhi, r_rhi = cx.sb_in(ps2, "s2_rf", [128, NH, 128], F32)
                rfq = [Res("s2_rfq%d" % q) for q in range(4)]
                Mt2 = [cx.sb_in(ps2, "s2_Mt%d" % i, [128, NH, 128], BF16) for i in range(2)]
                Mtq = [[Res("s2_Mtq%d_%d" % (i, q)) for q in range(8)] for i in range(2)]
                xcq = [Res("s2_xcq%d" % q) for q in range(6)]
                cTq = [[Res("s2_cTq%d_%d" % (i, q)) for q in range(2)] for i in range(2)]
                xsth = [Res("s2_xsth%d" % q) for q in range(2)]
                skbh = [[Res("s2_skbh%d_%d" % (i, q)) for q in range(2)] for i in range(2)]
                yvq = [Res("s2_yvq%d" % q) for q in range(4)]
                hTsq = [Res("s2_hTsq%d" % q) for q in range(4)]
                hTs, r_hTs = cx.sb_in(ps2, "s2_hTs", [128, DI], F32)
                hTb, r_hTb = cx.sb_in(ps2, "s2_hTb", [128, DI], BF16)
                htmp, r_htmp = cx.sb_in(ps2, "s2_htmp", [128, DI], F32)
                yv, r_yv = cx.sb_in(ps2, "s2_yv", [128, DI], F32)
                gjunk, r_gjunk = cx.sb_in(ps2, "s2_gjunk", [128, DI], BF16)
                ssg, r_ssg = cx.sb_in(ps2, "s2_ssg", [128, 1], F32)
                tmpg, r_tmpg = cx.sb_in(ps2, "s2_tmpg", [128, 1], F32)
                rstdg, r_rstdg = cx.sb_in(ps2, "s2_rstdg", [128, 1], F32)
                cx.memset("pool", hTs[:], 0.0, hTsq)
                cx.memset("pool", hTb[:], 0.0, [r_hTb])

                def s2_load(c):
                    xbt, r_xbt = xb2[c % 2]
                    s.dma("s2_ldx%d" % (c % 2), xbt[:, :, 3:131], xbd[c].rearrange("p (a b) -> p a b", b=128), [r_xb[c]], [r_xbt])
                    s.dma("s2_ldz%d" % (c % 3), sz3[c % 3][0][:], szd[c], [r_sz[c]], [sz3[c % 3][1]])
                    if c % GB == 0:
                        gn = min(GB, nch - c)
                        gi = (c // GB) % 2
                        s.dma("s2_ldd%d" % gi, dt3[gi][0][:, 0:gn, :], dtd[c:c + gn].rearrange("c p h -> p c h"),
                              [r_dt[cc] for cc in range(c, c + gn)], [dt3[gi][1]])

                def stage1(c):
                    sl = c % 2
                    xbt, r_xbt = xb2[sl]
                    gi = (c // GB) % 2
                    ci = c % GB
                    gn = min(GB, nch - (c - ci))
                    dtr8, r_dtr = dt3[gi]
                    dtv8, r_dtv = dtv2[gi]
                    eacs8, r_eacs = eacs2[gi]
                    dte8, r_dte = dte2[gi]
                    cdv8, r_cdv = cdv2[gi]
                    av8, r_av = av2[gi]
                    dtv = dtv8[:, ci, :]
                    dte = dte8[:, ci, :]
                    av = av8[:, ci, :]
                    cT, r_cT = cT2[sl]
                    skb, r_skb = skb2[sl]
                    B_tok, r_Bt = Bt2[sl]
                    xdt, r_xdt = xdt2[sl]
                    xdte, r_xdte = xdte2[sl]
                    Mt, r_Mt = Mt2[sl]
                    if c == 0:
                        cx.memset("pool", xbt[:, :, 0:3], 0.0, [r_xbt])
                    else:
                        pv, r_pv = xb2[1 - sl]
                        cx.cp("pool", xbt[:, :, 0:3], pv[:, :, 128:131], [r_pv], [r_xbt])
                    if c + 1 < nch:
                        s2_load(c + 1)
                    if ci == 0:
                        W = gn * NH
                        g2 = lambda t: t[:, 0:gn, :]
                        f2 = lambda t: t[:, 0:gn, :].rearrange("p a b -> p (a b)")
                        cx.tt("dve", g2(t0), g2(dtr8), bcast_mid(dtb[:], gn), ALU.add, [r_dtr, r_dtb], [r_t0])
                        cx.stt(f2(t1), f2(t0), -1.0, f2(t0), ALU.mult, ALU.max, [r_t0], [r_t1])
                        cx.act(f2(t1), f2(t1), AF.Exp, [r_t1], [r_t1], scale=-1.0)
                        cx.act(f2(t2), f2(t1), AF.Ln, [r_t1], [r_t2], bias=1.0)
                        cx.stt(f2(dtv8), f2(t0), 0.0, f2(t2), ALU.max, ALU.add, [r_t0, r_t2], [r_dtv])
                        cx.tt("dve", g2(av8), g2(dtv8), bcast_mid(A_b[:], gn), ALU.mult, [r_dtv, r_Ab], [r_av])
                        cx.mm(PS[0][0][:, 0:W], tri_f[:], f2(av8), True, True, [r_trif, r_av], [PS[0][1]])
                        cx.cp("dve", f2(acs), PS[0][0][:, 0:W], [PS[0][1]], [r_acs])
                        cx.mm(PS[0][0][:, 0:W], ones_f[:], f2(av8), True, True, [r_onesf, r_av], [PS[0][1]])
                        cx.tt("dve", f2(t0), PS[0][0][:, 0:W], f2(acs), ALU.subtract, [PS[0][1], r_acs], [r_t0])
                        cx.act(f2(cdv8), PS[0][0][:, 0:W], AF.Exp, [PS[0][1]], [r_cdv])
                        cx.act(f2(eacs8), f2(acs), AF.Exp, [r_acs], [r_eacs])
                        cx.act(f2(dte8), f2(t0), AF.Exp, [r_t0], [r_dte])
                    yield
                    def mask_q(q4):
                        cx.tt("dve", rhi[:, q4 * 8:(q4 + 1) * 8, :], bcast_mid(tri_f[:], 8), bcast_last(av[:, q4 * 8:(q4 + 1) * 8], 128), ALU.mult,
                              [r_trif, r_av], [rfq[q4]])
                    for q in range(8):
                        bi = 1 + (q % 2)
                        cx.mm(PS[bi][0][:, :], cb_row[0:4, q * 128:(q + 1) * 128], sel4[0:4, :], True, False, [r_cbrow, r_sel4], [PS[bi][1]])
                        for b4 in range(4):
                            blk = q * 4 + b4
                            o = PS[bi][0][:, b4 * 128:(b4 + 1) * 128]
                            for k in range(4):
                                cx.mm(o, diag[:, k, blk, :], xbt[:, blk, k:k + 128], False, (k == 3 and b4 == 3), [r_diag, r_xbt], [PS[bi][1]])
                        if q < 6:
                            cx.act(xc[:, q * 4:(q + 1) * 4, :].rearrange("p a b -> p (a b)"), PS[bi][0][:, :], AF.Silu, [PS[bi][1]], [xcq[q]])
                        else:
                            cx.act(cT[:, (q - 6) * 4:(q - 5) * 4, :].rearrange("p a b -> p (a b)"), PS[bi][0][:, :], AF.Silu, [PS[bi][1]], [cTq[sl][q - 6]])
                        yield
                        if q == 1:
                            mask_q(0)
                            yield
                            mask_q(1)
                            yield
                        if q == 3:
                            mask_q(2)
                            yield
                            mask_q(3)
                            yield
                    for half in range(2):
                        bi = 3 if half == 0 else 0
                        pb = psb(bi)
                        for b8 in range(8):
                            blk = half * 8 + b8
                            cx.tr(pb[:, b8 * 128:(b8 + 1) * 128], xc[:, blk, :], ident_b[:], [xcq[blk // 4], r_identb], [PS[bi][1]])
                        cx.cp("act", xs_tok[:, half * 1024:(half + 1) * 1024], pb[:, 0:1024], [PS[bi][1]], [xsth[half]])
                        cx.tt("dve", skb[:, half * 1024:(half + 1) * 1024].rearrange("p (h d) -> p h d", d=HP),
                              pb[:, 0:1024].rearrange("p (h d) -> p h d", d=HP), bcast_last(D_b[:, half * 16:(half + 1) * 16], HP), ALU.mult,
                              [PS[bi][1], r_Db], [skbh[sl][half]])
                        yield
                    pb = psb(3)
                    for b8 in range(8):
                        cx.tr(pb[:, b8 * 128:(b8 + 1) * 128], xc[:, 16 + b8, :], ident_b[:], [xcq[4 + b8 // 4], r_identb], [PS[3][1]])
                    cx.cp("act", B_tok[:], pb[:, 0:1024], [PS[3][1]], [r_Bt])
                    cx.tt("pool", xdt[:].rearrange("p (h d) -> p h d", d=HP), xs_tok[:].rearrange("p (h d) -> p h d", d=HP),
                          bcast_last(dtv, HP), ALU.mult, xsth + [r_dtv], [r_xdt])
                    yield
                    cx.tt("pool", xdte[:].rearrange("p (h d) -> p h d", d=HP), xdt[:].rearrange("p (h d) -> p h d", d=HP),
                          bcast_last(dte, HP), ALU.mult, [r_xdt, r_dte], [r_xdte])
                    for half in range(2):
                        bi = 1 + half
                        for g4 in range(4):
                            g = half * 4 + g4
                            cx.mm(PS[bi][0][:, g4 * 128:(g4 + 1) * 128], xc[:, 16 + g, :], cT[:, g, :], True, True, [xcq[4 + g // 4], cTq[sl][g // 4]], [PS[bi][1]])
                        cx.tt("dve", cbm[:, half * 4:(half + 1) * 4, :], PS[bi][0][:, :].rearrange("p (a b) -> p a b", b=128),
                              bcast_mid(tri_b[:], 4), ALU.mult, [PS[bi][1], r_trib], [r_cbm[half]])
                    yield
                    for q in range(8):
                        bi = q % 4
                        cx.mm(PS[bi][0][:, :], strict_b[:], rhi[:, q * 4:(q + 1) * 4, :].rearrange("p a b -> p (a b)"), True, True,
                              [r_strict, rfq[q // 2]], [PS[bi][1]])
                        cx.act(Mt[:, q * 4:(q + 1) * 4, :].rearrange("p a b -> p (a b)"), PS[bi][0][:, :], AF.Exp, [PS[bi][1]], [Mtq[sl][q]])
                        cx.tt("dve", Mt[:, q * 4:(q + 1) * 4, :], Mt[:, q * 4:(q + 1) * 4, :], bcast_mid(cbm[:, q, :], 4), ALU.mult,
                              [Mtq[sl][q], r_cbm[q // 4]], [Mtq[sl][q]])
                        yield

                def stage2(c):
                    sl = c % 2
                    szt, r_szt = sz3[c % 3]
                    ynt, r_ynt = yn2[sl]
                    gi = (c // GB) % 2
                    eacs8, r_eacs = eacs2[gi]
                    cdv8, r_cdv = cdv2[gi]
                    eacs = eacs8[:, c % GB, :]
                    cdv = cdv8[:, c % GB, :]
                    cT, r_cT = cT2[sl]
                    skb, r_skb = skb2[sl]
                    B_tok, r_Bt = Bt2[sl]
                    xdt, r_xdt = xdt2[sl]
                    xdte, r_xdte = xdte2[sl]
                    Mt, r_Mt = Mt2[sl]
                    ytmp, r_ytmp = htmp, r_htmp
                    for hh in range(2):
                        for b2 in range(2):
                            bi = 4 + b2
                            h0 = hh * 16 + b2 * 8
                            cx.mm(PS[bi][0][:, :], ident_b[:], skb[:, h0 * HP:(h0 + 8) * HP], True, False, [r_identb, skbh[sl][hh]], [PS[bi][1]])
                            for h8 in range(8):
                                h = h0 + h8
                                cx.mm(PS[bi][0][:, h8 * 64:(h8 + 1) * 64], Mt[:, h, :], xdt[:, h * HP:(h + 1) * HP], False, h8 == 7,
                                      [Mtq[sl][h // 4], r_xdt], [PS[bi][1]])
                        for g4 in range(4):
                            g = hh * 4 + g4
                            bi = 6 + (g4 // 2)
                            cx.mm(PS[bi][0][:, (g4 % 2) * 256:(g4 % 2 + 1) * 256], cT[:, g, :], hTb[:, g * 256:(g + 1) * 256], True, True,
                                  [cTq[sl][g // 4], r_hTb], [PS[bi][1]])
                        yield
                        for b2 in range(2):
                            h0 = hh * 16 + b2 * 8
                            cx.tt("dve", ytmp[:, b2 * 512:(b2 + 1) * 512].rearrange("p (h d) -> p h d", d=HP),
                                  PS[6 + b2][0][:, :].rearrange("p (h d) -> p h d", d=HP), bcast_last(eacs[:, h0:h0 + 8], HP), ALU.mult,
                                  [PS[6 + b2][1], r_eacs], [r_ytmp])
                            cx.tt("dve", yv[:, h0 * HP:(h0 + 8) * HP], PS[4 + b2][0][:, :], ytmp[:, b2 * 512:(b2 + 1) * 512], ALU.add,
                                  [PS[4 + b2][1], r_ytmp], [yvq[hh * 2 + b2]])
                            yield
                    cx.tt("pool", yv[:], yv[:], szt[:], ALU.mult, yvq + [r_szt], yvq)
                    yield
                    cx.act(gjunk[:], yv[:], AF.Square, yvq, [r_gjunk, r_ssg], accum_out=ssg[:, 0:1])
                    rstd_from_ss(ssg[:, 0:1], DI, rstdg[:, 0:1], tmpg[:, 0:1], [r_ssg], r_tmpg, r_rstdg)
                    cx.act(ynt[:], yv[:], AF.Copy, yvq + [r_rstdg], [r_ynt], scale=rstdg[:, 0:1])
                    s.dma("s2_sty%d" % sl, ynd[c], ynt[:], [r_ynt], [r_yn[c]])
                    yield
                    cx.tt("pool", htmp[:].rearrange("p (h d) -> p h d", d=HP), hTs[:].rearrange("p (h d) -> p h d", d=HP),
                          bcast_last(cdv, HP), ALU.mult, hTsq + [r_cdv], [r_htmp])
                    yield
                    for q in range(4):
                        bi = 4 + q
                        for g2 in range(2):
                            g = q * 2 + g2
                            cx.mm(PS[bi][0][:, g2 * 256:(g2 + 1) * 256], B_tok[:, g * 128:(g + 1) * 128], xdte[:, g * 256:(g + 1) * 256], True, True,
                                  [r_Bt, r_xdte], [PS[bi][1]])
                        cx.tt("dve", hTs[:, q * 512:(q + 1) * 512], PS[bi][0][:, :], htmp[:, q * 512:(q + 1) * 512], ALU.add,
                              [PS[bi][1], r_htmp], [hTsq[q]])
                        yield
                    cx.cp("act", hTb[:], hTs[:], hTsq, [r_hTb])

                s2_load(0)
                interleave(stage1(0))
                for c in range(nch):
                    interleave_pattern(stage1(c + 1) if c + 1 < nch else None, stage2(c),
                                       "121121212112121121212121212121211111111")

            s.barrier()
            psc.close()
            s.barrier()
            if nxt is not None:
                pw = ExitStack()
                Pn = att[nxt // 2]
                prefetched[nxt] = (pw, load_weights_bf16(pw, "aw_in", Pn["w_in"], ATT_IN), load_weights_bf16(pw, "aw_out", Pn["w_out"], D))
            with ExitStack() as ps3:
                w_out, r_wout = load_weights_bf16(ps3, "w_out", P["w_out"], D)
                postw, r_postw = cx.sb_in(ps3, "postw", [128, D], F32, track=False)
                s.dma("c_ld0", postw[:], postw_in[l], (), [r_postw])
                gwk, r_gwk = cx.sb_in(ps3, "gwk", [128, 16], F32, track=False)
                s.dma("c_ld1", gwk[:], P["gwk"][:, :], (), [r_gwk])
                for k in range(16):
                    cx.ts("dve", w_out[:, k, :], w_out[:, k, :], gwk[:, k:k + 1], ALU.mult, [r_wout, r_gwk], [r_wout])
                xt2 = [cx.sb_in(ps3, "s3_x%d" % i, [128, D], F32) for i in range(3)]
                yn2 = [cx.sb_in(ps3, "s3_yn%d" % i, [128, DI], BF16) for i in range(2)]
                ot2 = [(cx.sb_in(ps3, "s3_ot%d" % i, [128, D], F32)[0], [Res("s3_ot%d_%d" % (i, k)) for k in range(2)]) for i in range(2)]
                ynT2 = [(cx.sb_in(ps3, "s3_ynT%d" % i, [128, 16, 128], BF16)[0], [Res("s3_ynT%d_%d" % (i, k)) for k in range(2)]) for i in range(2)]
                junk, r_junk = cx.sb_in(ps3, "s3_junk", [128, D], BF16)
                ss2 = cx.sb_in(ps3, "s3_ss2", [128, 2], F32)[0]
                r_ss2 = [Res("s3_ss2_%d" % k) for k in range(2)]
                ss, r_ss = cx.sb_in(ps3, "s3_ss", [128, 1], F32)
                tmp1, r_tmp1 = cx.sb_in(ps3, "s3_tmp1", [128, 1], F32)
                rstd, r_rstd = cx.sb_in(ps3, "s3_rstd", [128, 1], F32)

                def s3_load(c):
                    sl = c % 2
                    s.dma("s3_ldx%d" % (c % 3), xt2[c % 3][0][:], x_src[c * CH:(c + 1) * CH, :], [x_src_res[c]], [xt2[c % 3][1]])
                    s.dma("s3_ldy%d" % sl, yn2[sl][0][:], ynd[c], [r_yn[c]], [yn2[sl][1]])

                def s3_transposes(c):
                    ynt, r_ynt = yn2[c % 2]
                    ynT, r_ynT = ynT2[c % 2]
                    for half in range(2):
                        bi = half
                        pb = psb(bi)
                        for b8 in range(8):
                            blk = half * 8 + b8
                            cx.tr(pb[:, b8 * 128:(b8 + 1) * 128], ynt[:, blk * 128:(blk + 1) * 128], ident_b[:], [r_ynt, r_identb], [PS[bi][1]])
                        cx.cp("dve" if half == 0 else "act", ynT[:, half * 8:(half + 1) * 8, :].rearrange("p a b -> p (a b)"), pb[:, 0:1024],
                              [PS[bi][1]], [r_ynT[half]])
                s3_load(0)
                if nch > 1:
                    s3_load(1)
                s3_transposes(0)
                for c in range(nch):
                    sl = c % 2
                    xt, r_x = xt2[c % 3]
                    ot, r_ot = ot2[sl]
                    ynT, r_ynT = ynT2[sl]
                    if c + 2 < nch:
                        s3_load(c + 2)
                    banks = [2 + 2 * (c % 2), 3 + 2 * (c % 2)]
                    for nb in range(2):
                        bi = banks[nb]
                        for k in range(16):
                            cx.mm(PS[bi][0][:, :], ynT[:, k, :], w_out[:, k, nb * 512:(nb + 1) * 512], k == 0, k == 15, [r_ynT[k // 8], r_wout], [PS[bi][1]])
                        if nb == 0 and c + 1 < nch:
                            s3_transposes(c + 1)
                    post_norm_residual(c, banks, xt[:], r_x, postw[:], r_postw,
                                       (junk, r_junk, ss2, r_ss2, ss, r_ss, tmp1, r_tmp1, rstd, r_rstd, ot, r_ot),
                                       x_dst[c * CH:(c + 1) * CH, :], x_dst_res[c], "s3_st%d" % sl)


        def swa_layer(l, x_src, x_src_res, x_dst, x_dst_res):
            j = l // 2
            P = att[j]
            NT = 4 * nch
            with ExitStack() as pa:
                if l in prefetched:
                    (w_in, r_win), (w_out, r_wout) = prefetched[l][1], prefetched[l][2]
                else:
                    w_in, r_win = load_weights_bf16(pa, "aw_in", P["w_in"], ATT_IN)
                    w_out, r_wout = load_weights_bf16(pa, "aw_out", P["w_out"], D)
                prew, r_prew = cx.sb_in(pa, "a_prew", [128, D], F32, track=False)
                postw, r_postw = cx.sb_in(pa, "a_postw", [128, D], F32, track=False)
                sinks, r_sinks = cx.sb_in(pa, "a_sinks", [128, AQ], F32, track=False)
                amask, r_amask = cx.sb_in(pa, "a_mask", [128, 2, 256], F32, track=False)
                invf, r_invf = cx.sb_in(pa, "a_invf", [128, 8], F32, track=False)
                posi, r_posi = cx.sb_in(pa, "a_posi", [128, nch], I32, track=False)
                posf, r_posf = cx.sb_in(pa, "a_posf", [128, nch], F32, track=False)
                cosT, r_cos = cx.sb_in(pa, "a_cos", [128, nch, 8], F32, track=False)
                sinT, r_sin = cx.sb_in(pa, "a_sin", [128, nch, 8], F32, track=False)
                s.dma("c_ld0", prew[:], prew_in[l], (), [r_prew])
                s.dma("c_ld1", postw[:], postw_in[l], (), [r_postw])
                s.dma("c_ld3", sinks[:], P["sinks"][:, :], (), [r_sinks])
                negsinks, r_negsinks = cx.sb_in(pa, "a_negsinks", [128, AQ], F32, track=False)
                cx.ts("dve", negsinks[:], sinks[:], -1.0, ALU.mult, [r_sinks], [r_negsinks])
                s.dma("c_ld4", amask[:], mask_in[:, :, :], (), [r_amask])
                s.dma("c_ld5", invf[:], invf_in[:, :], (), [r_invf])
                s.dma("c_ld6", posi[:], pos_in[:, :], (), [r_posi])
                with ExitStack() as prt:
                    ang, r_ang = cx.sb_in(prt, "a_ang", [128, nch, 8], F32)
                    kf, r_kf = cx.sb_in(prt, "a_kf", [128, nch, 8], F32)
                    ki, r_ki = cx.sb_in(prt, "a_ki", [128, nch, 8], I32)
                    rr, r_rr = cx.sb_in(prt, "a_rr", [128, nch, 8], F32)
                    fx, r_fx = cx.sb_in(prt, "a_fx", [128, nch, 8], F32)
                    cx.cp("dve", posf[:], posi[:], [r_posi], [r_posf])
                    cx.tt("dve", ang[:], bcast_last(posf[:], 8), bcast_mid(invf[:], nch), ALU.mult, [r_posf, r_invf], [r_ang])

                    def reduced_sin(dst, r_dst, shift):
                        cx.ts("dve", kf[:], ang[:], shift, ALU.add, [r_ang], [r_kf], s2=1.0 / (2.0 * np.pi), op1=ALU.mult)
                        cx.cp("dve", ki[:], kf[:], [r_kf], [r_ki])
                        cx.cp("dve", kf[:], ki[:], [r_ki], [r_kf])
                        cx.stt(rr[:], kf[:], -TWO_PI_HI, ang[:], ALU.mult, ALU.add, [r_kf, r_ang], [r_rr])
                        cx.stt(rr[:], kf[:], -TWO_PI_LO, rr[:], ALU.mult, ALU.add, [r_kf, r_rr], [r_rr])
                        cx.ts("dve", rr[:], rr[:], shift, ALU.add, [r_rr], [r_rr])
                        cx.ts("dve", fx[:], rr[:], float(np.pi), ALU.is_gt, [r_rr], [r_fx], s2=-2.0 * np.pi, op1=ALU.mult)
                        cx.tt("dve", rr[:], rr[:], fx[:], ALU.add, [r_rr, r_fx], [r_rr])
                        cx.ts("dve", fx[:], rr[:], -float(np.pi), ALU.is_lt, [r_rr], [r_fx], s2=2.0 * np.pi, op1=ALU.mult)
                        cx.tt("dve", rr[:], rr[:], fx[:], ALU.add, [r_rr, r_fx], [r_rr])
                        cx.ts("dve", rr[:], rr[:], 3.1415925, ALU.min, [r_rr], [r_rr], s2=-3.1415925, op1=ALU.max)
                        cx.act(dst[:], rr[:], AF.Sin, [r_rr], [r_dst])
                    reduced_sin(sinT, r_sin, 0.0)
                    reduced_sin(cosT, r_cos, float(np.pi / 2.0))
                    s.barrier()

                xt2 = [cx.sb_in(pa, "a_x%d" % i, [128, D], F32) for i in range(2)]
                xr2 = [cx.sb_in(pa, "a_xr%d" % i, [128, D], F32) for i in range(2)]
                ot2 = [(cx.sb_in(pa, "a_ot%d" % i, [128, D], F32)[0], [Res("a_ot%d_%d" % (i, k)) for k in range(2)]) for i in range(2)]
                junk, r_junk = cx.sb_in(pa, "a_junk", [128, D], BF16)
                ss, r_ss = cx.sb_in(pa, "a_ss", [128, 1], F32)
                tmp1, r_tmp1 = cx.sb_in(pa, "a_tmp1", [128, 1], F32)
                rstd, r_rstd = cx.sb_in(pa, "a_rstd", [128, 1], F32)
                junk_b, r_junk_b = cx.sb_in(pa, "a_junk_b", [128, 512], BF16)
                ss2 = cx.sb_in(pa, "a_ss2", [128, 2], F32)[0]
                r_ss2 = [Res("a_ss2_%d" % k) for k in range(2)]
                ss_b, r_ss_b = cx.sb_in(pa, "a_ss_b", [128, 1], F32)
                tmp1_b, r_tmp1_b = cx.sb_in(pa, "a_tmp1_b", [128, 1], F32)
                rstd_b, r_rstd_b = cx.sb_in(pa, "a_rstd_b", [128, 1], F32)
                hb, r_hb = cx.sb_in(pa, "a_hb", [128, D], BF16)
                hT, r_hT = cx.sb_in(pa, "a_hT", [128, 8, 128], BF16)
                qk, r_qk = cx.sb_in(pa, "a_qk", [128, 1280], F32)
                qkb, r_qkb = cx.sb_in(pa, "a_qkb", [128, 1280], BF16)
                ra, r_ra = cx.sb_in(pa, "a_ra", [128, 20, 8], F32)
                rb, r_rb = cx.sb_in(pa, "a_rb", [128, 20, 8], F32)
                sg3 = [cx.sb_in(pa, "a_sg%d" % i, [128, D], BF16) for i in range(4)]
                qT2 = [(cx.sb_in(pa, "a_qT%d" % i, [64, AQ, 128], BF16)[0], [Res("a_qT%d_%d" % (i, k)) for k in range(2)]) for i in range(2)]
                kT4 = [cx.sb_in(pa, "a_kT%d" % i, [64, AKV, 128], BF16) for i in range(4)]
                v4 = [cx.sb_in(pa, "a_v%d" % i, [128, 256], BF16) for i in range(5)]
                sm2 = [cx.sb_in(pa, "a_sm%d" % i, [128, 4, 256], F32) for i in range(3)]
                rmax2 = [cx.sb_in(pa, "a_rmax%d" % i, [128, 4], F32) for i in range(3)]
                negm2 = [cx.sb_in(pa, "a_negm%d" % i, [128, AQ], F32) for i in range(3)]
                negmq = [[Res("a_negmq%d_%d" % (i, k)) for k in range(4)] for i in range(3)]
                rsum2 = [cx.sb_in(pa, "a_rsum%d" % i, [128, AQ], F32) for i in range(3)]
                esk2 = [cx.sb_in(pa, "a_esk%d" % i, [128, AQ], F32) for i in range(3)]
                rden, r_rden = cx.sb_in(pa, "a_rden", [128, AQ], F32)
                pt2 = [cx.sb_in(pa, "a_pt%d" % i, [128, 4, 256], BF16) for i in range(3)]
                ptg = [[Res("a_ptg%d_%d" % (i, g)) for g in range(4)] for i in range(3)]
                rsh = [[Res("a_rsh%d_%d" % (i, h)) for h in range(AQ)] for i in range(3)]
                pT2 = [cx.sb_in(pa, "a_pT%d" % i, [128, 4, 2, 128], BF16) for i in range(3)]
                og, r_og = cx.sb_in(pa, "a_og", [128, D], BF16)
                otmp, r_otmp = cx.sb_in(pa, "a_otmp", [128, D], F32)
                ogT, r_ogT = cx.sb_in(pa, "a_ogT", [128, 8, 128], BF16)
                ntiles = (junk, r_junk, ss, r_ss, tmp1, r_tmp1, rstd, r_rstd, hb, r_hb)
                SB = [2, 3]
                PB = [4, 5]
                OB = [6, 7]

                def a_load(c):
                    s.dma("a_ldx%d" % (c % 2), xt2[c % 2][0][:], x_src[c * CH:(c + 1) * CH, :], [x_src_res[c]], [xt2[c % 2][1]])

                def F1(c):
                    xt, r_x = xt2[c % 2]
                    norm_chain(ntiles, xt[:], r_x, prew[:], r_prew)

                def F2(c):
                    transpose_h(hb, r_hb, hT, r_hT, 0)

                def F3(c):
                    vv, r_v = v4[c % 5]
                    for nb in range(3):
                        bi = nb % 2
                        for k in range(8):
                            cx.mm(PS[bi][0][:, :], hT[:, k, :], w_in[:, k, nb * 512:(nb + 1) * 512], k == 0, k == 7, [r_hT, r_win], [PS[bi][1]])
                        if nb < 2:
                            cx.cp("dve", qk[:, nb * 512:(nb + 1) * 512], PS[bi][0][:, :], [PS[bi][1]], [r_qk])
                        else:
                            cx.cp("dve", qk[:, 1024:1280], PS[bi][0][:, 0:256], [PS[bi][1]], [r_qk])
                            cx.cp("act", vv[:], PS[bi][0][:, 256:512], [PS[bi][1], r_qk], [r_v])

                def F4(c):
                    sg, r_sg = sg3[c % 4]
                    for nb in range(3, 5):
                        bi = nb % 2
                        for k in range(8):
                            cx.mm(PS[bi][0][:, :], hT[:, k, :], w_in[:, k, nb * 512:(nb + 1) * 512], k == 0, k == 7, [r_hT, r_win], [PS[bi][1]])
                        cx.act(sg[:, (nb - 3) * 512:(nb - 2) * 512], PS[bi][0][:, :], AF.Silu, [PS[bi][1]], [r_sg])

                def F45(c):
                    q3 = qk[:].rearrange("p (h d) -> p h d", d=AD)
                    qb3 = qkb[:].rearrange("p (h d) -> p h d", d=AD)
                    cosb = bcast_mid(cosT[:, c, :], 20)
                    sinb = bcast_mid(sinT[:, c, :], 20)
                    cx.cp("pool", qkb[:], qk[:], [r_qk], [r_qkb])
                    cx.tt("dve", ra[:], q3[:, :, 0:8], cosb, ALU.mult, [r_qk, r_cos], [r_ra])
                    cx.tt("dve", rb[:], q3[:, :, 8:16], sinb, ALU.mult, [r_qk, r_sin], [r_rb])
                    cx.tt("dve", qb3[:, :, 0:8], ra[:], rb[:], ALU.subtract, [r_ra, r_rb, r_qkb], [r_qkb])
                    cx.tt("dve", ra[:], q3[:, :, 8:16], cosb, ALU.mult, [r_qk, r_cos, r_qkb], [r_ra])
                    cx.tt("dve", rb[:], q3[:, :, 0:8], sinb, ALU.mult, [r_qk, r_sin, r_qkb], [r_rb])
                    cx.tt("dve", qb3[:, :, 8:16], ra[:], rb[:], ALU.add, [r_ra, r_rb, r_qkb], [r_qkb])

                def F6(c):
                    qT, r_qT = qT2[c % 2]
                    kT, r_kT = kT4[c % 4]
                    for half in range(2):
                        bi = half
                        pb = psb(bi)
                        for h8 in range(8):
                            h = half * 8 + h8
                            cx.tr(pb[0:64, h8 * 128:(h8 + 1) * 128], qkb[:, h * AD:(h + 1) * AD], ident_b[:], [r_qkb, r_identb], [PS[bi][1]])
                        cx.cp("act" if half else "dve", qT[:, half * 8:(half + 1) * 8, :].rearrange("p a b -> p (a b)"), pb[0:64, 0:1024],
                              [PS[bi][1]], [r_qT[half]])
                    pb = psb(0)
                    for h4 in range(4):
                        cx.tr(pb[0:64, h4 * 128:(h4 + 1) * 128], qkb[:, 1024 + h4 * AD:1024 + (h4 + 1) * AD], ident_b[:], [r_qkb, r_identb], [PS[0][1]])
                    cx.cp("dve", kT[:].rearrange("p a b -> p (a b)"), pb[0:64, 0:512], [PS[0][1]], [r_kT])

                def St(t):
                    c, kh = divmod(t, 4)
                    qT, r_qT = qT2[c % 2]
                    kT, r_kT = kT4[c % 4]
                    kTp, r_kTp = kT4[(c - 1) % 4]
                    for g in range(4):
                        h = kh * 4 + g
                        bi = SB[g // 2]
                        o = PS[bi][0][:, (g % 2) * 256:(g % 2 + 1) * 256]
                        if c > 0:
                            cx.mm(o[:, 0:128], qT[:, h, :], kTp[:, kh, :], True, True, [r_qT[h // 8], r_kTp], [PS[bi][1]])
                        cx.mm(o[:, 128:256], qT[:, h, :], kT[:, kh, :], True, True, [r_qT[h // 8], r_kT], [PS[bi][1]])

                def Mt_(t):
                    c, kh = divmod(t, 4)
                    sm, r_sm = sm2[t % 3]
                    rmax, r_rmax = rmax2[t % 3]
                    negm, r_negm = negm2[c % 3]
                    mk = amask[:, 0 if c == 0 else 1, :]
                    for i2 in range(2):
                        bi = SB[i2]
                        if c > 0:
                            cx.stt(sm[:, i2 * 2:(i2 + 1) * 2, :], PS[bi][0][:, :].rearrange("p (a b) -> p a b", b=256), 0.125,
                                   bcast_mid(mk, 2), ALU.mult, ALU.add, [PS[bi][1], r_amask], [r_sm])
                        else:
                            cx.memset("dve", sm[:, i2 * 2:(i2 + 1) * 2, 0:128], NEG, [r_sm])
                            cx.stt(sm[:, i2 * 2:(i2 + 1) * 2, 128:256], PS[bi][0][:, :].rearrange("p (a b) -> p a b", b=256)[:, :, 128:256], 0.125,
                                   bcast_mid(mk[:, 128:256], 2), ALU.mult, ALU.add, [PS[bi][1], r_amask], [r_sm])
                    cx.s.op("dve", lambda e, o_=rmax[:, 0:4], i_=sm[:]: e.tensor_reduce(out=o_, in_=i_, axis=mybir.AxisListType.X, op=ALU.max),
                            [r_sm], [r_rmax])
                    cx.stt(negm[:, kh * 4:(kh + 1) * 4], rmax[:], -1.0, negsinks[:, kh * 4:(kh + 1) * 4], ALU.mult, ALU.min,
                           [r_rmax, r_negsinks], [negmq[c % 3][kh]])

                def Et_(t):
                    c, kh = divmod(t, 4)
                    sm, r_sm = sm2[t % 3]
                    negm, r_negm = negm2[c % 3]
                    pt, r_pt = pt2[t % 3]
                    rsum, r_rsum = rsum2[c % 3]
                    for g in range(4):
                        h = kh * 4 + g
                        cx.act(pt[:, g, :], sm[:, g, :], AF.Exp, [r_sm, negmq[c % 3][kh]], [ptg[t % 3][g], rsh[c % 3][h]], bias=negm[:, h:h + 1], accum_out=rsum[:, h:h + 1])

                def Tt(t):
                    pt, r_pt = pt2[t % 3]
                    bi = PB[t % 2]
                    pbk = psb(bi)
                    for g in range(4):
                        for hf in range(2):
                            cx.tr(pbk[:, (g * 2 + hf) * 128:(g * 2 + hf + 1) * 128], pt[:, g, hf * 128:(hf + 1) * 128], ident_b[:],
                                  [ptg[t % 3][g], r_identb], [PS[bi][1]])

                def Ct(t):
                    pT, r_pT = pT2[t % 3]
                    bi = PB[t % 2]
                    cx.cp("dve", pT[:].rearrange("p a b c -> p (a b c)"), psb(bi)[:, 0:1024], [PS[bi][1]], [r_pT])

                def Vt(t):
                    c, kh = divmod(t, 4)
                    pT, r_pT = pT2[t % 3]
                    vv, r_v = v4[c % 5]
                    vp, r_vp = v4[(c - 1) % 5]
                    ob = OB[kh // 2]
                    for g in range(4):
                        h = kh * 4 + g
                        o = PS[ob][0][:, (h % 8) * 64:(h % 8 + 1) * 64]
                        if c > 0:
                            cx.mm(o, pT[:, g, 0, :], vp[:, kh * AD:(kh + 1) * AD], True, False, [r_pT, r_vp], [PS[ob][1]])
                            cx.mm(o, pT[:, g, 1, :], vv[:, kh * AD:(kh + 1) * AD], False, True, [r_pT, r_v], [PS[ob][1]])
                        else:
                            cx.mm(o, pT[:, g, 1, :], vv[:, kh * AD:(kh + 1) * AD], True, True, [r_pT, r_v], [PS[ob][1]])

                def BN(c):
                    sg, r_sg = sg3[c % 4]
                    rsum, r_rsum = rsum2[c % 3]
                    esk, r_esk = esk2[c % 3]
                    negm, r_negm = negm2[c % 3]
                    cx.tt("dve", esk[:], sinks[:], negm[:], ALU.add, [r_sinks] + negmq[c % 3], [r_esk])
                    s.dma("a_ldr%d" % (c % 2), xr2[c % 2][0][:], x_src[c * CH:(c + 1) * CH, :], [x_src_res[c]], [xr2[c % 2][1]])
                    cx.act(esk[:], esk[:], AF.Exp, [r_esk], [r_esk])
                    cx.tt("dve", rden[:], rsum[:], esk[:], ALU.add, rsh[c % 3] + [r_esk], [r_rden])
                    cx.s.op("dve", lambda e, o_=rden[:], i_=rden[:]: e.reciprocal(out=o_, in_=i_), [r_rden], [r_rden])
                    for half in range(2):
                        bi = OB[half]
                        cx.tt("dve", otmp[:, half * 512:(half + 1) * 512].rearrange("p (h d) -> p h d", d=AD),
                              PS[bi][0][:, :].rearrange("p (h d) -> p h d", d=AD), bcast_last(rden[:, half * 8:(half + 1) * 8], AD), ALU.mult,
                              [PS[bi][1], r_rden], [r_otmp])
                    cx.tt("pool", og[:], otmp[:], sg[:], ALU.mult, [r_otmp, r_sg], [r_og])

                def BT(c):
                    pb = psb(1)
                    for k in range(8):
                        cx.tr(pb[:, k * 128:(k + 1) * 128], og[:, k * 128:(k + 1) * 128], ident_b[:], [r_og, r_identb], [PS[1][1]])
                    cx.cp("act", ogT[:].rearrange("p a b -> p (a b)"), pb[:, 0:1024], [PS[1][1]], [r_ogT])

                def BO(c):
                    for nb in range(2):
                        bi = nb
                        for k in range(8):
                            cx.mm(PS[bi][0][:, :], ogT[:, k, :], w_out[:, k, nb * 512:(nb + 1) * 512], k == 0, k == 7, [r_ogT, r_wout], [PS[bi][1]])
                    ot, r_ot = ot2[c % 2]
                    post_norm_a([0, 1], (junk_b, r_junk_b, ss2, r_ss2, ss_b, r_ss_b, tmp1_b, r_tmp1_b, rstd_b, r_rstd_b, ot, r_ot))

                def BP(c):
                    xr, r_xr = xr2[c % 2]
                    ot, r_ot = ot2[c % 2]
                    post_norm_b([0, 1], xr[:], r_xr, postw[:], r_postw,
                                (junk_b, r_junk_b, ss2, r_ss2, ss_b, r_ss_b, tmp1_b, r_tmp1_b, rstd_b, r_rstd_b, ot, r_ot),
                                x_dst[c * CH:(c + 1) * CH, :], x_dst_res[c], "a_st%d" % (c % 2))

                def okc(c):
                    return 0 <= c < nch

                def okt(t):
                    return 0 <= t < NT

                a_load(0)
                if nch > 1:
                    a_load(1)
                for u in range(-5, 4 * (nch - 1) + 17):
                    if (u - 15) % 4 == 0 and okc((u - 15) // 4):
                        BP((u - 15) // 4)
                    if (u - 12) % 4 == 0 and okc((u - 12) // 4):
                        BN((u - 12) // 4)
                    if (u + 5) % 4 == 0 and okc((u + 5) // 4):
                        F1((u + 5) // 4)
                    if okt(u - 1):
                        Mt_(u - 1)
                    if okt(u - 3):
                        Et_(u - 3)
                    if okt(u - 6):
                        Ct(u - 6)
                    if okt(u - 5):
                        Tt(u - 5)
                    if okt(u - 8):
                        Vt(u - 8)
                    cf, ph = divmod(u + 4, 4)
                    if okc(cf):
                        if ph == 0:
                            F2(cf)
                        elif ph == 1:
                            F3(cf)
                            F4(cf)
                            if cf + 2 < nch:
                                a_load(cf + 2)
                        elif ph == 2:
                            F45(cf)
                        else:
                            F6(cf)
                    if (u - 13) % 4 == 0 and okc((u - 13) // 4):
                        BT((u - 13) // 4)
                    if (u - 14) % 4 == 0 and okc((u - 14) // 4):
                        BO((u - 14) // 4)
                    if okt(u):
                        St(u)


        src = x_in
        src_res = [Res("xin_%d" % c, track=True) for c in range(nch)]
        for li, l in enumerate(layers):
            last = li == len(layers) - 1
            if last:
                dst = out_dram
                dst_res = [Res("xout_%d" % c) for c in range(nch)]
            else:
                dst = xs_scr[li % 2]
                dst_res = dres("xscr%d_%d" % (li % 2, li))
            if l % 2 == 0:
                nxt = layers[li + 1] if (li + 1 < len(layers) and layers[li + 1] % 2 == 1) else None
                ssd_layer(l, src, src_res, dst, dst_res, nxt)
            else:
                swa_layer(l, src, src_res, dst, dst_res)
            s.barrier()
            if l in prefetched:
                prefetched[l][0].close()
            src, src_res = dst, dst_res

        block = st.enter_context(nc.Block())
        s.finish(block)
    return nc


def _rep(v, n=128):
    return np.ascontiguousarray(np.broadcast_to(np.asarray(v, np.float32)[None, :], (n, v.shape[0])))


def host_consts(nch):
    idx = np.arange(128)
    tri = (idx[:, None] <= idx[None, :]).astype(np.float32)
    strict = (idx[:, None] > idx[None, :]).astype(np.float32)
    qi = idx[:, None]
    kj = np.arange(256)[None, :]
    dist = qi + 128 - kj
    valid = (dist >= 0) & (dist < 128)
    m1 = np.where(valid, 0.0, NEG).astype(np.float32)
    m0 = np.where(valid & (kj >= 128), 0.0, NEG).astype(np.float32)
    amask = np.ascontiguousarray(np.stack([m0, m1], axis=1))
    invf = (500000.0 ** (-np.arange(0, 16, 2, dtype=np.float32) / 16.0)).astype(np.float32)
    sel4 = np.zeros((4, 512), np.float32)
    for r in range(4):
        sel4[r, r * 128:(r + 1) * 128] = 1.0
    return dict(ident=np.eye(128, dtype=np.float32), tri=tri, strict=strict, amask=amask, invf=_rep(invf), sel4=sel4)


def make_in_map(b, layers, nch, inputs):
    T = nch * CH
    m = dict(host_consts(nch))
    m["x"] = np.ascontiguousarray(inputs["x"][b, :T])
    pos = np.asarray(inputs["positions"][b, :T]).astype(np.int32)
    m["pos"] = np.ascontiguousarray(pos.reshape(nch, 128).T)
    m["prew"] = np.ascontiguousarray(np.broadcast_to(np.asarray(inputs["pre_norm"], np.float32)[:, None, :], (DEPTH, 128, D)))
    m["postw"] = np.ascontiguousarray(np.broadcast_to(np.asarray(inputs["post_norm"], np.float32)[:, None, :], (DEPTH, 128, D)))
    for l in layers:
        j = l // 2
        if l % 2 == 0:
            m["s%d_w_in" % j] = np.ascontiguousarray(inputs["ssm_w_in"][j])
            cw = np.asarray(inputs["ssm_conv_w"][j], np.float32)
            m["s%d_conv_w" % j] = np.ascontiguousarray(cw.reshape(4, 32, 128).transpose(2, 0, 1))
            cbv = np.asarray(inputs["ssm_conv_b"][j], np.float32)
            m["s%d_conv_b" % j] = np.ascontiguousarray(cbv.reshape(8, 4, 128).transpose(1, 0, 2).reshape(4, 1024))
            m["s%d_dtb" % j] = _rep(inputs["ssm_dt_bias"][j])
            m["s%d_alog" % j] = _rep(inputs["ssm_a_log"][j])
            m["s%d_d" % j] = _rep(inputs["ssm_d"][j])
            m["s%d_gwk" % j] = np.ascontiguousarray(np.asarray(inputs["ssm_gate_norm"][j], np.float32).reshape(16, 128).T)
            m["s%d_w_out" % j] = np.ascontiguousarray(inputs["ssm_w_out"][j])
        else:
            m["a%d_w_in" % j] = np.ascontiguousarray(inputs["att_w_in"][j])
            m["a%d_sinks" % j] = _rep(inputs["att_sinks"][j])
            m["a%d_w_out" % j] = np.ascontiguousarray(inputs["att_w_out"][j])
    return m


_NC_CACHE = {}


def kernel(**inputs):
    inputs = {k: np.asarray(v) for k, v in inputs.items()}
    layers = (0, 1, 2, 3)
    key = (layers, NCHUNK)
    if key not in _NC_CACHE:
        _NC_CACHE[key] = build_program(list(layers), NCHUNK)
    nc = _NC_CACHE[key]
    in_maps = [make_in_map(b, layers, NCHUNK, inputs) for b in range(BATCH)]
    res = run_bass_kernel_spmd(nc, in_maps, core_ids=list(range(BATCH)))
    out = np.stack([np.asarray(r["out"]).reshape(SEQ, D) for r in res.results], axis=0)
    return out.astype(np.float32)
```
